# Optimizing a Trainium2 kernel written in Bass

```python
import math
import jax, jax.numpy as jnp
from jax import lax
import numpy as np

D_MODEL = 1024
BATCH = 4
SEQ = 8192
DEPTH = 1

N_HEADS = 8
HEAD_DIM = 64
V_HEAD_DIM = 2 * HEAD_DIM
QK_WIDTH = N_HEADS * 2 * HEAD_DIM
V_WIDTH = N_HEADS * V_HEAD_DIM
ROPE_THETA = 500000.0
ROT_DIM = HEAD_DIM // 4
Q_BLOCK = 128
CONV_CHANNELS = D_MODEL
CONV_WIDTH = 31
D_FF = 2816
NORM_EPS = 1e-5
NEG_INF = -1e30
OFF_K = QK_WIDTH
OFF_V = OFF_K + QK_WIDTH
OFF_U = OFF_V + V_WIDTH
OFF_G = OFF_U + 2 * CONV_CHANNELS
IN_WIDTH = OFF_G + 2 * D_MODEL

kernel_name = "hybrid_diffattn_conformer_macaron"


def lambda_init(layer_idx):
    return 0.8 - 0.6 * math.exp(-0.3 * layer_idx)


def rmsnorm(x, gain):
    xf = x.astype(jnp.float32)
    y = xf * lax.rsqrt(jnp.mean(xf * xf, axis=-1, keepdims=True) + NORM_EPS)
    return (y * gain.astype(jnp.float32)).astype(x.dtype)


def layernorm(x, gain, bias):
    xf = x.astype(jnp.float32)
    mu = jnp.mean(xf, axis=-1, keepdims=True)
    var = jnp.mean(jnp.square(xf - mu), axis=-1, keepdims=True)
    y = (xf - mu) * lax.rsqrt(var + NORM_EPS)
    return (y * gain.astype(jnp.float32) + bias.astype(jnp.float32)).astype(x.dtype)


def swiglu(x, w_gate_up, w_down):
    a, b = jnp.split(x @ w_gate_up, 2, axis=-1)
    return (jax.nn.silu(a) * b) @ w_down


def rope_tables(seq_len):
    pos = jnp.arange(seq_len, dtype=jnp.float32)
    inv_freq = ROPE_THETA ** (-jnp.arange(0, ROT_DIM, 2, dtype=jnp.float32) / ROT_DIM)
    ang = pos[:, None] * inv_freq[None, :]
    return jnp.cos(ang), jnp.sin(ang)


def partial_rope(x, cos, sin):
    xr = x[..., :ROT_DIM].astype(jnp.float32)
    x1, x2 = xr[..., :ROT_DIM // 2], xr[..., ROT_DIM // 2:]
    c, s = cos[None, :, None, :], sin[None, :, None, :]
    rot = jnp.concatenate([x1 * c - x2 * s, x2 * c + x1 * s], axis=-1).astype(x.dtype)
    return jnp.concatenate([rot, x[..., ROT_DIM:]], axis=-1)


def diff_attention(q, k, v, lam):
    B, S = q.shape[0], q.shape[1]
    nb = S // Q_BLOCK
    scale = HEAD_DIM ** -0.5
    q_blocks = q.reshape(B, nb, Q_BLOCK, 2 * N_HEADS, HEAD_DIM).transpose(1, 0, 2, 3, 4)
    key_pos = jnp.arange(S)

    def one_block(args):
        qb, bi = args
        s = jnp.einsum('bqhd,bkhd->bhqk', qb, k, preferred_element_type=jnp.float32) * scale
        q_pos = bi * Q_BLOCK + jnp.arange(Q_BLOCK)
        s = jnp.where(key_pos[None, :] <= q_pos[:, None], s, NEG_INF)
        p = jax.nn.softmax(s, axis=-1).reshape(B, N_HEADS, 2, Q_BLOCK, S)
        a = p[:, :, 0] - lam * p[:, :, 1]
        return jnp.einsum('bhqk,bkhe->bqhe', a.astype(v.dtype), v)

    o = lax.map(one_block, (q_blocks, jnp.arange(nb)))
    return o.transpose(1, 0, 2, 3, 4).reshape(B, S, N_HEADS, V_HEAD_DIM)


def causal_depthwise_conv(u, w, b):
    out = lax.conv_general_dilated(
        u, w.reshape(CONV_WIDTH, 1, CONV_CHANNELS).astype(u.dtype),
        window_strides=(1,), padding=((CONV_WIDTH - 1, 0),),
        dimension_numbers=('NWC', 'WIO', 'NWC'),
        feature_group_count=CONV_CHANNELS)
    return out + b


def setup_inputs(seed: int = 0) -> dict:
    key = jax.random.key(seed)
    ks = jax.random.split(key, 24)
    f32 = jnp.float32

    def nrm(k, shape, scale):
        return jax.random.normal(k, shape, f32) * scale

    def gain(k, shape):
        return 1.0 + 0.02 * jax.random.normal(k, shape, f32)

    L, D = DEPTH, D_MODEL
    return {
        "x": nrm(ks[0], (BATCH, SEQ, D), 1.0),
        "ffn1_norm": gain(ks[1], (L, D)),
        "ffn1_w_gate_up": nrm(ks[2], (L, D, 2 * D_FF), D ** -0.5),
        "ffn1_w_down": nrm(ks[3], (L, D_FF, D), D_FF ** -0.5),
        "mix_norm": gain(ks[4], (L, D)),
        "w_in": nrm(ks[5], (L, D, IN_WIDTH), D ** -0.5),
        "b_gate": nrm(ks[6], (L, 2 * D), 0.01),
        "lambda_q1": nrm(ks[7], (L, HEAD_DIM), 0.1),
        "lambda_k1": nrm(ks[8], (L, HEAD_DIM), 0.1),
        "lambda_q2": nrm(ks[9], (L, HEAD_DIM), 0.1),
        "lambda_k2": nrm(ks[10], (L, HEAD_DIM), 0.1),
        "attn_subln": gain(ks[11], (L, V_HEAD_DIM)),
        "w_attn_out": nrm(ks[12], (L, V_WIDTH, D), V_WIDTH ** -0.5),
        "conv_w": nrm(ks[13], (L, CONV_WIDTH, CONV_CHANNELS), CONV_WIDTH ** -0.5),
        "conv_b": nrm(ks[14], (L, CONV_CHANNELS), 0.01),
        "conv_ln_g": gain(ks[15], (L, CONV_CHANNELS)),
        "conv_ln_b": nrm(ks[16], (L, CONV_CHANNELS), 0.01),
        "w_conv_out": nrm(ks[17], (L, CONV_CHANNELS, D), CONV_CHANNELS ** -0.5),
        "w_out": nrm(ks[18], (L, D, D), D ** -0.5),
        "ffn2_norm": gain(ks[19], (L, D)),
        "ffn2_w_gate_up": nrm(ks[20], (L, D, 2 * D_FF), D ** -0.5),
        "ffn2_w_down": nrm(ks[21], (L, D_FF, D), D_FF ** -0.5),
        "final_norm": gain(ks[22], (D,)),
    }


def reference(x, ffn1_norm, ffn1_w_gate_up, ffn1_w_down, mix_norm, w_in, b_gate,
              lambda_q1, lambda_k1, lambda_q2, lambda_k2, attn_subln, w_attn_out,
              conv_w, conv_b, conv_ln_g, conv_ln_b, w_conv_out, w_out,
              ffn2_norm, ffn2_w_gate_up, ffn2_w_down, final_norm):
    B, S = x.shape[0], x.shape[1]
    cos, sin = rope_tables(S)
    for l in range(DEPTH):
        lam_init = lambda_init(l)
        x = x + 0.5 * swiglu(rmsnorm(x, ffn1_norm[l]), ffn1_w_gate_up[l], ffn1_w_down[l])

        h = rmsnorm(x, mix_norm[l])
        z = h @ w_in[l]
        q, k, v, u, g = jnp.split(z, [OFF_K, OFF_V, OFF_U, OFF_G], axis=-1)

        q = partial_rope(q.reshape(B, S, 2 * N_HEADS, HEAD_DIM), cos, sin)
        k = partial_rope(k.reshape(B, S, 2 * N_HEADS, HEAD_DIM), cos, sin)
        v = v.reshape(B, S, N_HEADS, V_HEAD_DIM)
        lam = (jnp.exp(jnp.sum(lambda_q1[l].astype(jnp.float32) * lambda_k1[l].astype(jnp.float32)))
               - jnp.exp(jnp.sum(lambda_q2[l].astype(jnp.float32) * lambda_k2[l].astype(jnp.float32)))
               + lam_init)
        o = diff_attention(q, k, v, lam)
        o = rmsnorm(o, attn_subln[l]) * (1.0 - lam_init)
        y_a = o.reshape(B, S, V_WIDTH) @ w_attn_out[l]

        u_val, u_gate = jnp.split(u, 2, axis=-1)
        c = u_val * jax.nn.sigmoid(u_gate)
        c = causal_depthwise_conv(c, conv_w[l], conv_b[l])
        c = jax.nn.silu(layernorm(c, conv_ln_g[l], conv_ln_b[l]))
        y_b = c @ w_conv_out[l]

        g_a, g_b = jnp.split(jax.nn.sigmoid(g + b_gate[l]), 2, axis=-1)
        x = x + (g_a * y_a + g_b * y_b) @ w_out[l]

        x = x + 0.5 * swiglu(rmsnorm(x, ffn2_norm[l]), ffn2_w_gate_up[l], ffn2_w_down[l])
    return rmsnorm(x, final_norm)
```

```python
import math
import numpy as np
import concourse.bass as bass
import concourse.mybir as mybir
from concourse.bass_utils import run_bass_kernel_spmd

F32 = mybir.dt.float32
BF16 = mybir.dt.bfloat16
AF = mybir.ActivationFunctionType
ALU = mybir.AluOpType

D = 1024
TK = 512
DFF = 2816
NFF = 22
EPS = 1e-5
LAM_INIT = 0.8 - 0.6 * math.exp(-0.3 * 0)
NEG = -30000.0


class Buf:
    __slots__ = ("name", "w", "r")

    def __init__(self, name):
        self.name = name
        self.w = None
        self.r = []


class Op:
    __slots__ = ("eng", "fn", "deps", "kind", "chan", "cum", "sig", "needed")


class Tracker:
    ENGS = ("pe", "act", "dve", "pool", "sp")

    def __init__(self, nc):
        self.nc = nc
        self.ops = {e: [] for e in self.ENGS}
        self.all = []
        self.chan_cum = {}

    def op(self, eng, fn, reads=(), writes=(), nosame=False):
        o = Op()
        o.eng = eng
        o.fn = fn
        o.kind = "c"
        o.chan = None
        o.cum = 0
        o.sig = 0
        o.needed = False
        deps = set()
        for b in reads:
            if b.w is not None:
                deps.add(b.w)
        for b in writes:
            if b.w is not None:
                deps.add(b.w)
            for r in b.r:
                deps.add(r)
        for b in reads:
            b.r.append(o)
        for b in writes:
            b.w = o
            b.r = []
        deps.discard(o)
        if nosame:
            deps = {d for d in deps if not (d.kind == "c" and d.eng == eng)}
        o.deps = deps
        self.ops[eng].append(o)
        self.all.append(o)
        return o

    def dma(self, queue, chan, fn, reads=(), writes=()):
        o = self.op(queue, fn, reads, writes)
        o.kind = "d"
        o.chan = chan
        self.chan_cum[chan] = self.chan_cum.get(chan, 0) + 16
        o.cum = self.chan_cum[chan]
        return o

    def emit(self):
        nc = self.nc
        for o in self.all:
            for d in o.deps:
                d.needed = True
        for e in self.ENGS:
            c = 0
            for o in self.ops[e]:
                if o.kind == "c" and o.needed:
                    c += 1
                    o.sig = c
        self.esem = {e: nc.alloc_semaphore(name="prog_" + e) for e in self.ENGS}
        self.csem = {ch: nc.alloc_semaphore(name="ch_" + str(ch)) for ch in self.chan_cum}

        def run(e):
            def body(eng):
                waited = {}
                for o in self.ops[e]:
                    need = {}
                    for d in o.deps:
                        if d.kind == "c":
                            key = ("e", d.eng)
                            v = d.sig
                        else:
                            key = ("c", d.chan)
                            v = d.cum
                        if v > need.get(key, 0):
                            need[key] = v
                    for key, v in need.items():
                        if waited.get(key, 0) >= v:
                            continue
                        waited[key] = v
                        sem = self.esem[key[1]] if key[0] == "e" else self.csem[key[1]]
                        eng.wait_ge(sem, v)
                    ins = o.fn(eng)
                    if o.kind == "d":
                        ins.then_inc(self.csem[o.chan], 16)
                    elif o.needed:
                        ins.then_inc(self.esem[e], 1)
            return body

        with nc.Block() as block:
            block.tensor(run("pe"))
            block.scalar(run("act"))
            block.vector(run("dve"))
            block.gpsimd(run("pool"))
            block.sync(run("sp"))


def weight_blocks():
    cat = {}
    for f in ("1", "2"):
        cat["gu" + f] = [(8, 512, [("gu" + f, j * 256, 256, 0), ("gu" + f, DFF + j * 256, 256, 256)]) for j in range(11)]
        cat["dn" + f] = [(22, 128, [("dn" + f, m * 128, 128, 0)]) for m in range(8)]
    cat["wq"] = [(8, 512, [("win", j * 512, 512, 0)]) for j in range(2)]
    cat["wk"] = [(8, 512, [("win", 1024 + j * 512, 512, 0)]) for j in range(2)]
    cat["wv"] = [(8, 512, [("win", 2048 + j * 512, 512, 0)]) for j in range(2)]
    cat["wu"] = [(8, 512, [("win", 3072 + j * 256, 256, 0), ("win", 4096 + j * 256, 256, 256)]) for j in range(4)]
    cat["wg"] = [(8, 512, [("win", 5120 + j * 512, 512, 0)]) for j in range(4)]
    for n in ("wa", "wb", "wo"):
        cat[n] = [(8, 512, [(n, j * 512, 512, 0)]) for j in range(2)]
    return cat


def build(NSLOT):
    assert NSLOT % 2 == 0
    S = NSLOT * TK
    NPAIR = NSLOT // 2
    nc = bass.Bass("TRN2", target_bir_lowering=False)

    def din(name, shape, dt=F32):
        return nc.dram_tensor(name, list(shape), dt, kind="ExternalInput").ap()

    xs = din("xs", [S, D])
    cosT = din("cosT", [128, S])
    sinT = din("sinT", [128, S])
    kbias_d = din("kbias", [128, NSLOT * 4])
    cvec_d = din("cvec", [128, 320])
    subln_d = din("sublnB", [128, 128])
    lamv_d = din("lamv", [128, 256])
    wsrc = {
        "gu1": din("gu1", [D, 2 * DFF]), "dn1": din("dn1", [DFF, D]),
        "win": din("win", [D, 7168]),
        "wa": din("wa", [D, D]), "wb": din("wb", [D, D]), "wo": din("wo", [D, D]),
        "gu2": din("gu2", [D, 2 * DFF]), "dn2": din("dn2", [DFF, D]),
    }
    out_d = nc.dram_tensor("out", [NPAIR * TK, D], F32, kind="ExternalOutput").ap()

    cat = weight_blocks()
    blk_index = {}
    nblk = 0
    for name, blks in cat.items():
        for j in range(len(blks)):
            blk_index[(name, j)] = nblk
            nblk += 1
    wbf = nc.dram_tensor("wbf", [nblk, 128, 4096], BF16).ap()
    kT_s = nc.dram_tensor("kT_s", [16, 64, S], BF16).ap()
    v_s = nc.dram_tensor("v_s", [8, 128, S // 128, 128], BF16).ap()
    qT_s = nc.dram_tensor("qT_s", [2, 16, 64, TK], BF16).ap()

    T = Tracker(nc)

    def sb(name, shape, dt):
        return nc.alloc_sbuf_tensor(name, list(shape), dt)

    stage = sb("stage", [128, 4096], F32)
    xT = sb("xT", [128, 8, TK], F32)
    hT = sb("hT", [128, 8, TK], BF16)
    actb = sb("actb", [128, NFF * TK], BF16)
    NW = 3
    wring = [sb("wring%d" % i, [128, 4096], BF16) for i in range(NW)]
    rstd = sb("rstd", [128, TK], F32)
    accx = sb("accx", [128, TK], F32)
    accx2 = sb("accx2", [128, TK], F32)
    tmpa = [sb("tmpa%d" % i, [128, TK], F32) for i in range(3)]
    ropec = sb("ropec", [128, TK], F32)
    ropes = sb("ropes", [128, TK], F32)
    r12 = sb("r12", [128, 2, TK], F32)
    qk = sb("qk", [128, 8, TK], BF16)
    vtok = sb("vtok", [128, 8, 4, 128], BF16)
    NKV = 6
    kbuf = [sb("kbuf%d" % i, [128, TK], BF16) for i in range(NKV)]
    vbuf = [sb("vbuf%d" % i, [128, 4, 132], BF16) for i in range(NKV)]
    qTb = [sb("qTb%d" % i, [128, TK], BF16) for i in range(2)]
    NP = 3
    pT = [sb("pT%d" % i, [128, 2, TK], BF16) for i in range(NP)]
    sqr = [sb("sqr%d" % i, [128, TK], BF16) for i in range(3)]
    Otok = sb("Otok", [128, 4, D], BF16)
    ohs = sb("ohs", [128, 4, 128], F32)
    on0 = sb("on0", [128, 4, 128], F32)
    osb = sb("osb", [128, 8 * 129], F32)
    junk = sb("junk", [128, 128], F32)
    OT = sb("OT", [128, 8, TK], BF16)
    cb = sb("cb", [128, 8, TK], BF16)
    gates = sb("gates", [128, 16, TK], BF16)
    hhalo = sb("hhalo", [128, 8, 32], BF16)
    cvec = sb("cvec_sb", [128, 320], F32)
    sublnB = sb("sublnB_sb", [128, 128], F32)
    lamv = sb("lamv_sb", [128, 256], F32)
    lamt = sb("lamt", [128, 8], F32)
    kbias = sb("kbias_sb", [128, NSLOT * 4], F32)
    identf = sb("identf", [128, 128], F32)
    identb = sb("identb", [128, 128], BF16)
    onesb = sb("onesb", [128, 128], BF16)
    maskneg = sb("maskneg", [128, 128], BF16)
    maskf = sb("maskf", [128, 128], F32)
    small = sb("small", [128, 64], F32)

    act3 = actb[:].rearrange("p (c t) -> p c t", t=TK)
    cin = actb[:].bitcast(F32)[:, 0:8 * 544].rearrange("p (c t) -> p c t", t=544)
    stage_x = stage[:].rearrange("p (a f) -> p a f", f=D)
    cacc = stage[:].rearrange("p (c t) -> p c t", t=TK)

    PS = nc.alloc_psum_tensor("ps", [128, 8, 512], F32)
    P = [PS[:, i, :] for i in range(8)]

    B_stage = Buf("stage")
    B_cacc = [Buf("cacc%d" % i) for i in range(8)]
    B_stage_all = [B_stage] + B_cacc
    B_accx = Buf("accx")
    B_accx2 = Buf("accx2")
    B_xT = [Buf("xT%d" % i) for i in range(8)]
    B_hT = [Buf("hT%d" % i) for i in range(8)]
    B_act = [Buf("act%d" % i) for i in range(NFF)]
    B_wr = [Buf("wr%d" % i) for i in range(NW)]
    B_rstd = Buf("rstd")
    B_tmpa = [Buf("tmpa%d" % i) for i in range(3)]
    B_rope = Buf("rope")
    B_rope2 = Buf("rope2")
    B_r12 = [Buf("r1"), Buf("r2")]
    B_qk = [Buf("qk%d" % i) for i in range(8)]
    B_vtok = Buf("vtok")
    B_kb = [Buf("kb%d" % i) for i in range(NKV)]
    B_vb = [Buf("vb%d" % i) for i in range(NKV)]
    B_qT = [Buf("qT%d" % i) for i in range(2)]
    B_pT = [Buf("pT%d" % i) for i in range(NP)]
    B_sqr = [Buf("sqr%d" % i) for i in range(3)]
    B_Otok = [Buf("Otok%d" % i) for i in range(8)]
    B_ohs = Buf("ohs")
    B_on0 = Buf("on0")
    B_osb = [Buf("osb%d" % i) for i in range(3)]
    B_junk = Buf("junk")
    B_OT = [Buf("OT%d" % i) for i in range(8)]
    B_cb = [Buf("cb%d" % i) for i in range(8)]
    B_gates = [Buf("g%d" % i) for i in range(16)]
    B_hhalo = Buf("hhalo")
    B_const = Buf("const")
    B_small = Buf("small")
    B_P = [Buf("P%d" % i) for i in range(8)]
    B_wbf = [Buf("wbf%d" % i) for i in range(nblk)]
    B_kTs = [[Buf("kTs%d_%d" % (s, m)) for m in range(8)] for s in range(NSLOT)]
    B_vs = [Buf("vs%d" % s) for s in range(NSLOT)]
    B_qTs = [[Buf("qTs%d_%d" % (p_, m)) for m in range(8)] for p_ in range(2)]
    B_out = [Buf("out%d" % i) for i in range(4)]

    setup_loads = [
        (cvec[:], cvec_d), (sublnB[:], subln_d), (lamv[:], lamv_d), (kbias[:], kbias_d),
    ]
    for dst, src in setup_loads:
        T.dma("sp", "setup", lambda e, d=dst, s=src: e.dma_start(out=d, in_=s), writes=[B_const])
    T.op("pool", lambda e: e.memset(identf[:], 0.0), writes=[B_const])
    T.op("pool", lambda e: e.affine_select(out=identf[:], in_=identf[:], pattern=[[-1, 128]], compare_op=ALU.not_equal,
                                           fill=1.0, base=0, channel_multiplier=1), reads=[B_const], writes=[B_const])
    T.op("pool", lambda e: e.tensor_copy(out=identb[:], in_=identf[:]), reads=[B_const], writes=[B_const])
    T.op("pool", lambda e: e.memset(onesb[:], 1.0), writes=[B_const])
    T.op("pool", lambda e: e.memset(maskf[:], 0.0), writes=[B_const])
    T.op("pool", lambda e: e.affine_select(out=maskf[:], in_=maskf[:], pattern=[[1, 128]], compare_op=ALU.is_ge,
                                           fill=NEG, base=0, channel_multiplier=-1), reads=[B_const], writes=[B_const])
    T.op("pool", lambda e: e.tensor_copy(out=maskneg[:], in_=maskf[:]), reads=[B_const], writes=[B_const])
    for i in range(NKV):
        T.op("pool", lambda e, i=i: e.memset(vbuf[i][:, :, 128:132], 1.0), writes=[B_vb[i]])
    T.op("dve", lambda e: e.tensor_tensor(out=junk[:, 0:64], in0=lamv[:, 0:64], in1=lamv[:, 64:128], op=ALU.mult), reads=[B_const], writes=[B_junk])
    T.op("dve", lambda e: e.reduce_sum(out=lamt[:, 0:1], in_=junk[:, 0:64], axis=mybir.AxisListType.X), reads=[B_junk], writes=[B_small])
    T.op("dve", lambda e: e.tensor_tensor(out=junk[:, 64:128], in0=lamv[:, 128:192], in1=lamv[:, 192:256], op=ALU.mult), reads=[B_const], writes=[B_junk])
    T.op("dve", lambda e: e.reduce_sum(out=lamt[:, 1:2], in_=junk[:, 64:128], axis=mybir.AxisListType.X), reads=[B_junk], writes=[B_small])
    T.op("act", lambda e: e.activation(out=lamt[:, 2:4], in_=lamt[:, 0:2], func=AF.Exp), reads=[B_small], writes=[B_small])
    T.op("dve", lambda e: e.tensor_tensor(out=lamt[:, 4:5], in0=lamt[:, 3:4], in1=lamt[:, 2:3], op=ALU.subtract), reads=[B_small], writes=[B_small])
    T.op("dve", lambda e: e.tensor_scalar(out=lamt[:, 5:6], in0=lamt[:, 4:5], scalar1=-LAM_INIT, scalar2=None, op0=ALU.add), reads=[B_small], writes=[B_small])
    neglam = lamt[:, 5:6]

    pre_stage = [(stage, "STAGE"), (xT, None)]
    cast_engs = ["dve", "pool", "act"]
    B_xTall = Buf("xTall")
    ci = 0
    for name, blks in cat.items():
        for j, (KC, W, parts) in enumerate(blks):
            bi = blk_index[(name, j)]
            st_t, st_b = pre_stage[ci % 2]
            st_bl = B_stage_all if st_b == "STAGE" else [B_xTall]
            st_flat = st_t[:] if st_t is stage else st_t[:].rearrange("p c t -> p (c t)")
            st_v = st_flat[:, 0:KC * W].rearrange("p (k w) -> p k w", w=W)
            for (sn, c0, ncol, d0) in parts:
                src = wsrc[sn][:, c0:c0 + ncol].rearrange("(k p) n -> p k n", p=128)
                T.dma("sp", "pre_ld%d" % (ci % 2), lambda e, d=st_v[:, :, d0:d0 + ncol], s=src: e.dma_start(out=d, in_=s), writes=st_bl)
            wr = wring[ci % NW]
            ce = cast_engs[ci % 3]
            if ce == "act":
                T.op("act", lambda e, o=wr[:, 0:KC * W], i=st_flat[:, 0:KC * W]: e.activation(out=o, in_=i, func=AF.Copy), reads=st_bl, writes=[B_wr[ci % NW]])
            else:
                T.op(ce, lambda e, o=wr[:, 0:KC * W], i=st_flat[:, 0:KC * W]: e.tensor_copy(out=o, in_=i), reads=st_bl, writes=[B_wr[ci % NW]])
            T.dma("pool", "pre_st%d" % (ci % NW), lambda e, o=wbf[bi, :, 0:KC * W], i=wr[:, 0:KC * W]: e.dma_start(out=o, in_=i), reads=[B_wr[ci % NW]], writes=[B_wbf[bi]])
            ci += 1
    T.op("dve", lambda e: e.memset(small[:, 0:1], 0.0), reads=[B_xTall], writes=[B_small] + B_xT)

    wctr = [0]

    def wload(name, j):
        KC, W, _ = cat[name][j]
        bi = blk_index[(name, j)]
        r = wctr[0] % NW
        wctr[0] += 1
        T.dma("sp", "wr%d" % r, lambda e, o=wring[r][:, 0:KC * W], i=wbf[bi, :, 0:KC * W]: e.dma_start(out=o, in_=i), reads=[B_wbf[bi]], writes=[B_wr[r]])
        return wring[r][:, 0:KC * W].rearrange("p (k w) -> p k w", w=W), B_wr[r]

    bctr = [0]

    def bank():
        i = bctr[0] % 4
        bctr[0] += 1
        return P[i], B_P[i]

    ectr = [0]

    def evac_copy(out_ap, in_ap, reads, writes):
        ectr[0] += 1
        if ectr[0] % 2 == 0:
            T.op("act", lambda e: e.activation(out=out_ap, in_=in_ap, func=AF.Copy), reads=reads, writes=writes)
        else:
            T.op("dve", lambda e: e.tensor_copy(out=out_ap, in_=in_ap), reads=reads, writes=writes)

    def mm(out, lhsT, rhs, start, stop, reads, writes, **kw):
        T.op("pe", lambda e: e.matmul(out, lhsT=lhsT, rhs=rhs, start=start, stop=stop, **kw), reads=reads, writes=writes, nosame=True)

    tctr = [0]

    def tmp():
        i = tctr[0] % 3
        tctr[0] += 1
        return tmpa[i], B_tmpa[i]

    sqctr = [0]

    pend_stats = []

    def stats_flush():
        while pend_stats:
            c, i = pend_stats.pop(0)
            mm(P[7][:], onesb[:], sqr[i][:], c == 0, c == 7, [B_sqr[i], B_const], [B_P[7]])

    def stats_chunk(c):
        stats_flush()
        i = sqctr[0] % 3
        sqctr[0] += 1
        T.op("act", lambda e: e.activation(out=sqr[i][:], in_=xT[:, c, :], func=AF.Square), reads=[B_xT[c]], writes=[B_sqr[i]])
        pend_stats.append((c, i))

    def rmsnorm(gcol, dst, B_dst):
        stats_flush()
        t, bt = tmp()
        T.op("act", lambda e: e.activation(out=t[:], in_=P[7][:], func=AF.Sqrt, bias=EPS, scale=1.0 / D), reads=[B_P[7]], writes=[bt])
        T.op("dve", lambda e: e.reciprocal(out=rstd[:], in_=t[:]), reads=[bt], writes=[B_rstd])
        for c in range(8):
            T.op("dve", lambda e, c=c: e.scalar_tensor_tensor(out=dst[:, c, :], in0=xT[:, c, :], scalar=cvec[:, gcol + c:gcol + c + 1],
                                                             in1=rstd[:], op0=ALU.mult, op1=ALU.mult),
                 reads=[B_xT[c], B_rstd, B_const], writes=[B_dst[c]])

    def ffn(f):
        for j in range(11):
            w, bw = wload("gu" + f, j)
            for i in range(2):
                m = 2 * j + i
                pa, bpa = bank()
                pb, bpb = bank()
                for kc in range(8):
                    mm(pa[:], w[:, kc, i * 128:(i + 1) * 128], hT[:, kc, :], kc == 0, kc == 7, [bw, B_hT[kc]], [bpa])
                for kc in range(8):
                    mm(pb[:], w[:, kc, 256 + i * 128:256 + (i + 1) * 128], hT[:, kc, :], kc == 0, kc == 7, [bw, B_hT[kc]], [bpb])
                t, bt = tmp()
                T.op("act", lambda e, t=t, pa=pa: e.activation(out=t[:], in_=pa[:], func=AF.Silu), reads=[bpa], writes=[bt])
                T.op("dve", lambda e, t=t, pb=pb, m=m: e.tensor_tensor(out=act3[:, m, :], in0=pb[:], in1=t[:], op=ALU.mult), reads=[bpb, bt], writes=[B_act[m]])
        for m in range(8):
            w, bw = wload("dn" + f, m)
            p, bp = bank()
            for kc in range(NFF):
                mm(p[:], w[:, kc, :], act3[:, kc, :], kc == 0, kc == NFF - 1, [bw, B_act[kc]], [bp])
            T.op("dve", lambda e, p=p, m=m: e.scalar_tensor_tensor(out=xT[:, m, :], in0=p[:], scalar=0.5, in1=xT[:, m, :], op0=ALU.mult, op1=ALU.add),
                 reads=[bp, B_xT[m]], writes=[B_xT[m]])
            stats_chunk(m)

    def issue_x(s):
        for a in range(4):
            src = xs[s * TK + a * 128:s * TK + (a + 1) * 128, :]
            T.dma("sp", "xld%d" % a, lambda e, a=a, src=src: e.dma_start(out=stage_x[:, a, :], in_=src), writes=[B_cacc[2 * a], B_cacc[2 * a + 1]])

    def load_x():
        for fc in range(8):
            p, bp = bank()
            for a in range(4):
                mm(p[:, a * 128:(a + 1) * 128], stage_x[:, a, fc * 128:(fc + 1) * 128], identf[:], True, True, [B_cacc[2 * a], B_cacc[2 * a + 1], B_const], [bp])
            evac_copy(xT[:, fc, :], p[:], [bp], [B_xT[fc]])
            stats_chunk(fc)

    def load_rope(s):
        T.dma("sp", "rope", lambda e: e.dma_start(out=ropec[:], in_=cosT[:, s * TK:(s + 1) * TK]), writes=[B_rope])
        T.dma("sp", "rope2", lambda e: e.dma_start(out=ropes[:], in_=sinT[:, s * TK:(s + 1) * TK]), writes=[B_rope2])

    def proj_qk(wname, dst_fn, B_dst):
        for j in range(2):
            w, bw = wload(wname, j)
            for i in range(4):
                m = 4 * j + i
                p, bp = bank()
                for kc in range(8):
                    mm(p[:], w[:, kc, i * 128:(i + 1) * 128], hT[:, kc, :], kc == 0, kc == 7, [bw, B_hT[kc]], [bp])
                if m < 2:
                    evac_copy(r12[:, m, :], p[:], [bp], [B_r12[m]])
                else:
                    evac_copy(qk[:, m, :], p[:], [bp], [B_qk[m]])
                if m == 1:
                    t1, b1 = tmp()
                    t2, b2 = tmp()
                    T.op("dve", lambda e, t1=t1: e.tensor_tensor(out=t1[:], in0=r12[:, 0, :], in1=ropec[:], op=ALU.mult), reads=[B_r12[0], B_rope], writes=[b1])
                    T.op("dve", lambda e, t2=t2: e.tensor_tensor(out=t2[:], in0=r12[:, 1, :], in1=ropes[:], op=ALU.mult), reads=[B_r12[1], B_rope2], writes=[b2])
                    T.op("dve", lambda e, t1=t1, t2=t2: e.tensor_tensor(out=qk[:, 0, :], in0=t1[:], in1=t2[:], op=ALU.subtract), reads=[b1, b2], writes=[B_qk[0]])
                    t3, b3 = tmp()
                    t4, b4 = tmp()
                    T.op("dve", lambda e, t3=t3: e.tensor_tensor(out=t3[:], in0=r12[:, 1, :], in1=ropec[:], op=ALU.mult), reads=[B_r12[1], B_rope], writes=[b3])
                    T.op("dve", lambda e, t4=t4: e.tensor_tensor(out=t4[:], in0=r12[:, 0, :], in1=ropes[:], op=ALU.mult), reads=[B_r12[0], B_rope2], writes=[b4])
                    T.op("dve", lambda e, t3=t3, t4=t4: e.tensor_tensor(out=qk[:, 1, :], in0=t3[:], in1=t4[:], op=ALU.add), reads=[b3, b4], writes=[B_qk[1]])
        for m in range(8):
            dst, chan = dst_fn(m)
            T.dma("pool", chan, lambda e, dst=dst, m=m: e.dma_start(out=dst, in_=qk[:, m, :]), reads=[B_qk[m]], writes=[B_dst[m]])

    def qk_dst(base, tok0, ntok):
        def f(m):
            if m == 0:
                return base[:, 0:8, tok0:tok0 + ntok]
            if m == 1:
                return base[:, 8:16, tok0:tok0 + ntok]
            mp = m - 2
            gq = mp % 2
            db = mp // 2
            return base[8 * gq:8 * gq + 8, 16 + 16 * db:32 + 16 * db, tok0:tok0 + ntok]
        return f

    def proj_v(s):
        for j in range(2):
            w, bw = wload("wv", j)
            for a in range(4):
                p, bp = bank()
                for kc in range(8):
                    mm(p[:], hT[:, kc, a * 128:(a + 1) * 128], w[:, kc, :], kc == 0, kc == 7, [bw, B_hT[kc]], [bp])
                evac_copy(vtok[:, 4 * j:4 * j + 4, a, :], p[:].rearrange("p (h e) -> p h e", e=128), [bp], [B_vtok])
        dst = v_s[:, :, 4 * s:4 * s + 4, :].rearrange("h p k e -> p h k e")
        T.dma("pool", "vw%d" % (s % 2), lambda e: e.dma_start(out=dst, in_=vtok[:]), reads=[B_vtok], writes=[B_vs[s]])

    def proj_u():
        ph, bph = P[7], B_P[7]
        for j in range(4):
            w, bw = wload("wu", j)
            for i in range(2):
                m = 2 * j + i
                for kc in range(8):
                    mm(ph[:, m * 32:(m + 1) * 32], w[:, kc, i * 128:(i + 1) * 128], hhalo[:, kc, :], kc == 0, kc == 7, [bw, B_hhalo], [bph])
                for kc in range(8):
                    mm(ph[:, 256 + m * 32:256 + (m + 1) * 32], w[:, kc, 256 + i * 128:256 + (i + 1) * 128], hhalo[:, kc, :], kc == 0, kc == 7, [bw, B_hhalo], [bph])
                pa, bpa = bank()
                pb, bpb = bank()
                for kc in range(8):
                    mm(pa[:], w[:, kc, i * 128:(i + 1) * 128], hT[:, kc, :], kc == 0, kc == 7, [bw, B_hT[kc]], [bpa])
                for kc in range(8):
                    mm(pb[:], w[:, kc, 256 + i * 128:256 + (i + 1) * 128], hT[:, kc, :], kc == 0, kc == 7, [bw, B_hT[kc]], [bpb])
                t, bt = tmp()
                T.op("act", lambda e, t=t, pb=pb: e.activation(out=t[:], in_=pb[:], func=AF.Sigmoid), reads=[bpb], writes=[bt])
                T.op("dve", lambda e, t=t, pa=pa, m=m: e.tensor_tensor(out=cin[:, m, 32:544], in0=pa[:], in1=t[:], op=ALU.mult), reads=[bpa, bt], writes=[B_act[m]])
        t, bt = tmp()
        T.op("act", lambda e, t=t: e.activation(out=t[:, 0:256], in_=ph[:, 256:512], func=AF.Sigmoid), reads=[bph], writes=[bt])
        T.op("dve", lambda e, t=t: e.tensor_tensor(out=cin[:, :, 0:32], in0=ph[:, 0:256].rearrange("p (c t) -> p c t", t=32),
                                                  in1=t[:, 0:256].rearrange("p (c t) -> p c t", t=32), op=ALU.mult), reads=[bph, bt], writes=B_act[0:8])

    def proj_g():
        for j in range(4):
            w, bw = wload("wg", j)
            for i in range(4):
                cc = 4 * j + i
                p, bp = bank()
                for kc in range(8):
                    mm(p[:], w[:, kc, i * 128:(i + 1) * 128], hT[:, kc, :], kc == 0, kc == 7, [bw, B_hT[kc]], [bp])
                T.op("act", lambda e, p=p, cc=cc: e.activation(out=gates[:, cc, :], in_=p[:], func=AF.Sigmoid, bias=cvec[:, 32 + cc:33 + cc]),
                     reads=[bp, B_const], writes=[B_gates[cc]])

    conv_first = [True]

    def conv_chunk(m):
        for j in range(31):
            wcol = cvec[:, 72 + m * 31 + j:73 + m * 31 + j]
            if j == 0:
                T.op("dve", lambda e, wcol=wcol: e.tensor_scalar(out=cacc[:, m, :], in0=cin[:, m, 2:2 + TK], scalar1=wcol, scalar2=cvec[:, 48 + m:49 + m],
                                                                 op0=ALU.mult, op1=ALU.add),
                     reads=[B_act[m], B_const], writes=[B_cacc[m]])
            else:
                T.op("dve", lambda e, wcol=wcol, j=j: e.scalar_tensor_tensor(out=cacc[:, m, :], in0=cin[:, m, j + 2:j + 2 + TK], scalar=wcol, in1=cacc[:, m, :],
                                                                              op0=ALU.mult, op1=ALU.add),
                     reads=[B_act[m], B_const, B_cacc[m]], writes=[B_cacc[m]])
        if conv_first[0]:
            conv_first[0] = False
            T.op("dve", lambda e: e.tensor_copy(out=accx[:], in_=cacc[:, m, :]), reads=[B_cacc[m]], writes=[B_accx])
            T.op("dve", lambda e: e.tensor_tensor(out=accx2[:], in0=cacc[:, m, :], in1=cacc[:, m, :], op=ALU.mult), reads=[B_cacc[m]], writes=[B_accx2])
        else:
            t, bt = tmp()
            T.op("dve", lambda e: e.tensor_tensor(out=accx[:], in0=accx[:], in1=cacc[:, m, :], op=ALU.add), reads=[B_cacc[m], B_accx], writes=[B_accx])
            T.op("dve", lambda e: e.tensor_tensor(out=t[:], in0=cacc[:, m, :], in1=cacc[:, m, :], op=ALU.mult), reads=[B_cacc[m]], writes=[bt])
            T.op("dve", lambda e: e.tensor_tensor(out=accx2[:], in0=accx2[:], in1=t[:], op=ALU.add), reads=[bt, B_accx2], writes=[B_accx2])

    def conv_ln():
        conv_first[0] = True
        T.op("dve", lambda e: e.tensor_copy(out=sqr[0][:], in_=accx[:]), reads=[B_accx], writes=[B_sqr[0]])
        T.op("dve", lambda e: e.tensor_copy(out=sqr[1][:], in_=accx2[:]), reads=[B_accx2], writes=[B_sqr[1]])
        mm(P[7][:], onesb[:], sqr[0][:], True, True, [B_sqr[0], B_const], [B_P[7]])
        ps2, bps2 = bank()
        mm(ps2[:], onesb[:], sqr[1][:], True, True, [B_sqr[1], B_const], [bps2])
        mean, bmean = tmp()
        T.op("dve", lambda e: e.tensor_scalar(out=mean[:], in0=P[7][:], scalar1=1.0 / D, scalar2=None, op0=ALU.mult), reads=[B_P[7]], writes=[bmean])
        msq, bmsq = tmp()
        T.op("dve", lambda e: e.tensor_tensor(out=msq[:], in0=mean[:], in1=mean[:], op=ALU.mult), reads=[bmean], writes=[bmsq])
        T.op("dve", lambda e: e.scalar_tensor_tensor(out=msq[:], in0=ps2[:], scalar=1.0 / D, in1=msq[:], op0=ALU.mult, op1=ALU.subtract), reads=[bps2, bmsq], writes=[bmsq])
        T.op("act", lambda e: e.activation(out=msq[:], in_=msq[:], func=AF.Sqrt, bias=EPS, scale=1.0), reads=[bmsq], writes=[bmsq])
        T.op("dve", lambda e: e.reciprocal(out=rstd[:], in_=msq[:]), reads=[bmsq], writes=[B_rstd])
        for m in range(8):
            T.op("dve", lambda e, m=m: e.tensor_tensor(out=cacc[:, m, :], in0=cacc[:, m, :], in1=mean[:], op=ALU.subtract), reads=[B_cacc[m], bmean], writes=[B_cacc[m]])
            T.op("dve", lambda e, m=m: e.tensor_tensor(out=cacc[:, m, :], in0=cacc[:, m, :], in1=rstd[:], op=ALU.mult), reads=[B_cacc[m], B_rstd], writes=[B_cacc[m]])
            T.op("act", lambda e, m=m: e.activation(out=cb[:, m, :], in_=cacc[:, m, :], func=AF.Silu, bias=cvec[:, 64 + m:65 + m], scale=cvec[:, 56 + m:57 + m]),
                 reads=[B_cacc[m], B_const], writes=[B_cb[m]])

    kvctr = [0]
    pctr = [0]
    oslot = {}
    for idx in range(8):
        oslot[(idx // 4, idx % 4)] = (4 + idx // 3, (idx % 3) * 129)

    sgc = [0]

    def attention(s, g):
        par = g % 2
        pend_tr = []
        conv_chunk(0)
        for h in range(8):
            qb_t, bq = qTb[h % 2], B_qT[h % 2]
            qsrc = qT_s[par, 2 * h:2 * h + 2, :, :].rearrange("c d t -> (c d) t")
            T.dma("sp", "qld%d" % (h % 2), lambda e, qb_t=qb_t, qsrc=qsrc: e.dma_start(out=qb_t[:], in_=qsrc), reads=B_qTs[par], writes=[bq])
            steps = [(sl, t) for sl in range(s + 1) for t in range(4)]
            rmap = {}

            def kv_load(sl):
                if sl in rmap:
                    return rmap[sl]
                r = kvctr[0] % NKV
                kvctr[0] += 1
                ksrc = kT_s[2 * h:2 * h + 2, :, sl * TK:(sl + 1) * TK].rearrange("c d t -> (c d) t")
                vsrc = v_s[h, :, 4 * sl:4 * sl + 4, :]
                T.dma("sp", "kv%d" % r, lambda e: e.dma_start(out=kbuf[r][:], in_=ksrc), reads=B_kTs[sl], writes=[B_kb[r]])
                T.dma("sp", "vv%d" % r, lambda e: e.dma_start(out=vbuf[r][:, :, 0:128], in_=vsrc), reads=[B_vs[sl]], writes=[B_vb[r]])
                rmap[sl] = r
                return r

            started = set()

            def qk_step(i):
                sl, t = steps[i]
                r = kv_load(sl)
                diag = sl == s
                kt = 4 * sl + t
                q0 = 128 * t if diag else 0
                gi = sgc[0] % 2
                sgc[0] += 1
                for c in range(2):
                    bk = 2 * gi + c
                    mm(P[bk][:, q0:TK], kbuf[r][64 * c:64 * c + 64, t * 128:(t + 1) * 128], qb_t[64 * c:64 * c + 64, q0:TK], True, not diag,
                       [B_kb[r], bq], [B_P[bk]])
                    if diag:
                        mm(P[bk][:, q0:q0 + 128], identb[:], maskneg[:], False, True, [B_const], [B_P[bk]])
                pi = pctr[0] % NP
                pctr[0] += 1
                if kt < 4:
                    T.op("act", lambda e: e.activation(out=pT[pi][:, :, q0:TK], in_=PS[:, 2 * gi:2 * gi + 2, q0:TK], func=AF.Exp,
                                                       bias=kbias[:, kt:kt + 1], scale=0.125),
                         reads=[B_P[2 * gi], B_P[2 * gi + 1], B_const], writes=[B_pT[pi]])
                else:
                    T.op("act", lambda e: e.activation(out=pT[pi][:, :, q0:TK], in_=PS[:, 2 * gi:2 * gi + 2, q0:TK], func=AF.Exp, scale=0.125),
                         reads=[B_P[2 * gi], B_P[2 * gi + 1]], writes=[B_pT[pi]])
                return pi

            def pv_step(i, pi):
                sl, t = steps[i]
                r = rmap[sl]
                diag = sl == s
                for c in range(2):
                    for qb in range(t if diag else 0, 4):
                        bk, off = oslot[(c, qb)]
                        first = bk not in started
                        started.add(bk)
                        mm(P[bk][:, off:off + 129], pT[pi][:, c, qb * 128:(qb + 1) * 128], vbuf[r][:, t, 0:129], first, bool(diag and t == qb),
                           [B_pT[pi], B_vb[r]], [B_P[bk]], skip_group_check=True)

            nst = len(steps)
            pis = {0: qk_step(0), 1: qk_step(1)}
            for i in range(nst):
                if i + 2 < nst:
                    pis[i + 2] = qk_step(i + 2)
                pv_step(i, pis[i])
                if i == 5 and pend_tr:
                    pend_tr.pop(0)()
            for bk in (4, 5, 6):
                n = 3 if bk < 6 else 2
                eng = "dve"
                if eng == "act":
                    T.op("act", lambda e, bk=bk, n=n: e.activation(out=osb[:, (bk - 4) * 387:(bk - 4) * 387 + n * 129], in_=P[bk][:, 0:n * 129], func=AF.Copy),
                         reads=[B_P[bk]], writes=[B_osb[bk - 4]])
                else:
                    T.op("dve", lambda e, bk=bk, n=n: e.tensor_copy(out=osb[:, (bk - 4) * 387:(bk - 4) * 387 + n * 129], in_=P[bk][:, 0:n * 129]),
                         reads=[B_P[bk]], writes=[B_osb[bk - 4]])
            lview = osb[:, 0:8 * 129].rearrange("p (a b) -> p a b", b=129)[:, :, 128]
            T.op("dve", lambda e: e.reciprocal(out=small[:, 0:8], in_=lview), reads=B_osb, writes=[B_small])
            T.op("dve", lambda e: e.tensor_scalar(out=small[:, 8:12], in0=small[:, 4:8], scalar1=neglam, scalar2=None, op0=ALU.mult), reads=[B_small], writes=[B_small])
            for qb in range(4):
                o0 = qb * 129
                o1 = (4 + qb) * 129
                T.op("dve", lambda e, qb=qb, o0=o0: e.tensor_scalar(out=on0[:, qb, :], in0=osb[:, o0:o0 + 128], scalar1=small[:, qb:qb + 1], scalar2=None, op0=ALU.mult),
                     reads=B_osb + [B_small], writes=[B_on0])
            for qb in range(4):
                o1 = (4 + qb) * 129
                T.op("dve", lambda e, qb=qb, o1=o1: e.scalar_tensor_tensor(out=ohs[:, qb, :], in0=osb[:, o1:o1 + 128], scalar=small[:, 8 + qb:9 + qb], in1=on0[:, qb, :],
                                                                           op0=ALU.mult, op1=ALU.add),
                     reads=B_osb + [B_small, B_on0], writes=[B_ohs])
            for qb in range(4):
                T.op("dve", lambda e, qb=qb: e.scalar_tensor_tensor(out=junk[:], in0=ohs[:, qb, :], scalar=1.0, in1=ohs[:, qb, :], op0=ALU.mult, op1=ALU.mult,
                                                                    accum_out=small[:, 16 + qb:17 + qb]),
                     reads=[B_ohs], writes=[B_junk, B_small])
            def do_tr(h=h):
                k1 = (1.0 - LAM_INIT) ** 2
                T.op("act", lambda e: e.activation(out=small[:, 20:24], in_=small[:, 16:20], func=AF.Sqrt, bias=EPS / k1, scale=1.0 / (128.0 * k1)), reads=[B_small], writes=[B_small])
                T.op("dve", lambda e: e.reciprocal(out=small[:, 24:28], in_=small[:, 20:24]), reads=[B_small], writes=[B_small])
                for qb in range(4):
                    T.op("dve", lambda e, qb=qb: e.scalar_tensor_tensor(out=Otok[:, qb, h * 128:(h + 1) * 128], in0=ohs[:, qb, :], scalar=small[:, 24 + qb:25 + qb], in1=sublnB[:],
                                                                        op0=ALU.mult, op1=ALU.mult),
                         reads=[B_ohs, B_small, B_const], writes=[B_Otok[h]])
                p, bp = P[7], B_P[7]
                for qb in range(4):
                    mm(p[:, qb * 128:(qb + 1) * 128], Otok[:, qb, h * 128:(h + 1) * 128], identb[:], True, True, [B_Otok[h], B_const], [bp])
                evac_copy(OT[:, h, :], p[:], [bp], [B_OT[h]])
            pend_tr.append(do_tr)
            if h + 1 < 8:
                conv_chunk(h + 1)
        while pend_tr:
            pend_tr.pop(0)()

    def mix_out():
        for j in range(2):
            wA, bwA = wload("wa", j)
            for i in range(4):
                m = 4 * j + i
                pa, bpa = bank()
                for kc in range(8):
                    mm(pa[:], wA[:, kc, i * 128:(i + 1) * 128], OT[:, kc, :], kc == 0, kc == 7, [bwA, B_OT[kc]], [bpa])
                T.op("dve", lambda e, pa=pa, m=m: e.tensor_tensor(out=hT[:, m, :], in0=pa[:], in1=gates[:, m, :], op=ALU.mult), reads=[bpa, B_gates[m]], writes=[B_hT[m]])
        for j in range(2):
            wB, bwB = wload("wb", j)
            for i in range(4):
                m = 4 * j + i
                pb, bpb = bank()
                for kc in range(8):
                    mm(pb[:], wB[:, kc, i * 128:(i + 1) * 128], cb[:, kc, :], kc == 0, kc == 7, [bwB, B_cb[kc]], [bpb])
                t2, b2 = tmp()
                T.op("dve", lambda e, t2=t2, pb=pb, m=m: e.tensor_tensor(out=t2[:], in0=pb[:], in1=gates[:, 8 + m, :], op=ALU.mult), reads=[bpb, B_gates[8 + m]], writes=[b2])
                T.op("dve", lambda e, t2=t2, m=m: e.tensor_tensor(out=qk[:, m, :], in0=t2[:], in1=hT[:, m, :], op=ALU.add), reads=[b2, B_hT[m]], writes=[B_qk[m]])
        for j in range(2):
            w, bw = wload("wo", j)
            for i in range(4):
                m = 4 * j + i
                p, bp = bank()
                for kc in range(8):
                    mm(p[:], w[:, kc, i * 128:(i + 1) * 128], qk[:, kc, :], kc == 0, kc == 7, [bw, B_qk[kc]], [bp])
                T.op("dve", lambda e, p=p, m=m: e.tensor_tensor(out=xT[:, m, :], in0=p[:], in1=xT[:, m, :], op=ALU.add), reads=[bp, B_xT[m]], writes=[B_xT[m]])
                stats_chunk(m)

    def store_out(g):
        for a in range(4):
            for half in range(2):
                p, bp = bank()
                for i in range(4):
                    fc = half * 4 + i
                    mm(p[:, i * 128:(i + 1) * 128], xT[:, fc, a * 128:(a + 1) * 128], identf[:], True, True, [B_xT[fc], B_const], [bp])
                evac_copy(stage_x[:, a, half * 512:(half + 1) * 512], p[:], [bp], [B_cacc[2 * a + half]])
            dst = out_d[g * TK + a * 128:g * TK + (a + 1) * 128, :]
            T.dma("pool", "outst%d" % a, lambda e, a=a, dst=dst: e.dma_start(out=dst, in_=stage_x[:, a, :]), reads=[B_cacc[2 * a], B_cacc[2 * a + 1]], writes=[B_out[a]])

    for g in range(NPAIR):
        s0, s1 = 2 * g, 2 * g + 1
        issue_x(s0)
        load_x()
        issue_x(s1)
        rmsnorm(0, hT, B_hT)
        ffn("1")
        rmsnorm(8, hT, B_hT)
        load_rope(s0)
        proj_qk("wk", lambda m, s0=s0: (qk_dst(kT_s, s0 * TK, TK)(m), "kw%d" % (s0 % 2)), B_kTs[s0])
        proj_v(s0)
        T.op("pool", lambda e: e.tensor_copy(out=hhalo[:], in_=hT[:, :, TK - 32:TK]), reads=B_hT, writes=[B_hhalo])
        load_x()
        rmsnorm(0, hT, B_hT)
        ffn("1")
        rmsnorm(8, hT, B_hT)
        load_rope(s1)
        proj_qk("wk", lambda m, s1=s1: (qk_dst(kT_s, s1 * TK, TK)(m), "kw%d" % (s1 % 2)), B_kTs[s1])
        proj_v(s1)
        proj_qk("wq", lambda m, g=g: (qk_dst(qT_s[g % 2], 0, TK)(m), "qw"), B_qTs[g % 2])
        proj_u()
        proj_g()
        attention(s1, g)
        conv_ln()
        mix_out()
        rmsnorm(16, hT, B_hT)
        ffn("2")
        rmsnorm(24, xT, B_xT)
        store_out(g)
    T.op("sp", lambda e: e.nop(), reads=B_out)
    T.emit()
    return nc


def _perm_qk_cols():
    perm = np.zeros(1024, dtype=np.int64)
    for m in range(8):
        for p in range(128):
            if m == 0:
                c, i = p // 8, p % 8
                o = c * 64 + i
            elif m == 1:
                c, i = p // 8, p % 8
                o = c * 64 + 8 + i
            else:
                mp = m - 2
                gq, db = mp % 2, mp // 2
                c, i = 8 * gq + p // 16, p % 16
                o = c * 64 + 16 + 16 * db + i
            perm[m * 128 + p] = o
    return perm


def _fm(v):
    v = np.asarray(v, dtype=np.float32).reshape(-1, 128)
    return np.ascontiguousarray(v.T)


_NC_CACHE = {}


def kernel(**inputs):
    x = np.asarray(inputs["x"], dtype=np.float32)
    Bn, S, _ = x.shape
    NSLOT = S // TK
    ncores = 2 * Bn
    f32 = lambda a: np.ascontiguousarray(np.asarray(a, dtype=np.float32))

    perm = _perm_qk_cols()
    w_in = f32(inputs["w_in"][0])
    win = np.concatenate([w_in[:, perm], w_in[:, 1024 + perm], w_in[:, 2048:]], axis=1)
    win = np.ascontiguousarray(win)

    cvec = np.zeros((128, 320), np.float32)
    cvec[:, 0:8] = _fm(inputs["ffn1_norm"][0])
    cvec[:, 8:16] = _fm(inputs["mix_norm"][0])
    cvec[:, 16:24] = _fm(inputs["ffn2_norm"][0])
    cvec[:, 24:32] = _fm(inputs["final_norm"])
    cvec[:, 32:48] = _fm(inputs["b_gate"][0])
    cvec[:, 48:56] = _fm(inputs["conv_b"][0])
    cvec[:, 56:64] = _fm(inputs["conv_ln_g"][0])
    cvec[:, 64:72] = _fm(inputs["conv_ln_b"][0])
    cw = f32(inputs["conv_w"][0])
    cvec[:, 72:320] = cw.T.reshape(8, 128, 31).transpose(1, 0, 2).reshape(128, 248)
    sublnB = np.ascontiguousarray(np.broadcast_to(f32(inputs["attn_subln"][0])[None, :], (128, 128)))
    lamv = np.concatenate([f32(inputs["lambda_q1"][0]), f32(inputs["lambda_k1"][0]), f32(inputs["lambda_q2"][0]), f32(inputs["lambda_k2"][0])])
    lamv = np.ascontiguousarray(np.broadcast_to(lamv[None, :], (128, 256)))

    inv_freq = (np.float32(500000.0) ** (-np.arange(0, 16, 2, dtype=np.float32) / np.float32(16))).astype(np.float32)

    shared = {
        "cvec": cvec, "sublnB": sublnB, "lamv": lamv, "win": win,
        "gu1": f32(inputs["ffn1_w_gate_up"][0]), "dn1": f32(inputs["ffn1_w_down"][0]),
        "wa": f32(inputs["w_attn_out"][0]), "wb": f32(inputs["w_conv_out"][0]), "wo": f32(inputs["w_out"][0]),
        "gu2": f32(inputs["ffn2_w_gate_up"][0]), "dn2": f32(inputs["ffn2_w_down"][0]),
    }
    in_maps = []
    for core in range(ncores):
        b, j = core // 2, core % 2
        shift = TK * (1 - j)
        xs = np.zeros((S, D), np.float32)
        xs[shift:] = x[b, :S - shift]
        pos = np.maximum(np.arange(S, dtype=np.float32) - np.float32(shift), np.float32(0.0)).astype(np.float32)
        ang = pos[None, :] * inv_freq[:, None]
        cosT = np.ascontiguousarray(np.tile(np.cos(ang).astype(np.float32), (16, 1)))
        sinT = np.ascontiguousarray(np.tile(np.sin(ang).astype(np.float32), (16, 1)))
        kbias = np.zeros((128, NSLOT * 4), np.float32)
        if j == 0:
            kbias[:, 0:4] = NEG
        m = dict(shared)
        m.update({"xs": xs, "cosT": cosT, "sinT": sinT, "kbias": kbias})
        in_maps.append(m)

    if NSLOT not in _NC_CACHE:
        _NC_CACHE[NSLOT] = build(NSLOT)
    nc = _NC_CACHE[NSLOT]
    res = run_bass_kernel_spmd(nc, in_maps, core_ids=list(range(ncores)))
    out = np.zeros((Bn, S, D), np.float32)
    for core in range(ncores):
        b, j = core // 2, core % 2
        o = res.results[core]["out"]
        for g in range(NSLOT // 2):
            sbk = 2 * g + j
            out[b, sbk * TK:(sbk + 1) * TK] = o[g * TK:(g + 1) * TK]
    return out
```

```python
import math
import numpy as np
import concourse.bass as bass
import concourse.mybir as mybir
from concourse.bass_utils import run_bass_kernel_spmd

F32 = mybir.dt.float32
BF16 = mybir.dt.bfloat16
AF = mybir.ActivationFunctionType
ALU = mybir.AluOpType

D = 1024
TK = 512
DFF = 2816
NFF = 22
EPS = 1e-5
LAM_INIT = 0.8 - 0.6 * math.exp(-0.3 * 0)
NEG = -30000.0


class Buf:
    __slots__ = ("name", "w", "r")

    def __init__(self, name):
        self.name = name
        self.w = None
        self.r = []


class Op:
    __slots__ = ("eng", "fn", "deps", "kind", "chan", "cum", "sig", "needed", "dwait")


class Tracker:
    ENGS = ("pe", "act", "dve", "pool", "sp")

    def __init__(self, nc):
        self.nc = nc
        self.ops = {e: [] for e in self.ENGS}
        self.all = []
        self.chan_cum = {}

    def op(self, eng, fn, reads=(), writes=(), nosame=False):
        o = Op()
        o.eng = eng
        o.fn = fn
        o.kind = "c"
        o.chan = None
        o.cum = 0
        o.sig = 0
        o.needed = False
        deps = set()
        for b in reads:
            if b.w is not None:
                deps.add(b.w)
        for b in writes:
            if b.w is not None:
                deps.add(b.w)
            for r in b.r:
                deps.add(r)
        for b in reads:
            b.r.append(o)
        for b in writes:
            b.w = o
            b.r = []
        deps.discard(o)
        o.dwait = {}
        for d in deps:
            if d.kind == "d":
                o.dwait[d.chan] = self.chan_cum[d.chan]
        if nosame:
            deps = {d for d in deps if not (d.kind == "c" and d.eng == eng)}
        o.deps = deps
        self.ops[eng].append(o)
        self.all.append(o)
        return o

    def dma(self, queue, chan, fn, reads=(), writes=()):
        o = self.op(queue, fn, reads, writes)
        o.kind = "d"
        o.chan = chan
        self.chan_cum[chan] = self.chan_cum.get(chan, 0) + 16
        o.cum = self.chan_cum[chan]
        return o

    def emit(self):
        nc = self.nc
        for o in self.all:
            for d in o.deps:
                d.needed = True
        for e in self.ENGS:
            c = 0
            for o in self.ops[e]:
                if o.kind == "c" and o.needed:
                    c += 1
                    o.sig = c
        self.esem = {e: nc.alloc_semaphore(name="prog_" + e) for e in self.ENGS}
        self.csem = {ch: nc.alloc_semaphore(name="ch_" + str(ch)) for ch in self.chan_cum}

        def run(e):
            def body(eng):
                waited = {}
                for o in self.ops[e]:
                    need = {}
                    for d in o.deps:
                        if d.kind == "c":
                            key = ("e", d.eng)
                            v = d.sig
                        else:
                            key = ("c", d.chan)
                            v = o.dwait[d.chan]
                        if v > need.get(key, 0):
                            need[key] = v
                    for key, v in need.items():
                        if waited.get(key, 0) >= v:
                            continue
                        waited[key] = v
                        sem = self.esem[key[1]] if key[0] == "e" else self.csem[key[1]]
                        eng.wait_ge(sem, v)
                    ins = o.fn(eng)
                    if o.kind == "d":
                        ins.then_inc(self.csem[o.chan], 16)
                    elif o.needed:
                        ins.then_inc(self.esem[e], 1)
            return body

        with nc.Block() as block:
            block.tensor(run("pe"))
            block.scalar(run("act"))
            block.vector(run("dve"))
            block.gpsimd(run("pool"))
            block.sync(run("sp"))


def weight_blocks():
    cat = {}
    for f in ("1", "2"):
        cat["gu" + f] = [(8, 512, [("gu" + f, j * 256, 256, 0), ("gu" + f, DFF + j * 256, 256, 256)]) for j in range(11)]
        cat["dn" + f] = [(22, 128, [("dn" + f, m * 128, 128, 0)]) for m in range(8)]
    cat["wq"] = [(8, 512, [("win", j * 512, 512, 0)]) for j in range(2)]
    cat["wk"] = [(8, 512, [("win", 1024 + j * 512, 512, 0)]) for j in range(2)]
    cat["wv"] = [(8, 512, [("win", 2048 + j * 512, 512, 0)]) for j in range(2)]
    cat["wu"] = [(8, 512, [("win", 3072 + j * 256, 256, 0), ("win", 4096 + j * 256, 256, 256)]) for j in range(4)]
    cat["wg"] = [(8, 512, [("win", 5120 + j * 512, 512, 0)]) for j in range(4)]
    for n in ("wa", "wb", "wo"):
        cat[n] = [(8, 512, [(n, j * 512, 512, 0)]) for j in range(2)]
    return cat


def build(NSLOT):
    assert NSLOT % 2 == 0
    S = NSLOT * TK
    NPAIR = NSLOT // 2
    nc = bass.Bass("TRN2", target_bir_lowering=False)

    def din(name, shape, dt=F32):
        return nc.dram_tensor(name, list(shape), dt, kind="ExternalInput").ap()

    xs = din("xs", [S, D])
    cosT = din("cosT", [128, S])
    sinT = din("sinT", [128, S])
    kbias_d = din("kbias", [128, NSLOT * 4])
    cvec_d = din("cvec", [128, 320])
    subln_d = din("sublnB", [128, 128])
    lamv_d = din("lamv", [128, 256])
    wsrc = {
        "gu1": din("gu1", [D, 2 * DFF]), "dn1": din("dn1", [DFF, D]),
        "win": din("win", [D, 7168]),
        "wa": din("wa", [D, D]), "wb": din("wb", [D, D]), "wo": din("wo", [D, D]),
        "gu2": din("gu2", [D, 2 * DFF]), "dn2": din("dn2", [DFF, D]),
    }
    out_d = nc.dram_tensor("out", [NPAIR * TK, D], F32, kind="ExternalOutput").ap()

    cat = weight_blocks()
    blk_index = {}
    nblk = 0
    for name, blks in cat.items():
        for j in range(len(blks)):
            blk_index[(name, j)] = nblk
            nblk += 1
    wbf = nc.dram_tensor("wbf", [nblk, 128, 4096], BF16).ap()
    kT_s = nc.dram_tensor("kT_s", [16, 64, S], BF16).ap()
    v_s = nc.dram_tensor("v_s", [8, 128, S // 128, 128], BF16).ap()
    qT_s = nc.dram_tensor("qT_s", [2, 16, 64, TK], BF16).ap()

    T = Tracker(nc)

    def sb(name, shape, dt):
        return nc.alloc_sbuf_tensor(name, list(shape), dt)

    stage = sb("stage", [128, 4096], F32)
    xT = sb("xT", [128, 8, TK], F32)
    hT = sb("hT", [128, 8, TK], BF16)
    actb = sb("actb", [128, NFF * TK], BF16)
    NW = 3
    wring = [sb("wring%d" % i, [128, 4096], BF16) for i in range(NW)]
    rstd = sb("rstd", [128, TK], F32)
    accx = sb("accx", [128, TK], F32)
    accx2 = sb("accx2", [128, TK], F32)
    tmpa = [sb("tmpa%d" % i, [128, TK], F32) for i in range(3)]
    ropec = sb("ropec", [128, TK], F32)
    ropes = sb("ropes", [128, TK], F32)
    r12 = sb("r12", [128, 2, TK], F32)
    qk = sb("qk", [128, 8, TK], BF16)
    vtok = sb("vtok", [128, 8, 4, 128], BF16)
    NKV = 6
    kbuf = [sb("kbuf%d" % i, [128, TK], BF16) for i in range(NKV)]
    vbuf = [sb("vbuf%d" % i, [128, 4, 132], BF16) for i in range(NKV)]
    qTb = [sb("qTb%d" % i, [128, TK], BF16) for i in range(2)]
    NP = 3
    pT = [sb("pT%d" % i, [128, 2, TK], BF16) for i in range(NP)]
    sqr = [sb("sqr%d" % i, [128, TK], BF16) for i in range(3)]
    Otok = sb("Otok", [128, 4, D], BF16)
    ohs = sb("ohs", [128, 4, 128], F32)
    on0 = sb("on0", [128, 4, 128], F32)
    osb = sb("osb", [128, 8 * 129], F32)
    junk = sb("junk", [128, 128], F32)
    OT = sb("OT", [128, 8, TK], BF16)
    cb = sb("cb", [128, 8, TK], BF16)
    gates = sb("gates", [128, 16, TK], BF16)
    hhalo = sb("hhalo", [128, 8, 32], BF16)
    cvec = sb("cvec_sb", [128, 320], F32)
    sublnB = sb("sublnB_sb", [128, 128], F32)
    lamv = sb("lamv_sb", [128, 256], F32)
    lamt = sb("lamt", [128, 8], F32)
    kbias = sb("kbias_sb", [128, NSLOT * 4], F32)
    identf = sb("identf", [128, 128], F32)
    identb = sb("identb", [128, 128], BF16)
    onesb = sb("onesb", [128, 128], BF16)
    maskneg = sb("maskneg", [128, 128], BF16)
    maskf = sb("maskf", [128, 128], F32)
    small = sb("small", [128, 64], F32)

    act3 = actb[:].rearrange("p (c t) -> p c t", t=TK)
    cin = actb[:].bitcast(F32)[:, 0:8 * 544].rearrange("p (c t) -> p c t", t=544)
    stage_x = stage[:].rearrange("p (a f) -> p a f", f=D)
    cacc = stage[:].rearrange("p (c t) -> p c t", t=TK)

    PS = nc.alloc_psum_tensor("ps", [128, 8, 512], F32)
    P = [PS[:, i, :] for i in range(8)]

    B_stage = Buf("stage")
    B_cacc = [Buf("cacc%d" % i) for i in range(8)]
    B_stage_all = [B_stage] + B_cacc
    B_accx = Buf("accx")
    B_accx2 = Buf("accx2")
    B_xT = [Buf("xT%d" % i) for i in range(8)]
    B_hT = [Buf("hT%d" % i) for i in range(8)]
    B_act = [Buf("act%d" % i) for i in range(NFF)]
    B_wr = [Buf("wr%d" % i) for i in range(NW)]
    B_rstd = Buf("rstd")
    B_tmpa = [Buf("tmpa%d" % i) for i in range(3)]
    B_rope = Buf("rope")
    B_rope2 = Buf("rope2")
    B_r12 = [Buf("r1"), Buf("r2")]
    B_qk = [Buf("qk%d" % i) for i in range(8)]
    B_vtok = Buf("vtok")
    B_kb = [Buf("kb%d" % i) for i in range(NKV)]
    B_vb = [Buf("vb%d" % i) for i in range(NKV)]
    B_qT = [Buf("qT%d" % i) for i in range(2)]
    B_pT = [Buf("pT%d" % i) for i in range(NP)]
    B_sqr = [Buf("sqr%d" % i) for i in range(3)]
    B_Otok = [Buf("Otok%d" % i) for i in range(8)]
    B_ohs = Buf("ohs")
    B_on0 = Buf("on0")
    B_osb = [Buf("osb%d" % i) for i in range(3)]
    B_junk = Buf("junk")
    B_OT = [Buf("OT%d" % i) for i in range(8)]
    B_cb = [Buf("cb%d" % i) for i in range(8)]
    B_gates = [Buf("g%d" % i) for i in range(16)]
    B_hhalo = Buf("hhalo")
    B_const = Buf("const")
    B_small = Buf("small")
    B_P = [Buf("P%d" % i) for i in range(8)]
    B_wbf = [Buf("wbf%d" % i) for i in range(nblk)]
    B_kTs = [[Buf("kTs%d_%d" % (s, m)) for m in range(8)] for s in range(NSLOT)]
    B_vs = [Buf("vs%d" % s) for s in range(NSLOT)]
    B_qTs = [[Buf("qTs%d_%d" % (p_, m)) for m in range(8)] for p_ in range(2)]
    B_out = [Buf("out%d" % i) for i in range(4)]

    setup_loads = [
        (cvec[:], cvec_d), (sublnB[:], subln_d), (lamv[:], lamv_d), (kbias[:], kbias_d),
    ]
    for dst, src in setup_loads:
        T.dma("sp", "setup", lambda e, d=dst, s=src: e.dma_start(out=d, in_=s), writes=[B_const])
    T.op("pool", lambda e: e.memset(identf[:], 0.0), writes=[B_const])
    T.op("pool", lambda e: e.affine_select(out=identf[:], in_=identf[:], pattern=[[-1, 128]], compare_op=ALU.not_equal,
                                           fill=1.0, base=0, channel_multiplier=1), reads=[B_const], writes=[B_const])
    T.op("pool", lambda e: e.tensor_copy(out=identb[:], in_=identf[:]), reads=[B_const], writes=[B_const])
    T.op("pool", lambda e: e.memset(onesb[:], 1.0), writes=[B_const])
    T.op("pool", lambda e: e.memset(maskf[:], 0.0), writes=[B_const])
    T.op("pool", lambda e: e.affine_select(out=maskf[:], in_=maskf[:], pattern=[[1, 128]], compare_op=ALU.is_ge,
                                           fill=NEG, base=0, channel_multiplier=-1), reads=[B_const], writes=[B_const])
    T.op("pool", lambda e: e.tensor_copy(out=maskneg[:], in_=maskf[:]), reads=[B_const], writes=[B_const])
    for i in range(NKV):
        T.op("pool", lambda e, i=i: e.memset(vbuf[i][:, :, 128:132], 1.0), writes=[B_vb[i]])
    T.op("dve", lambda e: e.tensor_tensor(out=junk[:, 0:64], in0=lamv[:, 0:64], in1=lamv[:, 64:128], op=ALU.mult), reads=[B_const], writes=[B_junk])
    T.op("dve", lambda e: e.reduce_sum(out=lamt[:, 0:1], in_=junk[:, 0:64], axis=mybir.AxisListType.X), reads=[B_junk], writes=[B_small])
    T.op("dve", lambda e: e.tensor_tensor(out=junk[:, 64:128], in0=lamv[:, 128:192], in1=lamv[:, 192:256], op=ALU.mult), reads=[B_const], writes=[B_junk])
    T.op("dve", lambda e: e.reduce_sum(out=lamt[:, 1:2], in_=junk[:, 64:128], axis=mybir.AxisListType.X), reads=[B_junk], writes=[B_small])
    T.op("act", lambda e: e.activation(out=lamt[:, 2:4], in_=lamt[:, 0:2], func=AF.Exp), reads=[B_small], writes=[B_small])
    T.op("dve", lambda e: e.tensor_tensor(out=lamt[:, 4:5], in0=lamt[:, 3:4], in1=lamt[:, 2:3], op=ALU.subtract), reads=[B_small], writes=[B_small])
    T.op("dve", lambda e: e.tensor_scalar(out=lamt[:, 5:6], in0=lamt[:, 4:5], scalar1=-LAM_INIT, scalar2=None, op0=ALU.add), reads=[B_small], writes=[B_small])
    neglam = lamt[:, 5:6]

    pre_stage = [(stage, "STAGE"), (xT, None)]
    cast_engs = ["dve", "pool", "act"]
    B_xTall = Buf("xTall")
    ci = 0
    for name, blks in cat.items():
        for j, (KC, W, parts) in enumerate(blks):
            bi = blk_index[(name, j)]
            st_t, st_b = pre_stage[ci % 2]
            st_bl = B_stage_all if st_b == "STAGE" else [B_xTall]
            st_flat = st_t[:] if st_t is stage else st_t[:].rearrange("p c t -> p (c t)")
            st_v = st_flat[:, 0:KC * W].rearrange("p (k w) -> p k w", w=W)
            for (sn, c0, ncol, d0) in parts:
                src = wsrc[sn][:, c0:c0 + ncol].rearrange("(k p) n -> p k n", p=128)
                T.dma("sp", "pre_ld%d" % (ci % 2), lambda e, d=st_v[:, :, d0:d0 + ncol], s=src: e.dma_start(out=d, in_=s), writes=st_bl)
            wr = wring[ci % NW]
            ce = cast_engs[ci % 3]
            if ce == "act":
                T.op("act", lambda e, o=wr[:, 0:KC * W], i=st_flat[:, 0:KC * W]: e.activation(out=o, in_=i, func=AF.Copy), reads=st_bl, writes=[B_wr[ci % NW]])
            else:
                T.op(ce, lambda e, o=wr[:, 0:KC * W], i=st_flat[:, 0:KC * W]: e.tensor_copy(out=o, in_=i), reads=st_bl, writes=[B_wr[ci % NW]])
            T.dma("pool", "pre_st%d" % (ci % NW), lambda e, o=wbf[bi, :, 0:KC * W], i=wr[:, 0:KC * W]: e.dma_start(out=o, in_=i), reads=[B_wr[ci % NW]], writes=[B_wbf[bi]])
            ci += 1
    T.op("dve", lambda e: e.memset(small[:, 0:1], 0.0), reads=[B_xTall], writes=[B_small] + B_xT)

    wctr = [0]

    def wload(name, j):
        KC, W, _ = cat[name][j]
        bi = blk_index[(name, j)]
        r = wctr[0] % NW
        wctr[0] += 1
        T.dma("sp", "wr%d" % r, lambda e, o=wring[r][:, 0:KC * W], i=wbf[bi, :, 0:KC * W]: e.dma_start(out=o, in_=i), reads=[B_wbf[bi]], writes=[B_wr[r]])
        return wring[r][:, 0:KC * W].rearrange("p (k w) -> p k w", w=W), B_wr[r]

    bctr = [0]

    def bank():
        i = bctr[0] % 4
        bctr[0] += 1
        return P[i], B_P[i]

    ectr = [0]

    def evac_copy(out_ap, in_ap, reads, writes):
        ectr[0] += 1
        if ectr[0] % 2 == 0:
            T.op("act", lambda e: e.activation(out=out_ap, in_=in_ap, func=AF.Copy), reads=reads, writes=writes)
        else:
            T.op("dve", lambda e: e.tensor_copy(out=out_ap, in_=in_ap), reads=reads, writes=writes)

    def mm(out, lhsT, rhs, start, stop, reads, writes, **kw):
        T.op("pe", lambda e: e.matmul(out, lhsT=lhsT, rhs=rhs, start=start, stop=stop, **kw), reads=reads, writes=writes, nosame=True)

    tctr = [0]

    def tmp():
        i = tctr[0] % 3
        tctr[0] += 1
        return tmpa[i], B_tmpa[i]

    sqctr = [0]

    pend_stats = []

    def stats_flush():
        while pend_stats:
            c, i = pend_stats.pop(0)
            mm(P[7][:], onesb[:], sqr[i][:], c == 0, c == 7, [B_sqr[i], B_const], [B_P[7]])

    def stats_chunk(c):
        stats_flush()
        i = sqctr[0] % 3
        sqctr[0] += 1
        T.op("act", lambda e: e.activation(out=sqr[i][:], in_=xT[:, c, :], func=AF.Square), reads=[B_xT[c]], writes=[B_sqr[i]])
        pend_stats.append((c, i))

    def rmsnorm(gcol, dst, B_dst):
        stats_flush()
        t, bt = tmp()
        T.op("act", lambda e: e.activation(out=t[:], in_=P[7][:], func=AF.Sqrt, bias=EPS, scale=1.0 / D), reads=[B_P[7]], writes=[bt])
        T.op("dve", lambda e: e.reciprocal(out=rstd[:], in_=t[:]), reads=[bt], writes=[B_rstd])
        for c in range(8):
            T.op("dve", lambda e, c=c: e.scalar_tensor_tensor(out=dst[:, c, :], in0=xT[:, c, :], scalar=cvec[:, gcol + c:gcol + c + 1],
                                                             in1=rstd[:], op0=ALU.mult, op1=ALU.mult),
                 reads=[B_xT[c], B_rstd, B_const], writes=[B_dst[c]])

    def ffn(f):
        for j in range(11):
            w, bw = wload("gu" + f, j)
            for i in range(2):
                m = 2 * j + i
                pa, bpa = bank()
                pb, bpb = bank()
                for kc in range(8):
                    mm(pa[:], w[:, kc, i * 128:(i + 1) * 128], hT[:, kc, :], kc == 0, kc == 7, [bw, B_hT[kc]], [bpa])
                for kc in range(8):
                    mm(pb[:], w[:, kc, 256 + i * 128:256 + (i + 1) * 128], hT[:, kc, :], kc == 0, kc == 7, [bw, B_hT[kc]], [bpb])
                t, bt = tmp()
                T.op("act", lambda e, t=t, pa=pa: e.activation(out=t[:], in_=pa[:], func=AF.Silu), reads=[bpa], writes=[bt])
                T.op("dve", lambda e, t=t, pb=pb, m=m: e.tensor_tensor(out=act3[:, m, :], in0=pb[:], in1=t[:], op=ALU.mult), reads=[bpb, bt], writes=[B_act[m]])
        for m in range(8):
            w, bw = wload("dn" + f, m)
            p, bp = bank()
            for kc in range(NFF):
                mm(p[:], w[:, kc, :], act3[:, kc, :], kc == 0, kc == NFF - 1, [bw, B_act[kc]], [bp])
            T.op("dve", lambda e, p=p, m=m: e.scalar_tensor_tensor(out=xT[:, m, :], in0=p[:], scalar=0.5, in1=xT[:, m, :], op0=ALU.mult, op1=ALU.add),
                 reads=[bp, B_xT[m]], writes=[B_xT[m]])
            stats_chunk(m)

    def issue_x(s):
        for a in range(4):
            src = xs[s * TK + a * 128:s * TK + (a + 1) * 128, :]
            T.dma("sp", "xld%d" % a, lambda e, a=a, src=src: e.dma_start(out=stage_x[:, a, :], in_=src), writes=[B_cacc[2 * a], B_cacc[2 * a + 1]])

    def load_x():
        for fc in range(8):
            p, bp = bank()
            for a in range(4):
                mm(p[:, a * 128:(a + 1) * 128], stage_x[:, a, fc * 128:(fc + 1) * 128], identf[:], True, True, [B_cacc[2 * a], B_cacc[2 * a + 1], B_const], [bp])
            evac_copy(xT[:, fc, :], p[:], [bp], [B_xT[fc]])
            stats_chunk(fc)

    def load_rope(s):
        T.dma("sp", "rope", lambda e: e.dma_start(out=ropec[:], in_=cosT[:, s * TK:(s + 1) * TK]), writes=[B_rope])
        T.dma("sp", "rope2", lambda e: e.dma_start(out=ropes[:], in_=sinT[:, s * TK:(s + 1) * TK]), writes=[B_rope2])

    def proj_qk(wname, dst_fn, B_dst):
        for j in range(2):
            w, bw = wload(wname, j)
            for i in range(4):
                m = 4 * j + i
                p, bp = bank()
                for kc in range(8):
                    mm(p[:], w[:, kc, i * 128:(i + 1) * 128], hT[:, kc, :], kc == 0, kc == 7, [bw, B_hT[kc]], [bp])
                if m < 2:
                    evac_copy(r12[:, m, :], p[:], [bp], [B_r12[m]])
                else:
                    evac_copy(qk[:, m, :], p[:], [bp], [B_qk[m]])
                if m == 1:
                    t1, b1 = tmp()
                    t2, b2 = tmp()
                    T.op("dve", lambda e, t1=t1: e.tensor_tensor(out=t1[:], in0=r12[:, 0, :], in1=ropec[:], op=ALU.mult), reads=[B_r12[0], B_rope], writes=[b1])
                    T.op("dve", lambda e, t2=t2: e.tensor_tensor(out=t2[:], in0=r12[:, 1, :], in1=ropes[:], op=ALU.mult), reads=[B_r12[1], B_rope2], writes=[b2])
                    T.op("dve", lambda e, t1=t1, t2=t2: e.tensor_tensor(out=qk[:, 0, :], in0=t1[:], in1=t2[:], op=ALU.subtract), reads=[b1, b2], writes=[B_qk[0]])
                    t3, b3 = tmp()
                    t4, b4 = tmp()
                    T.op("dve", lambda e, t3=t3: e.tensor_tensor(out=t3[:], in0=r12[:, 1, :], in1=ropec[:], op=ALU.mult), reads=[B_r12[1], B_rope], writes=[b3])
                    T.op("dve", lambda e, t4=t4: e.tensor_tensor(out=t4[:], in0=r12[:, 0, :], in1=ropes[:], op=ALU.mult), reads=[B_r12[0], B_rope2], writes=[b4])
                    T.op("dve", lambda e, t3=t3, t4=t4: e.tensor_tensor(out=qk[:, 1, :], in0=t3[:], in1=t4[:], op=ALU.add), reads=[b3, b4], writes=[B_qk[1]])
        for m in range(8):
            dst, chan = dst_fn(m)
            T.dma("pool", chan, lambda e, dst=dst, m=m: e.dma_start(out=dst, in_=qk[:, m, :]), reads=[B_qk[m]], writes=[B_dst[m]])

    def qk_dst(base, tok0, ntok):
        def f(m):
            if m == 0:
                return base[:, 0:8, tok0:tok0 + ntok]
            if m == 1:
                return base[:, 8:16, tok0:tok0 + ntok]
            mp = m - 2
            gq = mp % 2
            db = mp // 2
            return base[8 * gq:8 * gq + 8, 16 + 16 * db:32 + 16 * db, tok0:tok0 + ntok]
        return f

    def proj_v(s):
        for j in range(2):
            w, bw = wload("wv", j)
            for a in range(4):
                p, bp = bank()
                for kc in range(8):
                    mm(p[:], hT[:, kc, a * 128:(a + 1) * 128], w[:, kc, :], kc == 0, kc == 7, [bw, B_hT[kc]], [bp])
                evac_copy(vtok[:, 4 * j:4 * j + 4, a, :], p[:].rearrange("p (h e) -> p h e", e=128), [bp], [B_vtok])
        dst = v_s[:, :, 4 * s:4 * s + 4, :].rearrange("h p k e -> p h k e")
        T.dma("pool", "vw%d" % (s % 2), lambda e: e.dma_start(out=dst, in_=vtok[:]), reads=[B_vtok], writes=[B_vs[s]])

    def proj_u():
        ph, bph = P[7], B_P[7]
        for j in range(4):
            w, bw = wload("wu", j)
            for i in range(2):
                m = 2 * j + i
                for kc in range(8):
                    mm(ph[:, m * 32:(m + 1) * 32], w[:, kc, i * 128:(i + 1) * 128], hhalo[:, kc, :], kc == 0, kc == 7, [bw, B_hhalo], [bph])
                for kc in range(8):
                    mm(ph[:, 256 + m * 32:256 + (m + 1) * 32], w[:, kc, 256 + i * 128:256 + (i + 1) * 128], hhalo[:, kc, :], kc == 0, kc == 7, [bw, B_hhalo], [bph])
                pa, bpa = bank()
                pb, bpb = bank()
                for kc in range(8):
                    mm(pa[:], w[:, kc, i * 128:(i + 1) * 128], hT[:, kc, :], kc == 0, kc == 7, [bw, B_hT[kc]], [bpa])
                for kc in range(8):
                    mm(pb[:], w[:, kc, 256 + i * 128:256 + (i + 1) * 128], hT[:, kc, :], kc == 0, kc == 7, [bw, B_hT[kc]], [bpb])
                t, bt = tmp()
                T.op("act", lambda e, t=t, pb=pb: e.activation(out=t[:], in_=pb[:], func=AF.Sigmoid), reads=[bpb], writes=[bt])
                T.op("dve", lambda e, t=t, pa=pa, m=m: e.tensor_tensor(out=cin[:, m, 32:544], in0=pa[:], in1=t[:], op=ALU.mult), reads=[bpa, bt], writes=[B_act[m]])
        t, bt = tmp()
        T.op("act", lambda e, t=t: e.activation(out=t[:, 0:256], in_=ph[:, 256:512], func=AF.Sigmoid), reads=[bph], writes=[bt])
        T.op("dve", lambda e, t=t: e.tensor_tensor(out=cin[:, :, 0:32], in0=ph[:, 0:256].rearrange("p (c t) -> p c t", t=32),
                                                  in1=t[:, 0:256].rearrange("p (c t) -> p c t", t=32), op=ALU.mult), reads=[bph, bt], writes=B_act[0:8])

    def proj_g():
        for j in range(4):
            w, bw = wload("wg", j)
            for i in range(4):
                cc = 4 * j + i
                p, bp = bank()
                for kc in range(8):
                    mm(p[:], w[:, kc, i * 128:(i + 1) * 128], hT[:, kc, :], kc == 0, kc == 7, [bw, B_hT[kc]], [bp])
                T.op("act", lambda e, p=p, cc=cc: e.activation(out=gates[:, cc, :], in_=p[:], func=AF.Sigmoid, bias=cvec[:, 32 + cc:33 + cc]),
                     reads=[bp, B_const], writes=[B_gates[cc]])

    conv_first = [True]

    def conv_chunk(m):
        for j in range(31):
            wcol = cvec[:, 72 + m * 31 + j:73 + m * 31 + j]
            if j == 0:
                T.op("dve", lambda e, wcol=wcol: e.tensor_scalar(out=cacc[:, m, :], in0=cin[:, m, 2:2 + TK], scalar1=wcol, scalar2=cvec[:, 48 + m:49 + m],
                                                                 op0=ALU.mult, op1=ALU.add),
                     reads=[B_act[m], B_const], writes=[B_cacc[m]])
            else:
                T.op("dve", lambda e, wcol=wcol, j=j: e.scalar_tensor_tensor(out=cacc[:, m, :], in0=cin[:, m, j + 2:j + 2 + TK], scalar=wcol, in1=cacc[:, m, :],
                                                                              op0=ALU.mult, op1=ALU.add),
                     reads=[B_act[m], B_const, B_cacc[m]], writes=[B_cacc[m]])
        if conv_first[0]:
            conv_first[0] = False
            T.op("dve", lambda e: e.tensor_copy(out=accx[:], in_=cacc[:, m, :]), reads=[B_cacc[m]], writes=[B_accx])
            T.op("dve", lambda e: e.tensor_tensor(out=accx2[:], in0=cacc[:, m, :], in1=cacc[:, m, :], op=ALU.mult), reads=[B_cacc[m]], writes=[B_accx2])
        else:
            t, bt = tmp()
            T.op("dve", lambda e: e.tensor_tensor(out=accx[:], in0=accx[:], in1=cacc[:, m, :], op=ALU.add), reads=[B_cacc[m], B_accx], writes=[B_accx])
            T.op("dve", lambda e: e.tensor_tensor(out=t[:], in0=cacc[:, m, :], in1=cacc[:, m, :], op=ALU.mult), reads=[B_cacc[m]], writes=[bt])
            T.op("dve", lambda e: e.tensor_tensor(out=accx2[:], in0=accx2[:], in1=t[:], op=ALU.add), reads=[bt, B_accx2], writes=[B_accx2])

    def conv_ln():
        conv_first[0] = True
        T.op("dve", lambda e: e.tensor_copy(out=sqr[0][:], in_=accx[:]), reads=[B_accx], writes=[B_sqr[0]])
        T.op("dve", lambda e: e.tensor_copy(out=sqr[1][:], in_=accx2[:]), reads=[B_accx2], writes=[B_sqr[1]])
        mm(P[7][:], onesb[:], sqr[0][:], True, True, [B_sqr[0], B_const], [B_P[7]])
        ps2, bps2 = bank()
        mm(ps2[:], onesb[:], sqr[1][:], True, True, [B_sqr[1], B_const], [bps2])
        mean, bmean = tmp()
        T.op("dve", lambda e: e.tensor_scalar(out=mean[:], in0=P[7][:], scalar1=1.0 / D, scalar2=None, op0=ALU.mult), reads=[B_P[7]], writes=[bmean])
        msq, bmsq = tmp()
        T.op("dve", lambda e: e.tensor_tensor(out=msq[:], in0=mean[:], in1=mean[:], op=ALU.mult), reads=[bmean], writes=[bmsq])
        T.op("dve", lambda e: e.scalar_tensor_tensor(out=msq[:], in0=ps2[:], scalar=1.0 / D, in1=msq[:], op0=ALU.mult, op1=ALU.subtract), reads=[bps2, bmsq], writes=[bmsq])
        T.op("act", lambda e: e.activation(out=msq[:], in_=msq[:], func=AF.Sqrt, bias=EPS, scale=1.0), reads=[bmsq], writes=[bmsq])
        T.op("dve", lambda e: e.reciprocal(out=rstd[:], in_=msq[:]), reads=[bmsq], writes=[B_rstd])
        for m in range(8):
            T.op("dve", lambda e, m=m: e.tensor_tensor(out=cacc[:, m, :], in0=cacc[:, m, :], in1=mean[:], op=ALU.subtract), reads=[B_cacc[m], bmean], writes=[B_cacc[m]])
            T.op("dve", lambda e, m=m: e.tensor_tensor(out=cacc[:, m, :], in0=cacc[:, m, :], in1=rstd[:], op=ALU.mult), reads=[B_cacc[m], B_rstd], writes=[B_cacc[m]])
            T.op("act", lambda e, m=m: e.activation(out=cb[:, m, :], in_=cacc[:, m, :], func=AF.Silu, bias=cvec[:, 64 + m:65 + m], scale=cvec[:, 56 + m:57 + m]),
                 reads=[B_cacc[m], B_const], writes=[B_cb[m]])

    kvctr = [0]
    pctr = [0]
    oslot = {}
    for idx in range(8):
        oslot[(idx // 4, idx % 4)] = (4 + idx // 3, (idx % 3) * 129)

    sgc = [0]

    def attention(s, g):
        par = g % 2
        pend_tr = []
        pend_a = []
        conv_chunk(0)
        for h in range(8):
            qb_t, bq = qTb[h % 2], B_qT[h % 2]
            qsrc = qT_s[par, 2 * h:2 * h + 2, :, :].rearrange("c d t -> (c d) t")
            T.dma("sp", "qld%d" % (h % 2), lambda e, qb_t=qb_t, qsrc=qsrc: e.dma_start(out=qb_t[:], in_=qsrc), reads=B_qTs[par], writes=[bq])
            steps = [(sl, t) for sl in range(s + 1) for t in range(4)]
            rmap = {}

            def kv_load(sl):
                if sl in rmap:
                    return rmap[sl]
                r = kvctr[0] % NKV
                kvctr[0] += 1
                ksrc = kT_s[2 * h:2 * h + 2, :, sl * TK:(sl + 1) * TK].rearrange("c d t -> (c d) t")
                vsrc = v_s[h, :, 4 * sl:4 * sl + 4, :]
                T.dma("sp", "kv%d" % r, lambda e: e.dma_start(out=kbuf[r][:], in_=ksrc), reads=B_kTs[sl], writes=[B_kb[r]])
                T.dma("sp", "vv%d" % r, lambda e: e.dma_start(out=vbuf[r][:, :, 0:128], in_=vsrc), reads=[B_vs[sl]], writes=[B_vb[r]])
                rmap[sl] = r
                return r

            started = set()

            def qk_step(i):
                sl, t = steps[i]
                r = kv_load(sl)
                diag = sl == s
                kt = 4 * sl + t
                q0 = 128 * t if diag else 0
                gi = sgc[0] % 2
                sgc[0] += 1
                for c in range(2):
                    bk = 2 * gi + c
                    mm(P[bk][:, q0:TK], kbuf[r][64 * c:64 * c + 64, t * 128:(t + 1) * 128], qb_t[64 * c:64 * c + 64, q0:TK], True, not diag,
                       [B_kb[r], bq], [B_P[bk]])
                    if diag:
                        mm(P[bk][:, q0:q0 + 128], identb[:], maskneg[:], False, True, [B_const], [B_P[bk]])
                pi = pctr[0] % NP
                pctr[0] += 1
                if kt < 4:
                    T.op("act", lambda e: e.activation(out=pT[pi][:, :, q0:TK], in_=PS[:, 2 * gi:2 * gi + 2, q0:TK], func=AF.Exp,
                                                       bias=kbias[:, kt:kt + 1], scale=0.125),
                         reads=[B_P[2 * gi], B_P[2 * gi + 1], B_const], writes=[B_pT[pi]])
                else:
                    T.op("act", lambda e: e.activation(out=pT[pi][:, :, q0:TK], in_=PS[:, 2 * gi:2 * gi + 2, q0:TK], func=AF.Exp, scale=0.125),
                         reads=[B_P[2 * gi], B_P[2 * gi + 1]], writes=[B_pT[pi]])
                return pi

            def pv_step(i, pi):
                sl, t = steps[i]
                r = rmap[sl]
                diag = sl == s
                for c in range(2):
                    for qb in range(t if diag else 0, 4):
                        bk, off = oslot[(c, qb)]
                        first = bk not in started
                        started.add(bk)
                        mm(P[bk][:, off:off + 129], pT[pi][:, c, qb * 128:(qb + 1) * 128], vbuf[r][:, t, 0:129], first, bool(diag and t == qb),
                           [B_pT[pi], B_vb[r]], [B_P[bk]], skip_group_check=True)

            nst = len(steps)
            pis = {0: qk_step(0), 1: qk_step(1)}
            for i in range(nst):
                if i + 2 < nst:
                    pis[i + 2] = qk_step(i + 2)
                pv_step(i, pis[i])
                if i == min(4, nst - 1):
                    while pend_a:
                        pend_a.pop(0)()
                    if h >= 1:
                        conv_chunk(h)
                if i == min(12, nst - 1):
                    while pend_tr:
                        pend_tr.pop(0)()
            for bk in (4, 5, 6):
                n = 3 if bk < 6 else 2
                eng = "dve"
                if eng == "act":
                    T.op("act", lambda e, bk=bk, n=n: e.activation(out=osb[:, (bk - 4) * 387:(bk - 4) * 387 + n * 129], in_=P[bk][:, 0:n * 129], func=AF.Copy),
                         reads=[B_P[bk]], writes=[B_osb[bk - 4]])
                else:
                    T.op("dve", lambda e, bk=bk, n=n: e.tensor_copy(out=osb[:, (bk - 4) * 387:(bk - 4) * 387 + n * 129], in_=P[bk][:, 0:n * 129]),
                         reads=[B_P[bk]], writes=[B_osb[bk - 4]])
            lview = osb[:, 0:8 * 129].rearrange("p (a b) -> p a b", b=129)[:, :, 128]
            T.op("dve", lambda e: e.reciprocal(out=small[:, 0:8], in_=lview), reads=B_osb, writes=[B_small])
            T.op("dve", lambda e: e.tensor_scalar(out=small[:, 8:12], in0=small[:, 4:8], scalar1=neglam, scalar2=None, op0=ALU.mult), reads=[B_small], writes=[B_small])
            for qb in range(4):
                o0 = qb * 129
                o1 = (4 + qb) * 129
                T.op("dve", lambda e, qb=qb, o0=o0: e.tensor_scalar(out=on0[:, qb, :], in0=osb[:, o0:o0 + 128], scalar1=small[:, qb:qb + 1], scalar2=None, op0=ALU.mult),
                     reads=B_osb + [B_small], writes=[B_on0])
            for qb in range(4):
                o1 = (4 + qb) * 129
                T.op("dve", lambda e, qb=qb, o1=o1: e.scalar_tensor_tensor(out=ohs[:, qb, :], in0=osb[:, o1:o1 + 128], scalar=small[:, 8 + qb:9 + qb], in1=on0[:, qb, :],
                                                                           op0=ALU.mult, op1=ALU.add),
                     reads=B_osb + [B_small, B_on0], writes=[B_ohs])
            for qb in range(4):
                T.op("dve", lambda e, qb=qb: e.scalar_tensor_tensor(out=junk[:], in0=ohs[:, qb, :], scalar=1.0, in1=ohs[:, qb, :], op0=ALU.mult, op1=ALU.mult,
                                                                    accum_out=small[:, 16 + qb:17 + qb]),
                     reads=[B_ohs], writes=[B_junk, B_small])
            def do_p2(h=h):
                k1 = (1.0 - LAM_INIT) ** 2
                T.op("act", lambda e: e.activation(out=small[:, 20:24], in_=small[:, 16:20], func=AF.Sqrt, bias=EPS / k1, scale=1.0 / (128.0 * k1)), reads=[B_small], writes=[B_small])
                T.op("dve", lambda e: e.reciprocal(out=small[:, 24:28], in_=small[:, 20:24]), reads=[B_small], writes=[B_small])
                for qb in range(4):
                    T.op("dve", lambda e, qb=qb: e.scalar_tensor_tensor(out=Otok[:, qb, h * 128:(h + 1) * 128], in0=ohs[:, qb, :], scalar=small[:, 24 + qb:25 + qb], in1=sublnB[:],
                                                                        op0=ALU.mult, op1=ALU.mult),
                         reads=[B_ohs, B_small, B_const], writes=[B_Otok[h]])
            pend_a.append(do_p2)

            def do_tr(h=h):
                p, bp = P[7], B_P[7]
                for qb in range(4):
                    mm(p[:, qb * 128:(qb + 1) * 128], Otok[:, qb, h * 128:(h + 1) * 128], identb[:], True, True, [B_Otok[h], B_const], [bp])
                evac_copy(OT[:, h, :], p[:], [bp], [B_OT[h]])
            pend_tr.append(do_tr)
        while pend_a:
            pend_a.pop(0)()
        while pend_tr:
            pend_tr.pop(0)()

    def mix_out():
        for j in range(2):
            wA, bwA = wload("wa", j)
            for i in range(4):
                m = 4 * j + i
                pa, bpa = bank()
                for kc in range(8):
                    mm(pa[:], wA[:, kc, i * 128:(i + 1) * 128], OT[:, kc, :], kc == 0, kc == 7, [bwA, B_OT[kc]], [bpa])
                T.op("dve", lambda e, pa=pa, m=m: e.tensor_tensor(out=hT[:, m, :], in0=pa[:], in1=gates[:, m, :], op=ALU.mult), reads=[bpa, B_gates[m]], writes=[B_hT[m]])
        for j in range(2):
            wB, bwB = wload("wb", j)
            for i in range(4):
                m = 4 * j + i
                pb, bpb = bank()
                for kc in range(8):
                    mm(pb[:], wB[:, kc, i * 128:(i + 1) * 128], cb[:, kc, :], kc == 0, kc == 7, [bwB, B_cb[kc]], [bpb])
                t2, b2 = tmp()
                T.op("dve", lambda e, t2=t2, pb=pb, m=m: e.tensor_tensor(out=t2[:], in0=pb[:], in1=gates[:, 8 + m, :], op=ALU.mult), reads=[bpb, B_gates[8 + m]], writes=[b2])
                T.op("dve", lambda e, t2=t2, m=m: e.tensor_tensor(out=qk[:, m, :], in0=t2[:], in1=hT[:, m, :], op=ALU.add), reads=[b2, B_hT[m]], writes=[B_qk[m]])
        for j in range(2):
            w, bw = wload("wo", j)
            for i in range(4):
                m = 4 * j + i
                p, bp = bank()
                for kc in range(8):
                    mm(p[:], w[:, kc, i * 128:(i + 1) * 128], qk[:, kc, :], kc == 0, kc == 7, [bw, B_qk[kc]], [bp])
                T.op("dve", lambda e, p=p, m=m: e.tensor_tensor(out=xT[:, m, :], in0=p[:], in1=xT[:, m, :], op=ALU.add), reads=[bp, B_xT[m]], writes=[B_xT[m]])
                stats_chunk(m)

    def store_out(g):
        for a in range(4):
            for half in range(2):
                p, bp = bank()
                for i in range(4):
                    fc = half * 4 + i
                    mm(p[:, i * 128:(i + 1) * 128], xT[:, fc, a * 128:(a + 1) * 128], identf[:], True, True, [B_xT[fc], B_const], [bp])
                evac_copy(stage_x[:, a, half * 512:(half + 1) * 512], p[:], [bp], [B_cacc[2 * a + half]])
            dst = out_d[g * TK + a * 128:g * TK + (a + 1) * 128, :]
            T.dma("pool", "outst%d" % a, lambda e, a=a, dst=dst: e.dma_start(out=dst, in_=stage_x[:, a, :]), reads=[B_cacc[2 * a], B_cacc[2 * a + 1]], writes=[B_out[a]])

    for g in range(NPAIR):
        s0, s1 = 2 * g, 2 * g + 1
        issue_x(s0)
        load_x()
        issue_x(s1)
        rmsnorm(0, hT, B_hT)
        ffn("1")
        rmsnorm(8, hT, B_hT)
        load_rope(s0)
        proj_qk("wk", lambda m, s0=s0: (qk_dst(kT_s, s0 * TK, TK)(m), "kw%d" % (s0 % 2)), B_kTs[s0])
        proj_v(s0)
        T.op("pool", lambda e: e.tensor_copy(out=hhalo[:], in_=hT[:, :, TK - 32:TK]), reads=B_hT, writes=[B_hhalo])
        load_x()
        rmsnorm(0, hT, B_hT)
        ffn("1")
        rmsnorm(8, hT, B_hT)
        load_rope(s1)
        proj_qk("wk", lambda m, s1=s1: (qk_dst(kT_s, s1 * TK, TK)(m), "kw%d" % (s1 % 2)), B_kTs[s1])
        proj_v(s1)
        proj_qk("wq", lambda m, g=g: (qk_dst(qT_s[g % 2], 0, TK)(m), "qw"), B_qTs[g % 2])
        proj_u()
        proj_g()
        attention(s1, g)
        conv_ln()
        mix_out()
        rmsnorm(16, hT, B_hT)
        ffn("2")
        rmsnorm(24, xT, B_xT)
        store_out(g)
    T.op("sp", lambda e: e.nop(), reads=B_out)
    T.emit()
    return nc


def _perm_qk_cols():
    perm = np.zeros(1024, dtype=np.int64)
    for m in range(8):
        for p in range(128):
            if m == 0:
                c, i = p // 8, p % 8
                o = c * 64 + i
            elif m == 1:
                c, i = p // 8, p % 8
                o = c * 64 + 8 + i
            else:
                mp = m - 2
                gq, db = mp % 2, mp // 2
                c, i = 8 * gq + p // 16, p % 16
                o = c * 64 + 16 + 16 * db + i
            perm[m * 128 + p] = o
    return perm


def _fm(v):
    v = np.asarray(v, dtype=np.float32).reshape(-1, 128)
    return np.ascontiguousarray(v.T)


_NC_CACHE = {}


def kernel(**inputs):
    x = np.asarray(inputs["x"], dtype=np.float32)
    Bn, S, _ = x.shape
    NSLOT = S // TK
    ncores = 2 * Bn
    f32 = lambda a: np.ascontiguousarray(np.asarray(a, dtype=np.float32))

    perm = _perm_qk_cols()
    w_in = f32(inputs["w_in"][0])
    win = np.concatenate([w_in[:, perm], w_in[:, 1024 + perm], w_in[:, 2048:]], axis=1)
    win = np.ascontiguousarray(win)

    cvec = np.zeros((128, 320), np.float32)
    cvec[:, 0:8] = _fm(inputs["ffn1_norm"][0])
    cvec[:, 8:16] = _fm(inputs["mix_norm"][0])
    cvec[:, 16:24] = _fm(inputs["ffn2_norm"][0])
    cvec[:, 24:32] = _fm(inputs["final_norm"])
    cvec[:, 32:48] = _fm(inputs["b_gate"][0])
    cvec[:, 48:56] = _fm(inputs["conv_b"][0])
    cvec[:, 56:64] = _fm(inputs["conv_ln_g"][0])
    cvec[:, 64:72] = _fm(inputs["conv_ln_b"][0])
    cw = f32(inputs["conv_w"][0])
    cvec[:, 72:320] = cw.T.reshape(8, 128, 31).transpose(1, 0, 2).reshape(128, 248)
    sublnB = np.ascontiguousarray(np.broadcast_to(f32(inputs["attn_subln"][0])[None, :], (128, 128)))
    lamv = np.concatenate([f32(inputs["lambda_q1"][0]), f32(inputs["lambda_k1"][0]), f32(inputs["lambda_q2"][0]), f32(inputs["lambda_k2"][0])])
    lamv = np.ascontiguousarray(np.broadcast_to(lamv[None, :], (128, 256)))

    inv_freq = (np.float32(500000.0) ** (-np.arange(0, 16, 2, dtype=np.float32) / np.float32(16))).astype(np.float32)

    shared = {
        "cvec": cvec, "sublnB": sublnB, "lamv": lamv, "win": win,
        "gu1": f32(inputs["ffn1_w_gate_up"][0]), "dn1": f32(inputs["ffn1_w_down"][0]),
        "wa": f32(inputs["w_attn_out"][0]), "wb": f32(inputs["w_conv_out"][0]), "wo": f32(inputs["w_out"][0]),
        "gu2": f32(inputs["ffn2_w_gate_up"][0]), "dn2": f32(inputs["ffn2_w_down"][0]),
    }
    in_maps = []
    for core in range(ncores):
        b, j = core // 2, core % 2
        shift = TK * (1 - j)
        xs = np.zeros((S, D), np.float32)
        xs[shift:] = x[b, :S - shift]
        pos = np.maximum(np.arange(S, dtype=np.float32) - np.float32(shift), np.float32(0.0)).astype(np.float32)
        ang = pos[None, :] * inv_freq[:, None]
        cosT = np.ascontiguousarray(np.tile(np.cos(ang).astype(np.float32), (16, 1)))
        sinT = np.ascontiguousarray(np.tile(np.sin(ang).astype(np.float32), (16, 1)))
        kbias = np.zeros((128, NSLOT * 4), np.float32)
        if j == 0:
            kbias[:, 0:4] = NEG
        m = dict(shared)
        m.update({"xs": xs, "cosT": cosT, "sinT": sinT, "kbias": kbias})
        in_maps.append(m)

    if NSLOT not in _NC_CACHE:
        _NC_CACHE[NSLOT] = build(NSLOT)
    nc = _NC_CACHE[NSLOT]
    res = run_bass_kernel_spmd(nc, in_maps, core_ids=list(range(ncores)))
    out = np.zeros((Bn, S, D), np.float32)
    for core in range(ncores):
        b, j = core // 2, core % 2
        o = res.results[core]["out"]
        for g in range(NSLOT // 2):
            sbk = 2 * g + j
            out[b, sbk * TK:(sbk + 1) * TK] = o[g * TK:(g + 1) * TK]
    return out
```

```python
import math
import numpy as np
import concourse.bass as bass
import concourse.mybir as mybir
from concourse.bass_utils import run_bass_kernel_spmd

F32 = mybir.dt.float32
BF16 = mybir.dt.bfloat16
AF = mybir.ActivationFunctionType
ALU = mybir.AluOpType

D = 1024
TK = 512
DFF = 2816
NFF = 22
EPS = 1e-5
LAM_INIT = 0.8 - 0.6 * math.exp(-0.3 * 0)
NEG = -30000.0


class Buf:
    __slots__ = ("name", "w", "r")

    def __init__(self, name):
        self.name = name
        self.w = None
        self.r = []


class Op:
    __slots__ = ("eng", "fn", "deps", "kind", "chan", "cum", "sig", "needed", "dwait")


class Tracker:
    ENGS = ("pe", "act", "dve", "pool", "sp")

    def __init__(self, nc):
        self.nc = nc
        self.ops = {e: [] for e in self.ENGS}
        self.all = []
        self.chan_cum = {}

    def op(self, eng, fn, reads=(), writes=(), nosame=False):
        o = Op()
        o.eng = eng
        o.fn = fn
        o.kind = "c"
        o.chan = None
        o.cum = 0
        o.sig = 0
        o.needed = False
        deps = set()
        for b in reads:
            if b.w is not None:
                deps.add(b.w)
        for b in writes:
            if b.w is not None:
                deps.add(b.w)
            for r in b.r:
                deps.add(r)
        for b in reads:
            b.r.append(o)
        for b in writes:
            b.w = o
            b.r = []
        deps.discard(o)
        o.dwait = {}
        for d in deps:
            if d.kind == "d":
                o.dwait[d.chan] = self.chan_cum[d.chan]
        if nosame:
            deps = {d for d in deps if not (d.kind == "c" and d.eng == eng)}
        o.deps = deps
        self.ops[eng].append(o)
        self.all.append(o)
        return o

    def dma(self, queue, chan, fn, reads=(), writes=(), soft=False):
        o = self.op(queue, fn, reads, writes)
        if soft:
            for d in o.deps:
                if d.kind == "d":
                    o.dwait[d.chan] = d.cum
        o.kind = "d"
        o.chan = chan
        self.chan_cum[chan] = self.chan_cum.get(chan, 0) + 16
        o.cum = self.chan_cum[chan]
        return o

    def emit(self):
        nc = self.nc
        for o in self.all:
            for d in o.deps:
                d.needed = True
        for e in self.ENGS:
            c = 0
            for o in self.ops[e]:
                if o.kind == "c" and o.needed:
                    c += 1
                    o.sig = c
        self.esem = {e: nc.alloc_semaphore(name="prog_" + e) for e in self.ENGS}
        self.csem = {ch: nc.alloc_semaphore(name="ch_" + str(ch)) for ch in self.chan_cum}

        pe_index = {}
        for i, o in enumerate(self.ops["pe"]):
            pe_index[o] = i
        pe_front = {}
        last_in_q = {e: None for e in self.ENGS}
        for o in self.all:
            f = pe_index.get(o, -1)
            for d in o.deps:
                f = max(f, pe_front[d])
            p = last_in_q[o.eng]
            if p is not None:
                f = max(f, pe_front[p])
            pe_front[o] = f
            last_in_q[o.eng] = o
        sig_owner = {e: {} for e in self.ENGS}
        for e in self.ENGS:
            for o in self.ops[e]:
                if o.kind == "c" and o.needed:
                    sig_owner[e][o.sig] = o
        TH, H = 40, 20

        def run(e):
            def body(eng):
                waited = {}
                sched = {}
                plan = []
                for pi, o in enumerate(self.ops[e]):
                    need = {}
                    for d in o.deps:
                        if d.kind == "c":
                            key = ("e", d.eng)
                            v = d.sig
                        else:
                            key = ("c", d.chan)
                            v = o.dwait[d.chan]
                        if v > need.get(key, 0):
                            need[key] = v
                    here = []
                    for key, v in need.items():
                        pos = pi
                        if e == "pe" and key[0] == "e" and key[1] != "pe":
                            q = sig_owner[key[1]].get(v)
                            if q is not None and pi - pe_front[q] >= TH:
                                pos = max(pe_front[q] + 1, pi - H)
                        if pos < pi:
                            sched.setdefault(pos, []).append((key, v))
                        else:
                            here.append((key, v))
                    plan.append(here)
                for pi, o in enumerate(self.ops[e]):
                    for key, v in sched.get(pi, []) + plan[pi]:
                        if waited.get(key, 0) >= v:
                            continue
                        waited[key] = v
                        sem = self.esem[key[1]] if key[0] == "e" else self.csem[key[1]]
                        eng.wait_ge(sem, v)
                    ins = o.fn(eng)
                    if o.kind == "d":
                        ins.then_inc(self.csem[o.chan], 16)
                    elif o.needed:
                        ins.then_inc(self.esem[e], 1)
            return body

        with nc.Block() as block:
            block.tensor(run("pe"))
            block.scalar(run("act"))
            block.vector(run("dve"))
            block.gpsimd(run("pool"))
            block.sync(run("sp"))


def weight_blocks():
    cat = {}
    for f in ("1", "2"):
        cat["gu" + f] = [(8, 512, [("gu" + f, j * 256, 256, 0), ("gu" + f, DFF + j * 256, 256, 256)]) for j in range(11)]
        cat["dn" + f] = [(22, 128, [("dn" + f, m * 128, 128, 0)]) for m in range(8)]
    cat["wq"] = [(8, 512, [("win", j * 512, 512, 0)]) for j in range(2)]
    cat["wk"] = [(8, 512, [("win", 1024 + j * 512, 512, 0)]) for j in range(2)]
    cat["wv"] = [(8, 512, [("win", 2048 + j * 512, 512, 0)]) for j in range(2)]
    cat["wu"] = [(8, 512, [("win", 3072 + j * 256, 256, 0), ("win", 4096 + j * 256, 256, 256)]) for j in range(4)]
    cat["wg"] = [(8, 512, [("win", 5120 + j * 512, 512, 0)]) for j in range(4)]
    for n in ("wa", "wb", "wo"):
        cat[n] = [(8, 512, [(n, j * 512, 512, 0)]) for j in range(2)]
    return cat


def build(NSLOT):
    assert NSLOT % 2 == 0
    S = NSLOT * TK
    NPAIR = NSLOT // 2
    nc = bass.Bass("TRN2", target_bir_lowering=False)

    def din(name, shape, dt=F32):
        return nc.dram_tensor(name, list(shape), dt, kind="ExternalInput").ap()

    xs = din("xs", [S, D])
    cosT = din("cosT", [128, S])
    sinT = din("sinT", [128, S])
    kbias_d = din("kbias", [128, NSLOT * 4])
    cvec_d = din("cvec", [128, 320])
    subln_d = din("sublnB", [128, 128])
    lamv_d = din("lamv", [128, 256])
    wsrc = {
        "gu1": din("gu1", [D, 2 * DFF]), "dn1": din("dn1", [DFF, D]),
        "win": din("win", [D, 7168]),
        "wa": din("wa", [D, D]), "wb": din("wb", [D, D]), "wo": din("wo", [D, D]),
        "gu2": din("gu2", [D, 2 * DFF]), "dn2": din("dn2", [DFF, D]),
    }
    out_d = nc.dram_tensor("out", [NPAIR * TK, D], F32, kind="ExternalOutput").ap()

    cat = weight_blocks()
    blk_index = {}
    nblk = 0
    for name, blks in cat.items():
        for j in range(len(blks)):
            blk_index[(name, j)] = nblk
            nblk += 1
    wbf = nc.dram_tensor("wbf", [nblk, 128, 4096], BF16).ap()
    kT_s = nc.dram_tensor("kT_s", [16, 64, S], BF16).ap()
    v_s = nc.dram_tensor("v_s", [8, 128, S // 128, 128], BF16).ap()
    qT_s = nc.dram_tensor("qT_s", [2, 16, 64, TK], BF16).ap()

    T = Tracker(nc)

    def sb(name, shape, dt):
        return nc.alloc_sbuf_tensor(name, list(shape), dt)

    stage = sb("stage", [128, 4096], F32)
    xT = sb("xT", [128, 8, TK], F32)
    hT = sb("hT", [128, 8, TK], BF16)
    actb = sb("actb", [128, NFF * TK], BF16)
    NW = 3
    wring = [sb("wring%d" % i, [128, 4096], BF16) for i in range(NW)]
    rstd = sb("rstd", [128, TK], F32)
    accx = sb("accx", [128, TK], F32)
    accx2 = sb("accx2", [128, TK], F32)
    tmpa = [sb("tmpa%d" % i, [128, TK], F32) for i in range(3)]
    ropec = sb("ropec", [128, TK], F32)
    ropes = sb("ropes", [128, TK], F32)
    r12 = sb("r12", [128, 2, TK], F32)
    qk = sb("qk", [128, 8, TK], BF16)
    vtok = sb("vtok", [128, 8, 4, 128], BF16)
    NKV = 6
    kbuf = [sb("kbuf%d" % i, [128, TK], BF16) for i in range(NKV)]
    vbuf = [sb("vbuf%d" % i, [128, 4, 132], BF16) for i in range(NKV)]
    qTb = [sb("qTb%d" % i, [128, TK], BF16) for i in range(2)]
    NP = 3
    pT = [sb("pT%d" % i, [128, 2, TK], BF16) for i in range(NP)]
    sqr = [sb("sqr%d" % i, [128, TK], BF16) for i in range(3)]
    Otok = sb("Otok", [128, 4, D], BF16)
    ohs = sb("ohs", [128, 4, 128], F32)
    on0 = sb("on0", [128, 4, 128], F32)
    osb = sb("osb", [128, 8 * 129], F32)
    junk = sb("junk", [128, 128], F32)
    OT = sb("OT", [128, 8, TK], BF16)
    cb = sb("cb", [128, 8, TK], BF16)
    gates = sb("gates", [128, 16, TK], BF16)
    hhalo = sb("hhalo", [128, 8, 32], BF16)
    cvec = sb("cvec_sb", [128, 320], F32)
    sublnB = sb("sublnB_sb", [128, 128], F32)
    lamv = sb("lamv_sb", [128, 256], F32)
    lamt = sb("lamt", [128, 8], F32)
    kbias = sb("kbias_sb", [128, NSLOT * 4], F32)
    identf = sb("identf", [128, 128], F32)
    identb = sb("identb", [128, 128], BF16)
    onesb = sb("onesb", [128, 128], BF16)
    maskneg = sb("maskneg", [128, 128], BF16)
    maskf = sb("maskf", [128, 128], F32)
    small = sb("small", [128, 64], F32)

    act3 = actb[:].rearrange("p (c t) -> p c t", t=TK)
    cin = actb[:].bitcast(F32)[:, 0:8 * 544].rearrange("p (c t) -> p c t", t=544)
    stage_x = stage[:].rearrange("p (a f) -> p a f", f=D)
    cacc = stage[:].rearrange("p (c t) -> p c t", t=TK)

    PS = nc.alloc_psum_tensor("ps", [128, 8, 512], F32)
    P = [PS[:, i, :] for i in range(8)]

    B_stage = Buf("stage")
    B_cacc = [Buf("cacc%d" % i) for i in range(8)]
    B_stage_all = [B_stage] + B_cacc
    B_accx = Buf("accx")
    B_accx2 = Buf("accx2")
    B_xT = [Buf("xT%d" % i) for i in range(8)]
    B_hT = [Buf("hT%d" % i) for i in range(8)]
    B_act = [Buf("act%d" % i) for i in range(NFF)]
    B_wr = [Buf("wr%d" % i) for i in range(NW)]
    B_rstd = Buf("rstd")
    B_tmpa = [Buf("tmpa%d" % i) for i in range(3)]
    B_rope = Buf("rope")
    B_rope2 = Buf("rope2")
    B_r12 = [Buf("r1"), Buf("r2")]
    B_qk = [Buf("qk%d" % i) for i in range(8)]
    B_vtok = Buf("vtok")
    B_kb = [Buf("kb%d" % i) for i in range(NKV)]
    B_vb = [Buf("vb%d" % i) for i in range(NKV)]
    B_qT = [Buf("qT%d" % i) for i in range(2)]
    B_pT = [Buf("pT%d" % i) for i in range(NP)]
    B_sqr = [Buf("sqr%d" % i) for i in range(3)]
    B_Otok = [Buf("Otok%d" % i) for i in range(8)]
    B_ohs = Buf("ohs")
    B_on0 = Buf("on0")
    B_osb = [Buf("osb%d" % i) for i in range(3)]
    B_junk = Buf("junk")
    B_OT = [Buf("OT%d" % i) for i in range(8)]
    B_cb = [Buf("cb%d" % i) for i in range(8)]
    B_gates = [Buf("g%d" % i) for i in range(16)]
    B_hhalo = Buf("hhalo")
    B_const = Buf("const")
    B_small = Buf("small")
    B_P = [Buf("P%d" % i) for i in range(8)]
    B_wbf = [[Buf("wbf%d_0" % i), Buf("wbf%d_1" % i)] for i in range(nblk)]
    B_kTs = [[Buf("kTs%d_%d" % (s, m)) for m in range(8)] for s in range(NSLOT)]
    B_vs = [Buf("vs%d" % s) for s in range(NSLOT)]
    B_qTs = [[Buf("qTs%d_%d" % (p_, m)) for m in range(8)] for p_ in range(2)]
    B_out = [Buf("out%d" % i) for i in range(4)]

    setup_loads = [
        (cvec[:], cvec_d), (sublnB[:], subln_d), (lamv[:], lamv_d), (kbias[:], kbias_d),
    ]
    for dst, src in setup_loads:
        T.dma("sp", "setup", lambda e, d=dst, s=src: e.dma_start(out=d, in_=s), writes=[B_const])
    T.op("pool", lambda e: e.memset(identf[:], 0.0), writes=[B_const])
    T.op("pool", lambda e: e.affine_select(out=identf[:], in_=identf[:], pattern=[[-1, 128]], compare_op=ALU.not_equal,
                                           fill=1.0, base=0, channel_multiplier=1), reads=[B_const], writes=[B_const])
    T.op("pool", lambda e: e.tensor_copy(out=identb[:], in_=identf[:]), reads=[B_const], writes=[B_const])
    T.op("pool", lambda e: e.memset(onesb[:], 1.0), writes=[B_const])
    T.op("pool", lambda e: e.memset(maskf[:], 0.0), writes=[B_const])
    T.op("pool", lambda e: e.affine_select(out=maskf[:], in_=maskf[:], pattern=[[1, 128]], compare_op=ALU.is_ge,
                                           fill=NEG, base=0, channel_multiplier=-1), reads=[B_const], writes=[B_const])
    T.op("pool", lambda e: e.tensor_copy(out=maskneg[:], in_=maskf[:]), reads=[B_const], writes=[B_const])
    for i in range(NKV):
        T.op("pool", lambda e, i=i: e.memset(vbuf[i][:, :, 128:132], 1.0), writes=[B_vb[i]])
    T.op("dve", lambda e: e.tensor_tensor(out=junk[:, 0:64], in0=lamv[:, 0:64], in1=lamv[:, 64:128], op=ALU.mult), reads=[B_const], writes=[B_junk])
    T.op("dve", lambda e: e.reduce_sum(out=lamt[:, 0:1], in_=junk[:, 0:64], axis=mybir.AxisListType.X), reads=[B_junk], writes=[B_small])
    T.op("dve", lambda e: e.tensor_tensor(out=junk[:, 64:128], in0=lamv[:, 128:192], in1=lamv[:, 192:256], op=ALU.mult), reads=[B_const], writes=[B_junk])
    T.op("dve", lambda e: e.reduce_sum(out=lamt[:, 1:2], in_=junk[:, 64:128], axis=mybir.AxisListType.X), reads=[B_junk], writes=[B_small])
    T.op("act", lambda e: e.activation(out=lamt[:, 2:4], in_=lamt[:, 0:2], func=AF.Exp), reads=[B_small], writes=[B_small])
    T.op("dve", lambda e: e.tensor_tensor(out=lamt[:, 4:5], in0=lamt[:, 3:4], in1=lamt[:, 2:3], op=ALU.subtract), reads=[B_small], writes=[B_small])
    T.op("dve", lambda e: e.tensor_scalar(out=lamt[:, 5:6], in0=lamt[:, 4:5], scalar1=-LAM_INIT, scalar2=None, op0=ALU.add), reads=[B_small], writes=[B_small])
    neglam = lamt[:, 5:6]

    grp_done = []
    cur_chan = None
    for name in ["gu1", "dn1", "wk", "wv", "wq", "wu", "wg", "wa", "wb", "wo", "gu2", "dn2"]:
        nb = len(cat[name])
        for j, (KC, W, parts) in enumerate(cat[name]):
            bi = blk_index[(name, j)]
            dstv = wbf[bi, :, 0:KC * W].rearrange("p (k w) -> p k w", w=W)
            chan = "pp_%s_%d" % (name, j // 2)
            first_of_group = chan != cur_chan
            last_of_group = (j % 2 == 1) or (j == nb - 1)
            if first_of_group:
                cur_chan = chan
                grp_done.append(Buf("grp_" + chan))
            for pi_, (sn, c0, ncol, d0) in enumerate(parts):
                src = wsrc[sn][:, c0:c0 + ncol].rearrange("(k p) n -> p k n", p=128)
                rd = []
                if first_of_group and pi_ == 0 and len(grp_done) >= 3:
                    rd = [grp_done[-3]]
                wr = [B_wbf[bi][pi_]]
                if last_of_group and pi_ == len(parts) - 1:
                    wr.append(grp_done[-1])
                T.dma("pool", chan, lambda e, d=dstv[:, :, d0:d0 + ncol], s=src: e.dma_start(out=d, in_=s), reads=rd, writes=wr)

    wctr = [0]

    def wload(name, j):
        KC, W, _ = cat[name][j]
        bi = blk_index[(name, j)]
        r = wctr[0] % NW
        wctr[0] += 1
        T.dma("sp", "wr%d" % r, lambda e, o=wring[r][:, 0:KC * W], i=wbf[bi, :, 0:KC * W]: e.dma_start(out=o, in_=i), reads=B_wbf[bi], writes=[B_wr[r]])
        return wring[r][:, 0:KC * W].rearrange("p (k w) -> p k w", w=W), B_wr[r]

    bctr = [0]

    def bank():
        i = bctr[0] % 4
        bctr[0] += 1
        return P[i], B_P[i]

    ectr = [0]

    def evac_copy(out_ap, in_ap, reads, writes):
        ectr[0] += 1
        if ectr[0] % 2 == 0:
            T.op("act", lambda e: e.activation(out=out_ap, in_=in_ap, func=AF.Copy), reads=reads, writes=writes)
        else:
            T.op("dve", lambda e: e.tensor_copy(out=out_ap, in_=in_ap), reads=reads, writes=writes)

    def mm(out, lhsT, rhs, start, stop, reads, writes, **kw):
        T.op("pe", lambda e: e.matmul(out, lhsT=lhsT, rhs=rhs, start=start, stop=stop, **kw), reads=reads, writes=writes, nosame=True)

    tctr = [0]

    def tmp():
        i = tctr[0] % 3
        tctr[0] += 1
        return tmpa[i], B_tmpa[i]

    sqctr = [0]

    pend_stats = []

    def stats_flush():
        while pend_stats:
            c, i = pend_stats.pop(0)
            mm(P[7][:], onesb[:], sqr[i][:], c == 0, c == 7, [B_sqr[i], B_const], [B_P[7]])

    def stats_chunk(c):
        stats_flush()
        i = sqctr[0] % 3
        sqctr[0] += 1
        T.op("act", lambda e: e.activation(out=sqr[i][:], in_=xT[:, c, :], func=AF.Square), reads=[B_xT[c]], writes=[B_sqr[i]])
        pend_stats.append((c, i))

    def rmsnorm(gcol, dst, B_dst):
        stats_flush()
        t, bt = tmp()
        T.op("act", lambda e: e.activation(out=t[:], in_=P[7][:], func=AF.Sqrt, bias=EPS, scale=1.0 / D), reads=[B_P[7]], writes=[bt])
        T.op("dve", lambda e: e.reciprocal(out=rstd[:], in_=t[:]), reads=[bt], writes=[B_rstd])
        for c in range(8):
            T.op("dve", lambda e, c=c: e.scalar_tensor_tensor(out=dst[:, c, :], in0=xT[:, c, :], scalar=cvec[:, gcol + c:gcol + c + 1],
                                                             in1=rstd[:], op0=ALU.mult, op1=ALU.mult),
                 reads=[B_xT[c], B_rstd, B_const], writes=[B_dst[c]])

    def ffn(f):
        for j in range(11):
            w, bw = wload("gu" + f, j)
            for i in range(2):
                m = 2 * j + i
                pa, bpa = bank()
                pb, bpb = bank()
                for kc in range(8):
                    mm(pa[:], w[:, kc, i * 128:(i + 1) * 128], hT[:, kc, :], kc == 0, kc == 7, [bw, B_hT[kc]], [bpa])
                for kc in range(8):
                    mm(pb[:], w[:, kc, 256 + i * 128:256 + (i + 1) * 128], hT[:, kc, :], kc == 0, kc == 7, [bw, B_hT[kc]], [bpb])
                t, bt = tmp()
                T.op("act", lambda e, t=t, pa=pa: e.activation(out=t[:], in_=pa[:], func=AF.Silu), reads=[bpa], writes=[bt])
                T.op("dve", lambda e, t=t, pb=pb, m=m: e.tensor_tensor(out=act3[:, m, :], in0=pb[:], in1=t[:], op=ALU.mult), reads=[bpb, bt], writes=[B_act[m]])
        for m in range(8):
            w, bw = wload("dn" + f, m)
            p, bp = bank()
            for kc in range(NFF):
                mm(p[:], w[:, kc, :], act3[:, kc, :], kc == 0, kc == NFF - 1, [bw, B_act[kc]], [bp])
            T.op("dve", lambda e, p=p, m=m: e.scalar_tensor_tensor(out=xT[:, m, :], in0=p[:], scalar=0.5, in1=xT[:, m, :], op0=ALU.mult, op1=ALU.add),
                 reads=[bp, B_xT[m]], writes=[B_xT[m]])
            stats_chunk(m)

    def issue_x(s):
        for a in range(4):
            src = xs[s * TK + a * 128:s * TK + (a + 1) * 128, :]
            T.dma("sp", "xld%d" % a, lambda e, a=a, src=src: e.dma_start(out=stage_x[:, a, :], in_=src), writes=[B_cacc[2 * a], B_cacc[2 * a + 1]])

    def load_x():
        for fc in range(8):
            p, bp = bank()
            for a in range(4):
                mm(p[:, a * 128:(a + 1) * 128], stage_x[:, a, fc * 128:(fc + 1) * 128], identf[:], True, True, [B_cacc[2 * a], B_cacc[2 * a + 1], B_const], [bp])
            evac_copy(xT[:, fc, :], p[:], [bp], [B_xT[fc]])
            stats_chunk(fc)

    def load_rope(s):
        T.dma("sp", "rope", lambda e: e.dma_start(out=ropec[:], in_=cosT[:, s * TK:(s + 1) * TK]), writes=[B_rope])
        T.dma("sp", "rope2", lambda e: e.dma_start(out=ropes[:], in_=sinT[:, s * TK:(s + 1) * TK]), writes=[B_rope2])

    def proj_qk(wname, dst_fn, B_dst):
        for j in range(2):
            w, bw = wload(wname, j)
            for i in range(4):
                m = 4 * j + i
                p, bp = bank()
                for kc in range(8):
                    mm(p[:], w[:, kc, i * 128:(i + 1) * 128], hT[:, kc, :], kc == 0, kc == 7, [bw, B_hT[kc]], [bp])
                if m < 2:
                    evac_copy(r12[:, m, :], p[:], [bp], [B_r12[m]])
                else:
                    evac_copy(qk[:, m, :], p[:], [bp], [B_qk[m]])
                if m == 1:
                    t1, b1 = tmp()
                    t2, b2 = tmp()
                    T.op("dve", lambda e, t1=t1: e.tensor_tensor(out=t1[:], in0=r12[:, 0, :], in1=ropec[:], op=ALU.mult), reads=[B_r12[0], B_rope], writes=[b1])
                    T.op("dve", lambda e, t2=t2: e.tensor_tensor(out=t2[:], in0=r12[:, 1, :], in1=ropes[:], op=ALU.mult), reads=[B_r12[1], B_rope2], writes=[b2])
                    T.op("dve", lambda e, t1=t1, t2=t2: e.tensor_tensor(out=qk[:, 0, :], in0=t1[:], in1=t2[:], op=ALU.subtract), reads=[b1, b2], writes=[B_qk[0]])
                    t3, b3 = tmp()
                    t4, b4 = tmp()
                    T.op("dve", lambda e, t3=t3: e.tensor_tensor(out=t3[:], in0=r12[:, 1, :], in1=ropec[:], op=ALU.mult), reads=[B_r12[1], B_rope], writes=[b3])
                    T.op("dve", lambda e, t4=t4: e.tensor_tensor(out=t4[:], in0=r12[:, 0, :], in1=ropes[:], op=ALU.mult), reads=[B_r12[0], B_rope2], writes=[b4])
                    T.op("dve", lambda e, t3=t3, t4=t4: e.tensor_tensor(out=qk[:, 1, :], in0=t3[:], in1=t4[:], op=ALU.add), reads=[b3, b4], writes=[B_qk[1]])
        for m in range(8):
            dst, chan = dst_fn(m)
            T.dma("pool", chan, lambda e, dst=dst, m=m: e.dma_start(out=dst, in_=qk[:, m, :]), reads=[B_qk[m]], writes=[B_dst[m]])

    def qk_dst(base, tok0, ntok):
        def f(m):
            if m == 0:
                return base[:, 0:8, tok0:tok0 + ntok]
            if m == 1:
                return base[:, 8:16, tok0:tok0 + ntok]
            mp = m - 2
            gq = mp % 2
            db = mp // 2
            return base[8 * gq:8 * gq + 8, 16 + 16 * db:32 + 16 * db, tok0:tok0 + ntok]
        return f

    def proj_v(s):
        for j in range(2):
            w, bw = wload("wv", j)
            for a in range(4):
                p, bp = bank()
                for kc in range(8):
                    mm(p[:], hT[:, kc, a * 128:(a + 1) * 128], w[:, kc, :], kc == 0, kc == 7, [bw, B_hT[kc]], [bp])
                evac_copy(vtok[:, 4 * j:4 * j + 4, a, :], p[:].rearrange("p (h e) -> p h e", e=128), [bp], [B_vtok])
        dst = v_s[:, :, 4 * s:4 * s + 4, :].rearrange("h p k e -> p h k e")
        T.dma("pool", "vw%d" % (s % 2), lambda e: e.dma_start(out=dst, in_=vtok[:]), reads=[B_vtok], writes=[B_vs[s]])

    def proj_u():
        ph, bph = P[7], B_P[7]
        for j in range(4):
            w, bw = wload("wu", j)
            for i in range(2):
                m = 2 * j + i
                for kc in range(8):
                    mm(ph[:, m * 32:(m + 1) * 32], w[:, kc, i * 128:(i + 1) * 128], hhalo[:, kc, :], kc == 0, kc == 7, [bw, B_hhalo], [bph])
                for kc in range(8):
                    mm(ph[:, 256 + m * 32:256 + (m + 1) * 32], w[:, kc, 256 + i * 128:256 + (i + 1) * 128], hhalo[:, kc, :], kc == 0, kc == 7, [bw, B_hhalo], [bph])
                pa, bpa = bank()
                pb, bpb = bank()
                for kc in range(8):
                    mm(pa[:], w[:, kc, i * 128:(i + 1) * 128], hT[:, kc, :], kc == 0, kc == 7, [bw, B_hT[kc]], [bpa])
                for kc in range(8):
                    mm(pb[:], w[:, kc, 256 + i * 128:256 + (i + 1) * 128], hT[:, kc, :], kc == 0, kc == 7, [bw, B_hT[kc]], [bpb])
                t, bt = tmp()
                T.op("act", lambda e, t=t, pb=pb: e.activation(out=t[:], in_=pb[:], func=AF.Sigmoid), reads=[bpb], writes=[bt])
                T.op("dve", lambda e, t=t, pa=pa, m=m: e.tensor_tensor(out=cin[:, m, 32:544], in0=pa[:], in1=t[:], op=ALU.mult), reads=[bpa, bt], writes=[B_act[m]])
        t, bt = tmp()
        T.op("act", lambda e, t=t: e.activation(out=t[:, 0:256], in_=ph[:, 256:512], func=AF.Sigmoid), reads=[bph], writes=[bt])
        T.op("dve", lambda e, t=t: e.tensor_tensor(out=cin[:, :, 0:32], in0=ph[:, 0:256].rearrange("p (c t) -> p c t", t=32),
                                                  in1=t[:, 0:256].rearrange("p (c t) -> p c t", t=32), op=ALU.mult), reads=[bph, bt], writes=B_act[0:8])

    def proj_g():
        for j in range(4):
            w, bw = wload("wg", j)
            for i in range(4):
                cc = 4 * j + i
                p, bp = bank()
                for kc in range(8):
                    mm(p[:], w[:, kc, i * 128:(i + 1) * 128], hT[:, kc, :], kc == 0, kc == 7, [bw, B_hT[kc]], [bp])
                T.op("act", lambda e, p=p, cc=cc: e.activation(out=gates[:, cc, :], in_=p[:], func=AF.Sigmoid, bias=cvec[:, 32 + cc:33 + cc]),
                     reads=[bp, B_const], writes=[B_gates[cc]])

    conv_first = [True]

    def conv_chunk(m):
        for j in range(31):
            wcol = cvec[:, 72 + m * 31 + j:73 + m * 31 + j]
            if j == 0:
                T.op("dve", lambda e, wcol=wcol: e.tensor_scalar(out=cacc[:, m, :], in0=cin[:, m, 2:2 + TK], scalar1=wcol, scalar2=cvec[:, 48 + m:49 + m],
                                                                 op0=ALU.mult, op1=ALU.add),
                     reads=[B_act[m], B_const], writes=[B_cacc[m]])
            else:
                T.op("dve", lambda e, wcol=wcol, j=j: e.scalar_tensor_tensor(out=cacc[:, m, :], in0=cin[:, m, j + 2:j + 2 + TK], scalar=wcol, in1=cacc[:, m, :],
                                                                              op0=ALU.mult, op1=ALU.add),
                     reads=[B_act[m], B_const, B_cacc[m]], writes=[B_cacc[m]])
        if conv_first[0]:
            conv_first[0] = False
            T.op("dve", lambda e: e.tensor_copy(out=accx[:], in_=cacc[:, m, :]), reads=[B_cacc[m]], writes=[B_accx])
            T.op("dve", lambda e: e.tensor_tensor(out=accx2[:], in0=cacc[:, m, :], in1=cacc[:, m, :], op=ALU.mult), reads=[B_cacc[m]], writes=[B_accx2])
        else:
            t, bt = tmp()
            T.op("dve", lambda e: e.tensor_tensor(out=accx[:], in0=accx[:], in1=cacc[:, m, :], op=ALU.add), reads=[B_cacc[m], B_accx], writes=[B_accx])
            T.op("dve", lambda e: e.tensor_tensor(out=t[:], in0=cacc[:, m, :], in1=cacc[:, m, :], op=ALU.mult), reads=[B_cacc[m]], writes=[bt])
            T.op("dve", lambda e: e.tensor_tensor(out=accx2[:], in0=accx2[:], in1=t[:], op=ALU.add), reads=[bt, B_accx2], writes=[B_accx2])

    def conv_ln():
        conv_first[0] = True
        T.op("dve", lambda e: e.tensor_copy(out=sqr[0][:], in_=accx[:]), reads=[B_accx], writes=[B_sqr[0]])
        T.op("dve", lambda e: e.tensor_copy(out=sqr[1][:], in_=accx2[:]), reads=[B_accx2], writes=[B_sqr[1]])
        mm(P[7][:], onesb[:], sqr[0][:], True, True, [B_sqr[0], B_const], [B_P[7]])
        ps2, bps2 = bank()
        mm(ps2[:], onesb[:], sqr[1][:], True, True, [B_sqr[1], B_const], [bps2])
        mean, bmean = tmp()
        T.op("dve", lambda e: e.tensor_scalar(out=mean[:], in0=P[7][:], scalar1=1.0 / D, scalar2=None, op0=ALU.mult), reads=[B_P[7]], writes=[bmean])
        msq, bmsq = tmp()
        T.op("dve", lambda e: e.tensor_tensor(out=msq[:], in0=mean[:], in1=mean[:], op=ALU.mult), reads=[bmean], writes=[bmsq])
        T.op("dve", lambda e: e.scalar_tensor_tensor(out=msq[:], in0=ps2[:], scalar=1.0 / D, in1=msq[:], op0=ALU.mult, op1=ALU.subtract), reads=[bps2, bmsq], writes=[bmsq])
        T.op("act", lambda e: e.activation(out=msq[:], in_=msq[:], func=AF.Sqrt, bias=EPS, scale=1.0), reads=[bmsq], writes=[bmsq])
        T.op("dve", lambda e: e.reciprocal(out=rstd[:], in_=msq[:]), reads=[bmsq], writes=[B_rstd])
        for m in range(8):
            T.op("dve", lambda e, m=m: e.tensor_tensor(out=cacc[:, m, :], in0=cacc[:, m, :], in1=mean[:], op=ALU.subtract), reads=[B_cacc[m], bmean], writes=[B_cacc[m]])
            T.op("dve", lambda e, m=m: e.tensor_tensor(out=cacc[:, m, :], in0=cacc[:, m, :], in1=rstd[:], op=ALU.mult), reads=[B_cacc[m], B_rstd], writes=[B_cacc[m]])
            T.op("act", lambda e, m=m: e.activation(out=cb[:, m, :], in_=cacc[:, m, :], func=AF.Silu, bias=cvec[:, 64 + m:65 + m], scale=cvec[:, 56 + m:57 + m]),
                 reads=[B_cacc[m], B_const], writes=[B_cb[m]])

    kvctr = [0]
    pctr = [0]
    oslot = {}
    for idx in range(8):
        oslot[(idx // 4, idx % 4)] = (4 + idx // 3, (idx % 3) * 129)

    sgc = [0]

    def attention(s, g):
        par = g % 2
        pend_tr = []
        pend_a = []
        conv_chunk(0)
        for h in range(8):
            qb_t, bq = qTb[h % 2], B_qT[h % 2]
            qsrc = qT_s[par, 2 * h:2 * h + 2, :, :].rearrange("c d t -> (c d) t")
            T.dma("sp", "qld%d" % (h % 2), lambda e, qb_t=qb_t, qsrc=qsrc: e.dma_start(out=qb_t[:], in_=qsrc), reads=B_qTs[par], writes=[bq])
            steps = [(sl, t) for sl in range(s + 1) for t in range(4)]
            rmap = {}

            def kv_load(sl):
                if sl in rmap:
                    return rmap[sl]
                r = kvctr[0] % NKV
                kvctr[0] += 1
                ksrc = kT_s[2 * h:2 * h + 2, :, sl * TK:(sl + 1) * TK].rearrange("c d t -> (c d) t")
                vsrc = v_s[h, :, 4 * sl:4 * sl + 4, :]
                T.dma("sp", "kv%d" % r, lambda e: e.dma_start(out=kbuf[r][:], in_=ksrc), reads=B_kTs[sl], writes=[B_kb[r]])
                T.dma("sp", "vv%d" % r, lambda e: e.dma_start(out=vbuf[r][:, :, 0:128], in_=vsrc), reads=[B_vs[sl]], writes=[B_vb[r]])
                rmap[sl] = r
                return r

            started = set()

            def qk_step(i):
                sl, t = steps[i]
                r = kv_load(sl)
                diag = sl == s
                kt = 4 * sl + t
                q0 = 128 * t if diag else 0
                gi = sgc[0] % 2
                sgc[0] += 1
                for c in range(2):
                    bk = 2 * gi + c
                    mm(P[bk][:, q0:TK], kbuf[r][64 * c:64 * c + 64, t * 128:(t + 1) * 128], qb_t[64 * c:64 * c + 64, q0:TK], True, not diag,
                       [B_kb[r], bq], [B_P[bk]])
                    if diag:
                        mm(P[bk][:, q0:q0 + 128], identb[:], maskneg[:], False, True, [B_const], [B_P[bk]])
                pi = pctr[0] % NP
                pctr[0] += 1
                if kt < 4:
                    T.op("act", lambda e: e.activation(out=pT[pi][:, :, q0:TK], in_=PS[:, 2 * gi:2 * gi + 2, q0:TK], func=AF.Exp,
                                                       bias=kbias[:, kt:kt + 1], scale=0.125),
                         reads=[B_P[2 * gi], B_P[2 * gi + 1], B_const], writes=[B_pT[pi]])
                else:
                    T.op("act", lambda e: e.activation(out=pT[pi][:, :, q0:TK], in_=PS[:, 2 * gi:2 * gi + 2, q0:TK], func=AF.Exp, scale=0.125),
                         reads=[B_P[2 * gi], B_P[2 * gi + 1]], writes=[B_pT[pi]])
                return pi

            def pv_step(i, pi):
                sl, t = steps[i]
                r = rmap[sl]
                diag = sl == s
                for c in range(2):
                    for qb in range(t if diag else 0, 4):
                        bk, off = oslot[(c, qb)]
                        first = bk not in started
                        started.add(bk)
                        mm(P[bk][:, off:off + 129], pT[pi][:, c, qb * 128:(qb + 1) * 128], vbuf[r][:, t, 0:129], first, bool(diag and t == qb),
                           [B_pT[pi], B_vb[r]], [B_P[bk]], skip_group_check=True)

            nst = len(steps)
            pis = {0: qk_step(0), 1: qk_step(1)}
            for i in range(nst):
                if i + 2 < nst:
                    pis[i + 2] = qk_step(i + 2)
                pv_step(i, pis[i])
                if i == min(4, nst - 1):
                    while pend_a:
                        pend_a.pop(0)()
                    if h >= 1:
                        conv_chunk(h)
                if i == min(12, nst - 1):
                    while pend_tr:
                        pend_tr.pop(0)()
            for bk in (4, 5, 6):
                n = 3 if bk < 6 else 2
                eng = "dve"
                if eng == "act":
                    T.op("act", lambda e, bk=bk, n=n: e.activation(out=osb[:, (bk - 4) * 387:(bk - 4) * 387 + n * 129], in_=P[bk][:, 0:n * 129], func=AF.Copy),
                         reads=[B_P[bk]], writes=[B_osb[bk - 4]])
                else:
                    T.op("dve", lambda e, bk=bk, n=n: e.tensor_copy(out=osb[:, (bk - 4) * 387:(bk - 4) * 387 + n * 129], in_=P[bk][:, 0:n * 129]),
                         reads=[B_P[bk]], writes=[B_osb[bk - 4]])
            lview = osb[:, 0:8 * 129].rearrange("p (a b) -> p a b", b=129)[:, :, 128]
            T.op("dve", lambda e: e.reciprocal(out=small[:, 0:8], in_=lview), reads=B_osb, writes=[B_small])
            T.op("dve", lambda e: e.tensor_scalar(out=small[:, 8:12], in0=small[:, 4:8], scalar1=neglam, scalar2=None, op0=ALU.mult), reads=[B_small], writes=[B_small])
            for qb in range(4):
                o0 = qb * 129
                o1 = (4 + qb) * 129
                T.op("dve", lambda e, qb=qb, o0=o0: e.tensor_scalar(out=on0[:, qb, :], in0=osb[:, o0:o0 + 128], scalar1=small[:, qb:qb + 1], scalar2=None, op0=ALU.mult),
                     reads=B_osb + [B_small], writes=[B_on0])
            for qb in range(4):
                o1 = (4 + qb) * 129
                T.op("dve", lambda e, qb=qb, o1=o1: e.scalar_tensor_tensor(out=ohs[:, qb, :], in0=osb[:, o1:o1 + 128], scalar=small[:, 8 + qb:9 + qb], in1=on0[:, qb, :],
                                                                           op0=ALU.mult, op1=ALU.add),
                     reads=B_osb + [B_small, B_on0], writes=[B_ohs])
            for qb in range(4):
                T.op("dve", lambda e, qb=qb: e.scalar_tensor_tensor(out=junk[:], in0=ohs[:, qb, :], scalar=1.0, in1=ohs[:, qb, :], op0=ALU.mult, op1=ALU.mult,
                                                                    accum_out=small[:, 16 + qb:17 + qb]),
                     reads=[B_ohs], writes=[B_junk, B_small])
            def do_p2(h=h):
                k1 = (1.0 - LAM_INIT) ** 2
                T.op("act", lambda e: e.activation(out=small[:, 20:24], in_=small[:, 16:20], func=AF.Sqrt, bias=EPS / k1, scale=1.0 / (128.0 * k1)), reads=[B_small], writes=[B_small])
                T.op("dve", lambda e: e.reciprocal(out=small[:, 24:28], in_=small[:, 20:24]), reads=[B_small], writes=[B_small])
                for qb in range(4):
                    T.op("dve", lambda e, qb=qb: e.scalar_tensor_tensor(out=Otok[:, qb, h * 128:(h + 1) * 128], in0=ohs[:, qb, :], scalar=small[:, 24 + qb:25 + qb], in1=sublnB[:],
                                                                        op0=ALU.mult, op1=ALU.mult),
                         reads=[B_ohs, B_small, B_const], writes=[B_Otok[h]])
            pend_a.append(do_p2)

            def do_tr(h=h):
                p, bp = P[7], B_P[7]
                for qb in range(4):
                    mm(p[:, qb * 128:(qb + 1) * 128], Otok[:, qb, h * 128:(h + 1) * 128], identb[:], True, True, [B_Otok[h], B_const], [bp])
                evac_copy(OT[:, h, :], p[:], [bp], [B_OT[h]])
            pend_tr.append(do_tr)
        while pend_a:
            pend_a.pop(0)()
        while pend_tr:
            pend_tr.pop(0)()

    def mix_out():
        for j in range(2):
            wA, bwA = wload("wa", j)
            for i in range(4):
                m = 4 * j + i
                pa, bpa = bank()
                for kc in range(8):
                    mm(pa[:], wA[:, kc, i * 128:(i + 1) * 128], OT[:, kc, :], kc == 0, kc == 7, [bwA, B_OT[kc]], [bpa])
                T.op("dve", lambda e, pa=pa, m=m: e.tensor_tensor(out=hT[:, m, :], in0=pa[:], in1=gates[:, m, :], op=ALU.mult), reads=[bpa, B_gates[m]], writes=[B_hT[m]])
        for j in range(2):
            wB, bwB = wload("wb", j)
            for i in range(4):
                m = 4 * j + i
                pb, bpb = bank()
                for kc in range(8):
                    mm(pb[:], wB[:, kc, i * 128:(i + 1) * 128], cb[:, kc, :], kc == 0, kc == 7, [bwB, B_cb[kc]], [bpb])
                t2, b2 = tmp()
                T.op("dve", lambda e, t2=t2, pb=pb, m=m: e.tensor_tensor(out=t2[:], in0=pb[:], in1=gates[:, 8 + m, :], op=ALU.mult), reads=[bpb, B_gates[8 + m]], writes=[b2])
                T.op("dve", lambda e, t2=t2, m=m: e.tensor_tensor(out=qk[:, m, :], in0=t2[:], in1=hT[:, m, :], op=ALU.add), reads=[b2, B_hT[m]], writes=[B_qk[m]])
        for j in range(2):
            w, bw = wload("wo", j)
            for i in range(4):
                m = 4 * j + i
                p, bp = bank()
                for kc in range(8):
                    mm(p[:], w[:, kc, i * 128:(i + 1) * 128], qk[:, kc, :], kc == 0, kc == 7, [bw, B_qk[kc]], [bp])
                T.op("dve", lambda e, p=p, m=m: e.tensor_tensor(out=xT[:, m, :], in0=p[:], in1=xT[:, m, :], op=ALU.add), reads=[bp, B_xT[m]], writes=[B_xT[m]])
                stats_chunk(m)

    def store_out(g):
        for a in range(4):
            for half in range(2):
                p, bp = bank()
                for i in range(4):
                    fc = half * 4 + i
                    mm(p[:, i * 128:(i + 1) * 128], xT[:, fc, a * 128:(a + 1) * 128], identf[:], True, True, [B_xT[fc], B_const], [bp])
                evac_copy(stage_x[:, a, half * 512:(half + 1) * 512], p[:], [bp], [B_cacc[2 * a + half]])
            dst = out_d[g * TK + a * 128:g * TK + (a + 1) * 128, :]
            T.dma("pool", "outst%d" % a, lambda e, a=a, dst=dst: e.dma_start(out=dst, in_=stage_x[:, a, :]), reads=[B_cacc[2 * a], B_cacc[2 * a + 1]], writes=[B_out[a]])

    for g in range(NPAIR):
        s0, s1 = 2 * g, 2 * g + 1
        issue_x(s0)
        load_x()
        issue_x(s1)
        rmsnorm(0, hT, B_hT)
        ffn("1")
        rmsnorm(8, hT, B_hT)
        load_rope(s0)
        proj_qk("wk", lambda m, s0=s0: (qk_dst(kT_s, s0 * TK, TK)(m), "kw%d" % (s0 % 2)), B_kTs[s0])
        proj_v(s0)
        T.op("dve", lambda e: e.tensor_copy(out=hhalo[:], in_=hT[:, :, TK - 32:TK]), reads=B_hT, writes=[B_hhalo])
        load_x()
        rmsnorm(0, hT, B_hT)
        ffn("1")
        rmsnorm(8, hT, B_hT)
        load_rope(s1)
        proj_qk("wk", lambda m, s1=s1: (qk_dst(kT_s, s1 * TK, TK)(m), "kw%d" % (s1 % 2)), B_kTs[s1])
        proj_v(s1)
        proj_qk("wq", lambda m, g=g: (qk_dst(qT_s[g % 2], 0, TK)(m), "qw"), B_qTs[g % 2])
        proj_u()
        proj_g()
        attention(s1, g)
        conv_ln()
        mix_out()
        rmsnorm(16, hT, B_hT)
        ffn("2")
        rmsnorm(24, xT, B_xT)
        store_out(g)
    T.op("sp", lambda e: e.nop(), reads=B_out)
    T.emit()
    return nc


def _perm_qk_cols():
    perm = np.zeros(1024, dtype=np.int64)
    for m in range(8):
        for p in range(128):
            if m == 0:
                c, i = p // 8, p % 8
                o = c * 64 + i
            elif m == 1:
                c, i = p // 8, p % 8
                o = c * 64 + 8 + i
            else:
                mp = m - 2
                gq, db = mp % 2, mp // 2
                c, i = 8 * gq + p // 16, p % 16
                o = c * 64 + 16 + 16 * db + i
            perm[m * 128 + p] = o
    return perm


def _fm(v):
    v = np.asarray(v, dtype=np.float32).reshape(-1, 128)
    return np.ascontiguousarray(v.T)


_NC_CACHE = {}


def kernel(**inputs):
    x = np.asarray(inputs["x"], dtype=np.float32)
    Bn, S, _ = x.shape
    NSLOT = S // TK
    ncores = 2 * Bn
    f32 = lambda a: np.ascontiguousarray(np.asarray(a, dtype=np.float32))

    perm = _perm_qk_cols()
    w_in = f32(inputs["w_in"][0])
    win = np.concatenate([w_in[:, perm], w_in[:, 1024 + perm], w_in[:, 2048:]], axis=1)
    win = np.ascontiguousarray(win)

    cvec = np.zeros((128, 320), np.float32)
    cvec[:, 0:8] = _fm(inputs["ffn1_norm"][0])
    cvec[:, 8:16] = _fm(inputs["mix_norm"][0])
    cvec[:, 16:24] = _fm(inputs["ffn2_norm"][0])
    cvec[:, 24:32] = _fm(inputs["final_norm"])
    cvec[:, 32:48] = _fm(inputs["b_gate"][0])
    cvec[:, 48:56] = _fm(inputs["conv_b"][0])
    cvec[:, 56:64] = _fm(inputs["conv_ln_g"][0])
    cvec[:, 64:72] = _fm(inputs["conv_ln_b"][0])
    cw = f32(inputs["conv_w"][0])
    cvec[:, 72:320] = cw.T.reshape(8, 128, 31).transpose(1, 0, 2).reshape(128, 248)
    sublnB = np.ascontiguousarray(np.broadcast_to(f32(inputs["attn_subln"][0])[None, :], (128, 128)))
    lamv = np.concatenate([f32(inputs["lambda_q1"][0]), f32(inputs["lambda_k1"][0]), f32(inputs["lambda_q2"][0]), f32(inputs["lambda_k2"][0])])
    lamv = np.ascontiguousarray(np.broadcast_to(lamv[None, :], (128, 256)))

    inv_freq = (np.float32(500000.0) ** (-np.arange(0, 16, 2, dtype=np.float32) / np.float32(16))).astype(np.float32)

    shared = {
        "cvec": cvec, "sublnB": sublnB, "lamv": lamv, "win": win,
        "gu1": f32(inputs["ffn1_w_gate_up"][0]), "dn1": f32(inputs["ffn1_w_down"][0]),
        "wa": f32(inputs["w_attn_out"][0]), "wb": f32(inputs["w_conv_out"][0]), "wo": f32(inputs["w_out"][0]),
        "gu2": f32(inputs["ffn2_w_gate_up"][0]), "dn2": f32(inputs["ffn2_w_down"][0]),
    }
    in_maps = []
    for core in range(ncores):
        b, j = core // 2, core % 2
        shift = TK * (1 - j)
        xs = np.zeros((S, D), np.float32)
        xs[shift:] = x[b, :S - shift]
        pos = np.maximum(np.arange(S, dtype=np.float32) - np.float32(shift), np.float32(0.0)).astype(np.float32)
        ang = pos[None, :] * inv_freq[:, None]
        cosT = np.ascontiguousarray(np.tile(np.cos(ang).astype(np.float32), (16, 1)))
        sinT = np.ascontiguousarray(np.tile(np.sin(ang).astype(np.float32), (16, 1)))
        kbias = np.zeros((128, NSLOT * 4), np.float32)
        if j == 0:
            kbias[:, 0:4] = NEG
        m = dict(shared)
        m.update({"xs": xs, "cosT": cosT, "sinT": sinT, "kbias": kbias})
        in_maps.append(m)

    if NSLOT not in _NC_CACHE:
        _NC_CACHE[NSLOT] = build(NSLOT)
    nc = _NC_CACHE[NSLOT]
    res = run_bass_kernel_spmd(nc, in_maps, core_ids=list(range(ncores)))
    out = np.zeros((Bn, S, D), np.float32)
    for core in range(ncores):
        b, j = core // 2, core % 2
        o = res.results[core]["out"]
        for g in range(NSLOT // 2):
            sbk = 2 * g + j
            out[b, sbk * TK:(sbk + 1) * TK] = o[g * TK:(g + 1) * TK]
    return out
```

```python
import math
import numpy as np
import concourse.bass as bass
import concourse.mybir as mybir
from concourse.bass_utils import run_bass_kernel_spmd

F32 = mybir.dt.float32
BF16 = mybir.dt.bfloat16
AF = mybir.ActivationFunctionType
ALU = mybir.AluOpType

D = 1024
TK = 512
DFF = 2816
NFF = 22
EPS = 1e-5
LAM_INIT = 0.8 - 0.6 * math.exp(-0.3 * 0)
NEG = -30000.0


class Buf:
    __slots__ = ("name", "w", "r")

    def __init__(self, name):
        self.name = name
        self.w = None
        self.r = []


class Op:
    __slots__ = ("eng", "fn", "deps", "kind", "chan", "cum", "sig", "needed", "dwait")


class Tracker:
    ENGS = ("pe", "act", "dve", "pool", "sp")

    def __init__(self, nc):
        self.nc = nc
        self.ops = {e: [] for e in self.ENGS}
        self.all = []
        self.chan_cum = {}

    def op(self, eng, fn, reads=(), writes=(), nosame=False):
        o = Op()
        o.eng = eng
        o.fn = fn
        o.kind = "c"
        o.chan = None
        o.cum = 0
        o.sig = 0
        o.needed = False
        deps = set()
        for b in reads:
            if b.w is not None:
                deps.add(b.w)
        for b in writes:
            if b.w is not None:
                deps.add(b.w)
            for r in b.r:
                deps.add(r)
        for b in reads:
            b.r.append(o)
        for b in writes:
            b.w = o
            b.r = []
        deps.discard(o)
        o.dwait = {}
        for d in deps:
            if d.kind == "d":
                o.dwait[d.chan] = self.chan_cum[d.chan]
        if nosame:
            deps = {d for d in deps if not (d.kind == "c" and d.eng == eng)}
        o.deps = deps
        self.ops[eng].append(o)
        self.all.append(o)
        return o

    def dma(self, queue, chan, fn, reads=(), writes=(), soft=False):
        o = self.op(queue, fn, reads, writes)
        if soft:
            for d in o.deps:
                if d.kind == "d":
                    o.dwait[d.chan] = d.cum
        o.kind = "d"
        o.chan = chan
        self.chan_cum[chan] = self.chan_cum.get(chan, 0) + 16
        o.cum = self.chan_cum[chan]
        return o

    def emit(self):
        nc = self.nc
        for o in self.all:
            for d in o.deps:
                d.needed = True
        for e in self.ENGS:
            c = 0
            for o in self.ops[e]:
                if o.kind == "c" and o.needed:
                    c += 1
                    o.sig = c
        self.esem = {e: nc.alloc_semaphore(name="prog_" + e) for e in self.ENGS}
        self.csem = {ch: nc.alloc_semaphore(name="ch_" + str(ch)) for ch in self.chan_cum}

        pe_index = {}
        for i, o in enumerate(self.ops["pe"]):
            pe_index[o] = i
        pe_front = {}
        last_in_q = {e: None for e in self.ENGS}
        for o in self.all:
            f = pe_index.get(o, -1)
            for d in o.deps:
                f = max(f, pe_front[d])
            p = last_in_q[o.eng]
            if p is not None:
                f = max(f, pe_front[p])
            pe_front[o] = f
            last_in_q[o.eng] = o
        sig_owner = {e: {} for e in self.ENGS}
        for e in self.ENGS:
            for o in self.ops[e]:
                if o.kind == "c" and o.needed:
                    sig_owner[e][o.sig] = o
        TH, H = 40, 20

        def run(e):
            def body(eng):
                waited = {}
                sched = {}
                plan = []
                for pi, o in enumerate(self.ops[e]):
                    need = {}
                    for d in o.deps:
                        if d.kind == "c":
                            key = ("e", d.eng)
                            v = d.sig
                        else:
                            key = ("c", d.chan)
                            v = o.dwait[d.chan]
                        if v > need.get(key, 0):
                            need[key] = v
                    here = []
                    for key, v in need.items():
                        pos = pi
                        if e == "pe" and key[0] == "e" and key[1] != "pe":
                            q = sig_owner[key[1]].get(v)
                            if q is not None and pi - pe_front[q] >= TH:
                                pos = max(pe_front[q] + 1, pi - H)
                        if pos < pi:
                            sched.setdefault(pos, []).append((key, v))
                        else:
                            here.append((key, v))
                    plan.append(here)
                for pi, o in enumerate(self.ops[e]):
                    for key, v in sched.get(pi, []) + plan[pi]:
                        if waited.get(key, 0) >= v:
                            continue
                        waited[key] = v
                        sem = self.esem[key[1]] if key[0] == "e" else self.csem[key[1]]
                        eng.wait_ge(sem, v)
                    ins = o.fn(eng)
                    if o.kind == "d":
                        ins.then_inc(self.csem[o.chan], 16)
                    elif o.needed:
                        ins.then_inc(self.esem[e], 1)
            return body

        with nc.Block() as block:
            block.tensor(run("pe"))
            block.scalar(run("act"))
            block.vector(run("dve"))
            block.gpsimd(run("pool"))
            block.sync(run("sp"))


def weight_blocks():
    cat = {}
    for f in ("1", "2"):
        cat["gu" + f] = [(8, 512, [("gu" + f, j * 256, 256, 0), ("gu" + f, DFF + j * 256, 256, 256)]) for j in range(11)]
        cat["dn" + f] = [(22, 128, [("dn" + f, m * 128, 128, 0)]) for m in range(8)]
    cat["wq"] = [(8, 512, [("win", j * 512, 512, 0)]) for j in range(2)]
    cat["wk"] = [(8, 512, [("win", 1024 + j * 512, 512, 0)]) for j in range(2)]
    cat["wv"] = [(8, 512, [("win", 2048 + j * 512, 512, 0)]) for j in range(2)]
    cat["wu"] = [(8, 512, [("win", 3072 + j * 256, 256, 0), ("win", 4096 + j * 256, 256, 256)]) for j in range(4)]
    cat["wg"] = [(8, 512, [("win", 5120 + j * 512, 512, 0)]) for j in range(4)]
    for n in ("wa", "wb", "wo"):
        cat[n] = [(8, 512, [(n, j * 512, 512, 0)]) for j in range(2)]
    return cat


def build(NSLOT):
    assert NSLOT % 2 == 0
    S = NSLOT * TK
    NPAIR = NSLOT // 2
    nc = bass.Bass("TRN2", target_bir_lowering=False)

    def din(name, shape, dt=F32):
        return nc.dram_tensor(name, list(shape), dt, kind="ExternalInput").ap()

    xs = din("xs", [S, D])
    cosT = din("cosT", [128, S])
    sinT = din("sinT", [128, S])
    kbias_d = din("kbias", [128, NSLOT * 4])
    cvec_d = din("cvec", [128, 320])
    subln_d = din("sublnB", [128, 128])
    lamv_d = din("lamv", [128, 256])
    wsrc = {
        "gu1": din("gu1", [D, 2 * DFF]), "dn1": din("dn1", [DFF, D]),
        "win": din("win", [D, 7168]),
        "wa": din("wa", [D, D]), "wb": din("wb", [D, D]), "wo": din("wo", [D, D]),
        "gu2": din("gu2", [D, 2 * DFF]), "dn2": din("dn2", [DFF, D]),
    }
    out_d = nc.dram_tensor("out", [NPAIR * TK, D], F32, kind="ExternalOutput").ap()

    cat = weight_blocks()
    blk_index = {}
    nblk = 0
    for name, blks in cat.items():
        for j in range(len(blks)):
            blk_index[(name, j)] = nblk
            nblk += 1
    wbf = nc.dram_tensor("wbf", [nblk, 128, 4096], BF16).ap()
    kT_s = nc.dram_tensor("kT_s", [16, 64, S], BF16).ap()
    v_s = nc.dram_tensor("v_s", [8, 128, S // 128, 128], BF16).ap()
    qT_s = nc.dram_tensor("qT_s", [2, 16, 64, TK], BF16).ap()

    T = Tracker(nc)

    def sb(name, shape, dt):
        return nc.alloc_sbuf_tensor(name, list(shape), dt)

    stage = sb("stage", [128, 4096], F32)
    xT = sb("xT", [128, 8, TK], F32)
    hT = sb("hT", [128, 8, TK], BF16)
    actb = sb("actb", [128, NFF * TK], BF16)
    NW = 3
    wring = [sb("wring%d" % i, [128, 4096], BF16) for i in range(NW)]
    rstd = sb("rstd", [128, TK], F32)
    accx = sb("accx", [128, TK], F32)
    accx2 = sb("accx2", [128, TK], F32)
    tmpa = [sb("tmpa%d" % i, [128, TK], F32) for i in range(3)]
    ropec = sb("ropec", [128, TK], F32)
    ropes = sb("ropes", [128, TK], F32)
    r12 = sb("r12", [128, 2, TK], F32)
    qk = sb("qk", [128, 8, TK], BF16)
    vtok = sb("vtok", [128, 8, 4, 128], BF16)
    NKV = 6
    kbuf = [sb("kbuf%d" % i, [128, TK], BF16) for i in range(NKV)]
    vbuf = [sb("vbuf%d" % i, [128, 4, 132], BF16) for i in range(NKV)]
    qTb = [sb("qTb%d" % i, [128, TK], BF16) for i in range(2)]
    NP = 3
    pT = [sb("pT%d" % i, [128, 2, TK], BF16) for i in range(NP)]
    sqr = [sb("sqr%d" % i, [128, TK], BF16) for i in range(3)]
    Otok = sb("Otok", [128, 4, D], BF16)
    ohs = sb("ohs", [128, 4, 128], F32)
    on0 = sb("on0", [128, 4, 128], F32)
    osb = sb("osb", [128, 8 * 129], F32)
    junk = sb("junk", [128, 128], F32)
    OT = sb("OT", [128, 8, TK], BF16)
    cb = sb("cb", [128, 8, TK], BF16)
    gates = sb("gates", [128, 16, TK], BF16)
    hhalo = sb("hhalo", [128, 8, 32], BF16)
    cvec = sb("cvec_sb", [128, 320], F32)
    sublnB = sb("sublnB_sb", [128, 128], F32)
    lamv = sb("lamv_sb", [128, 256], F32)
    lamt = sb("lamt", [128, 8], F32)
    kbias = sb("kbias_sb", [128, NSLOT * 4], F32)
    identf = sb("identf", [128, 128], F32)
    identb = sb("identb", [128, 128], BF16)
    onesb = sb("onesb", [128, 128], BF16)
    maskneg = sb("maskneg", [128, 128], BF16)
    maskf = sb("maskf", [128, 128], F32)
    small = sb("small", [128, 64], F32)
    mhalf = sb("mhalf", [128, 4], F32)

    act3 = actb[:].rearrange("p (c t) -> p c t", t=TK)
    cin = actb[:].bitcast(F32)[:, 0:8 * 544].rearrange("p (c t) -> p c t", t=544)
    stage_x = stage[:].rearrange("p (a f) -> p a f", f=D)
    cacc = stage[:].rearrange("p (c t) -> p c t", t=TK)

    PS = nc.alloc_psum_tensor("ps", [128, 8, 512], F32)
    P = [PS[:, i, :] for i in range(8)]

    B_stage = Buf("stage")
    B_cacc = [Buf("cacc%d" % i) for i in range(8)]
    B_stage_all = [B_stage] + B_cacc
    B_accx = Buf("accx")
    B_accx2 = Buf("accx2")
    B_xT = [Buf("xT%d" % i) for i in range(8)]
    B_hT = [Buf("hT%d" % i) for i in range(8)]
    B_act = [Buf("act%d" % i) for i in range(NFF)]
    B_wr = [Buf("wr%d" % i) for i in range(NW)]
    B_rstd = Buf("rstd")
    B_tmpa = [Buf("tmpa%d" % i) for i in range(3)]
    B_rope = Buf("rope")
    B_rope2 = Buf("rope2")
    B_r12 = [Buf("r1"), Buf("r2")]
    B_qk = [Buf("qk%d" % i) for i in range(8)]
    B_vtok = Buf("vtok")
    B_kb = [Buf("kb%d" % i) for i in range(NKV)]
    B_vb = [Buf("vb%d" % i) for i in range(NKV)]
    B_qT = [Buf("qT%d" % i) for i in range(2)]
    B_pT = [Buf("pT%d" % i) for i in range(NP)]
    B_sqr = [Buf("sqr%d" % i) for i in range(3)]
    B_Otok = [Buf("Otok%d" % i) for i in range(8)]
    B_ohs = Buf("ohs")
    B_on0 = Buf("on0")
    B_osb = [Buf("osb%d" % i) for i in range(3)]
    B_junk = Buf("junk")
    B_OT = [Buf("OT%d" % i) for i in range(8)]
    B_cb = [Buf("cb%d" % i) for i in range(8)]
    B_gates = [Buf("g%d" % i) for i in range(16)]
    B_hhalo = Buf("hhalo")
    B_const = Buf("const")
    B_small = Buf("small")
    B_P = [Buf("P%d" % i) for i in range(8)]
    B_wbf = [[Buf("wbf%d_0" % i), Buf("wbf%d_1" % i)] for i in range(nblk)]
    B_kTs = [[Buf("kTs%d_%d" % (s, m)) for m in range(8)] for s in range(NSLOT)]
    B_vs = [Buf("vs%d" % s) for s in range(NSLOT)]
    B_qTs = [[Buf("qTs%d_%d" % (p_, m)) for m in range(8)] for p_ in range(2)]
    B_out = [Buf("out%d" % i) for i in range(4)]

    setup_loads = [
        (cvec[:], cvec_d), (sublnB[:], subln_d), (lamv[:], lamv_d), (kbias[:], kbias_d),
    ]
    for dst, src in setup_loads:
        T.dma("sp", "setup", lambda e, d=dst, s=src: e.dma_start(out=d, in_=s), writes=[B_const])
    T.op("pool", lambda e: e.memset(identf[:], 0.0), writes=[B_const])
    T.op("pool", lambda e: e.affine_select(out=identf[:], in_=identf[:], pattern=[[-1, 128]], compare_op=ALU.not_equal,
                                           fill=1.0, base=0, channel_multiplier=1), reads=[B_const], writes=[B_const])
    T.op("pool", lambda e: e.tensor_copy(out=identb[:], in_=identf[:]), reads=[B_const], writes=[B_const])
    T.op("pool", lambda e: e.memset(onesb[:], 1.0), writes=[B_const])
    T.op("pool", lambda e: e.memset(mhalf[:], -0.5), writes=[B_const])
    T.op("pool", lambda e: e.memset(maskf[:], 0.0), writes=[B_const])
    T.op("pool", lambda e: e.affine_select(out=maskf[:], in_=maskf[:], pattern=[[1, 128]], compare_op=ALU.is_ge,
                                           fill=NEG, base=0, channel_multiplier=-1), reads=[B_const], writes=[B_const])
    T.op("pool", lambda e: e.tensor_copy(out=maskneg[:], in_=maskf[:]), reads=[B_const], writes=[B_const])
    for i in range(NKV):
        T.op("pool", lambda e, i=i: e.memset(vbuf[i][:, :, 128:132], 1.0), writes=[B_vb[i]])
    T.op("dve", lambda e: e.tensor_tensor(out=junk[:, 0:64], in0=lamv[:, 0:64], in1=lamv[:, 64:128], op=ALU.mult), reads=[B_const], writes=[B_junk])
    T.op("dve", lambda e: e.reduce_sum(out=lamt[:, 0:1], in_=junk[:, 0:64], axis=mybir.AxisListType.X), reads=[B_junk], writes=[B_small])
    T.op("dve", lambda e: e.tensor_tensor(out=junk[:, 64:128], in0=lamv[:, 128:192], in1=lamv[:, 192:256], op=ALU.mult), reads=[B_const], writes=[B_junk])
    T.op("dve", lambda e: e.reduce_sum(out=lamt[:, 1:2], in_=junk[:, 64:128], axis=mybir.AxisListType.X), reads=[B_junk], writes=[B_small])
    T.op("act", lambda e: e.activation(out=lamt[:, 2:4], in_=lamt[:, 0:2], func=AF.Exp), reads=[B_small], writes=[B_small])
    T.op("dve", lambda e: e.tensor_tensor(out=lamt[:, 4:5], in0=lamt[:, 3:4], in1=lamt[:, 2:3], op=ALU.subtract), reads=[B_small], writes=[B_small])
    T.op("dve", lambda e: e.tensor_scalar(out=lamt[:, 5:6], in0=lamt[:, 4:5], scalar1=-LAM_INIT, scalar2=None, op0=ALU.add), reads=[B_small], writes=[B_small])
    neglam = lamt[:, 5:6]

    grp_done = []
    cur_chan = None
    for name in ["gu1", "dn1", "wk", "wv", "wq", "wu", "wg", "wa", "wb", "wo", "gu2", "dn2"]:
        nb = len(cat[name])
        for j, (KC, W, parts) in enumerate(cat[name]):
            bi = blk_index[(name, j)]
            dstv = wbf[bi, :, 0:KC * W].rearrange("p (k w) -> p k w", w=W)
            chan = "pp_%s_%d" % (name, j // 2)
            first_of_group = chan != cur_chan
            last_of_group = (j % 2 == 1) or (j == nb - 1)
            if first_of_group:
                cur_chan = chan
                grp_done.append(Buf("grp_" + chan))
            for pi_, (sn, c0, ncol, d0) in enumerate(parts):
                src = wsrc[sn][:, c0:c0 + ncol].rearrange("(k p) n -> p k n", p=128)
                rd = []
                if first_of_group and pi_ == 0 and len(grp_done) >= 3:
                    rd = [grp_done[-3]]
                wr = [B_wbf[bi][pi_]]
                if last_of_group and pi_ == len(parts) - 1:
                    wr.append(grp_done[-1])
                T.dma("pool", chan, lambda e, d=dstv[:, :, d0:d0 + ncol], s=src: e.dma_start(out=d, in_=s), reads=rd, writes=wr)

    wctr = [0]

    def wload(name, j):
        KC, W, _ = cat[name][j]
        bi = blk_index[(name, j)]
        r = wctr[0] % NW
        wctr[0] += 1
        T.dma("sp", "wr%d" % r, lambda e, o=wring[r][:, 0:KC * W], i=wbf[bi, :, 0:KC * W]: e.dma_start(out=o, in_=i), reads=B_wbf[bi], writes=[B_wr[r]])
        return wring[r][:, 0:KC * W].rearrange("p (k w) -> p k w", w=W), B_wr[r]

    bctr = [0]

    def bank():
        i = bctr[0] % 4
        bctr[0] += 1
        return P[i], B_P[i]

    ectr = [0]

    def evac_copy(out_ap, in_ap, reads, writes):
        ectr[0] += 1
        if ectr[0] % 2 == 0:
            T.op("act", lambda e: e.activation(out=out_ap, in_=in_ap, func=AF.Copy), reads=reads, writes=writes)
        else:
            T.op("dve", lambda e: e.tensor_copy(out=out_ap, in_=in_ap), reads=reads, writes=writes)

    def mm(out, lhsT, rhs, start, stop, reads, writes, **kw):
        T.op("pe", lambda e: e.matmul(out, lhsT=lhsT, rhs=rhs, start=start, stop=stop, **kw), reads=reads, writes=writes, nosame=True)

    tctr = [0]

    def tmp():
        i = tctr[0] % 3
        tctr[0] += 1
        return tmpa[i], B_tmpa[i]

    sqctr = [0]

    pend_stats = []

    def stats_flush():
        while pend_stats:
            c, i = pend_stats.pop(0)
            mm(P[7][:], onesb[:], sqr[i][:], c == 0, c == 7, [B_sqr[i], B_const], [B_P[7]])

    def stats_chunk(c):
        stats_flush()
        i = sqctr[0] % 3
        sqctr[0] += 1
        T.op("act", lambda e: e.activation(out=sqr[i][:], in_=xT[:, c, :], func=AF.Square), reads=[B_xT[c]], writes=[B_sqr[i]])
        pend_stats.append((c, i))

    def rmsnorm(gcol, dst, B_dst):
        stats_flush()
        t, bt = tmp()
        T.op("act", lambda e: e.activation(out=t[:], in_=P[7][:], func=AF.Sqrt, bias=EPS, scale=1.0 / D), reads=[B_P[7]], writes=[bt])
        T.op("dve", lambda e: e.reciprocal(out=rstd[:], in_=t[:]), reads=[bt], writes=[B_rstd])
        for c in range(8):
            T.op("dve", lambda e, c=c: e.scalar_tensor_tensor(out=dst[:, c, :], in0=xT[:, c, :], scalar=cvec[:, gcol + c:gcol + c + 1],
                                                             in1=rstd[:], op0=ALU.mult, op1=ALU.mult),
                 reads=[B_xT[c], B_rstd, B_const], writes=[B_dst[c]])

    def ffn(f):
        for j in range(11):
            w, bw = wload("gu" + f, j)
            for i in range(2):
                m = 2 * j + i
                pa, bpa = bank()
                pb, bpb = bank()
                for kc in range(8):
                    mm(pa[:], w[:, kc, i * 128:(i + 1) * 128], hT[:, kc, :], kc == 0, kc == 7, [bw, B_hT[kc]], [bpa])
                for kc in range(8):
                    mm(pb[:], w[:, kc, 256 + i * 128:256 + (i + 1) * 128], hT[:, kc, :], kc == 0, kc == 7, [bw, B_hT[kc]], [bpb])
                t, bt = tmp()
                T.op("act", lambda e, t=t, pa=pa: e.activation(out=t[:], in_=pa[:], func=AF.Silu), reads=[bpa], writes=[bt])
                T.op("dve", lambda e, t=t, pb=pb, m=m: e.tensor_tensor(out=act3[:, m, :], in0=pb[:], in1=t[:], op=ALU.mult), reads=[bpb, bt], writes=[B_act[m]])
        for m in range(8):
            w, bw = wload("dn" + f, m)
            p, bp = bank()
            for kc in range(NFF):
                mm(p[:], w[:, kc, :], act3[:, kc, :], kc == 0, kc == NFF - 1, [bw, B_act[kc]], [bp])
            T.op("dve", lambda e, p=p, m=m: e.scalar_tensor_tensor(out=xT[:, m, :], in0=p[:], scalar=0.5, in1=xT[:, m, :], op0=ALU.mult, op1=ALU.add),
                 reads=[bp, B_xT[m]], writes=[B_xT[m]])
            stats_chunk(m)

    def issue_x(s):
        for a in range(4):
            src = xs[s * TK + a * 128:s * TK + (a + 1) * 128, :]
            T.dma("sp", "xld%d" % a, lambda e, a=a, src=src: e.dma_start(out=stage_x[:, a, :], in_=src), writes=[B_cacc[2 * a], B_cacc[2 * a + 1]])

    def load_x():
        for fc in range(8):
            p, bp = bank()
            for a in range(4):
                mm(p[:, a * 128:(a + 1) * 128], stage_x[:, a, fc * 128:(fc + 1) * 128], identf[:], True, True, [B_cacc[2 * a], B_cacc[2 * a + 1], B_const], [bp])
            evac_copy(xT[:, fc, :], p[:], [bp], [B_xT[fc]])
            stats_chunk(fc)

    def load_rope(s):
        T.dma("sp", "rope", lambda e: e.dma_start(out=ropec[:], in_=cosT[:, s * TK:(s + 1) * TK]), writes=[B_rope])
        T.dma("sp", "rope2", lambda e: e.dma_start(out=ropes[:], in_=sinT[:, s * TK:(s + 1) * TK]), writes=[B_rope2])

    def proj_qk(wname, dst_fn, B_dst):
        for j in range(2):
            w, bw = wload(wname, j)
            for i in range(4):
                m = 4 * j + i
                p, bp = bank()
                for kc in range(8):
                    mm(p[:], w[:, kc, i * 128:(i + 1) * 128], hT[:, kc, :], kc == 0, kc == 7, [bw, B_hT[kc]], [bp])
                if m < 2:
                    evac_copy(r12[:, m, :], p[:], [bp], [B_r12[m]])
                else:
                    evac_copy(qk[:, m, :], p[:], [bp], [B_qk[m]])
                if m == 1:
                    t1, b1 = tmp()
                    t2, b2 = tmp()
                    T.op("dve", lambda e, t1=t1: e.tensor_tensor(out=t1[:], in0=r12[:, 0, :], in1=ropec[:], op=ALU.mult), reads=[B_r12[0], B_rope], writes=[b1])
                    T.op("dve", lambda e, t2=t2: e.tensor_tensor(out=t2[:], in0=r12[:, 1, :], in1=ropes[:], op=ALU.mult), reads=[B_r12[1], B_rope2], writes=[b2])
                    T.op("dve", lambda e, t1=t1, t2=t2: e.tensor_tensor(out=qk[:, 0, :], in0=t1[:], in1=t2[:], op=ALU.subtract), reads=[b1, b2], writes=[B_qk[0]])
                    t3, b3 = tmp()
                    t4, b4 = tmp()
                    T.op("dve", lambda e, t3=t3: e.tensor_tensor(out=t3[:], in0=r12[:, 1, :], in1=ropec[:], op=ALU.mult), reads=[B_r12[1], B_rope], writes=[b3])
                    T.op("dve", lambda e, t4=t4: e.tensor_tensor(out=t4[:], in0=r12[:, 0, :], in1=ropes[:], op=ALU.mult), reads=[B_r12[0], B_rope2], writes=[b4])
                    T.op("dve", lambda e, t3=t3, t4=t4: e.tensor_tensor(out=qk[:, 1, :], in0=t3[:], in1=t4[:], op=ALU.add), reads=[b3, b4], writes=[B_qk[1]])
        for m in range(8):
            dst, chan = dst_fn(m)
            T.dma("pool", chan, lambda e, dst=dst, m=m: e.dma_start(out=dst, in_=qk[:, m, :]), reads=[B_qk[m]], writes=[B_dst[m]])

    def qk_dst(base, tok0, ntok):
        def f(m):
            if m == 0:
                return base[:, 0:8, tok0:tok0 + ntok]
            if m == 1:
                return base[:, 8:16, tok0:tok0 + ntok]
            mp = m - 2
            gq = mp % 2
            db = mp // 2
            return base[8 * gq:8 * gq + 8, 16 + 16 * db:32 + 16 * db, tok0:tok0 + ntok]
        return f

    def proj_v(s):
        for j in range(2):
            w, bw = wload("wv", j)
            for a in range(4):
                p, bp = bank()
                for kc in range(8):
                    mm(p[:], hT[:, kc, a * 128:(a + 1) * 128], w[:, kc, :], kc == 0, kc == 7, [bw, B_hT[kc]], [bp])
                evac_copy(vtok[:, 4 * j:4 * j + 4, a, :], p[:].rearrange("p (h e) -> p h e", e=128), [bp], [B_vtok])
        dst = v_s[:, :, 4 * s:4 * s + 4, :].rearrange("h p k e -> p h k e")
        T.dma("pool", "vw%d" % (s % 2), lambda e: e.dma_start(out=dst, in_=vtok[:]), reads=[B_vtok], writes=[B_vs[s]])

    def proj_u():
        ph, bph = P[7], B_P[7]
        for j in range(4):
            w, bw = wload("wu", j)
            for i in range(2):
                m = 2 * j + i
                for kc in range(8):
                    mm(ph[:, m * 32:(m + 1) * 32], w[:, kc, i * 128:(i + 1) * 128], hhalo[:, kc, :], kc == 0, kc == 7, [bw, B_hhalo], [bph])
                for kc in range(8):
                    mm(ph[:, 256 + m * 32:256 + (m + 1) * 32], w[:, kc, 256 + i * 128:256 + (i + 1) * 128], hhalo[:, kc, :], kc == 0, kc == 7, [bw, B_hhalo], [bph])
                pa, bpa = bank()
                pb, bpb = bank()
                for kc in range(8):
                    mm(pa[:], w[:, kc, i * 128:(i + 1) * 128], hT[:, kc, :], kc == 0, kc == 7, [bw, B_hT[kc]], [bpa])
                for kc in range(8):
                    mm(pb[:], w[:, kc, 256 + i * 128:256 + (i + 1) * 128], hT[:, kc, :], kc == 0, kc == 7, [bw, B_hT[kc]], [bpb])
                t, bt = tmp()
                T.op("act", lambda e, t=t, pb=pb: e.activation(out=t[:], in_=pb[:], func=AF.Sigmoid), reads=[bpb], writes=[bt])
                T.op("dve", lambda e, t=t, pa=pa, m=m: e.tensor_tensor(out=cin[:, m, 32:544], in0=pa[:], in1=t[:], op=ALU.mult), reads=[bpa, bt], writes=[B_act[m]])
        t, bt = tmp()
        T.op("act", lambda e, t=t: e.activation(out=t[:, 0:256], in_=ph[:, 256:512], func=AF.Sigmoid), reads=[bph], writes=[bt])
        T.op("dve", lambda e, t=t: e.tensor_tensor(out=cin[:, :, 0:32], in0=ph[:, 0:256].rearrange("p (c t) -> p c t", t=32),
                                                  in1=t[:, 0:256].rearrange("p (c t) -> p c t", t=32), op=ALU.mult), reads=[bph, bt], writes=B_act[0:8])

    def proj_g():
        for j in range(4):
            w, bw = wload("wg", j)
            for i in range(4):
                cc = 4 * j + i
                p, bp = bank()
                for kc in range(8):
                    mm(p[:], w[:, kc, i * 128:(i + 1) * 128], hT[:, kc, :], kc == 0, kc == 7, [bw, B_hT[kc]], [bp])
                T.op("act", lambda e, p=p, cc=cc: e.activation(out=gates[:, cc, :], in_=p[:], func=AF.Sigmoid, bias=cvec[:, 32 + cc:33 + cc]),
                     reads=[bp, B_const], writes=[B_gates[cc]])

    conv_first = [True]

    def conv_chunk(m):
        for j in range(31):
            wcol = cvec[:, 72 + m * 31 + j:73 + m * 31 + j]
            if j == 0:
                T.op("dve", lambda e, wcol=wcol: e.tensor_scalar(out=cacc[:, m, :], in0=cin[:, m, 2:2 + TK], scalar1=wcol, scalar2=cvec[:, 48 + m:49 + m],
                                                                 op0=ALU.mult, op1=ALU.add),
                     reads=[B_act[m], B_const], writes=[B_cacc[m]])
            else:
                T.op("dve", lambda e, wcol=wcol, j=j: e.scalar_tensor_tensor(out=cacc[:, m, :], in0=cin[:, m, j + 2:j + 2 + TK], scalar=wcol, in1=cacc[:, m, :],
                                                                              op0=ALU.mult, op1=ALU.add),
                     reads=[B_act[m], B_const, B_cacc[m]], writes=[B_cacc[m]])
        if conv_first[0]:
            conv_first[0] = False
            T.op("dve", lambda e: e.tensor_copy(out=accx[:], in_=cacc[:, m, :]), reads=[B_cacc[m]], writes=[B_accx])
            T.op("dve", lambda e: e.tensor_tensor(out=accx2[:], in0=cacc[:, m, :], in1=cacc[:, m, :], op=ALU.mult), reads=[B_cacc[m]], writes=[B_accx2])
        else:
            t, bt = tmp()
            T.op("dve", lambda e: e.tensor_tensor(out=accx[:], in0=accx[:], in1=cacc[:, m, :], op=ALU.add), reads=[B_cacc[m], B_accx], writes=[B_accx])
            T.op("dve", lambda e: e.tensor_tensor(out=t[:], in0=cacc[:, m, :], in1=cacc[:, m, :], op=ALU.mult), reads=[B_cacc[m]], writes=[bt])
            T.op("dve", lambda e: e.tensor_tensor(out=accx2[:], in0=accx2[:], in1=t[:], op=ALU.add), reads=[bt, B_accx2], writes=[B_accx2])

    def conv_ln():
        conv_first[0] = True
        T.op("dve", lambda e: e.tensor_copy(out=sqr[0][:], in_=accx[:]), reads=[B_accx], writes=[B_sqr[0]])
        T.op("dve", lambda e: e.tensor_copy(out=sqr[1][:], in_=accx2[:]), reads=[B_accx2], writes=[B_sqr[1]])
        mm(P[7][:], onesb[:], sqr[0][:], True, True, [B_sqr[0], B_const], [B_P[7]])
        ps2, bps2 = bank()
        mm(ps2[:], onesb[:], sqr[1][:], True, True, [B_sqr[1], B_const], [bps2])
        mean, bmean = tmp()
        T.op("dve", lambda e: e.tensor_scalar(out=mean[:], in0=P[7][:], scalar1=1.0 / D, scalar2=None, op0=ALU.mult), reads=[B_P[7]], writes=[bmean])
        msq, bmsq = tmp()
        T.op("dve", lambda e: e.tensor_tensor(out=msq[:], in0=mean[:], in1=mean[:], op=ALU.mult), reads=[bmean], writes=[bmsq])
        T.op("dve", lambda e: e.scalar_tensor_tensor(out=msq[:], in0=ps2[:], scalar=1.0 / D, in1=msq[:], op0=ALU.mult, op1=ALU.subtract), reads=[bps2, bmsq], writes=[bmsq])
        T.op("act", lambda e: e.activation(out=msq[:], in_=msq[:], func=AF.Sqrt, bias=EPS, scale=1.0), reads=[bmsq], writes=[bmsq])
        T.op("dve", lambda e: e.reciprocal(out=rstd[:], in_=msq[:]), reads=[bmsq], writes=[B_rstd])
        for m in range(8):
            T.op("dve", lambda e, m=m: e.tensor_tensor(out=cacc[:, m, :], in0=cacc[:, m, :], in1=mean[:], op=ALU.subtract), reads=[B_cacc[m], bmean], writes=[B_cacc[m]])
            T.op("dve", lambda e, m=m: e.tensor_tensor(out=cacc[:, m, :], in0=cacc[:, m, :], in1=rstd[:], op=ALU.mult), reads=[B_cacc[m], B_rstd], writes=[B_cacc[m]])
            T.op("act", lambda e, m=m: e.activation(out=cb[:, m, :], in_=cacc[:, m, :], func=AF.Silu, bias=cvec[:, 64 + m:65 + m], scale=cvec[:, 56 + m:57 + m]),
                 reads=[B_cacc[m], B_const], writes=[B_cb[m]])

    kvctr = [0]
    pctr = [0]
    oslot = {}
    for idx in range(8):
        oslot[(idx // 4, idx % 4)] = (4 + idx // 3, (idx % 3) * 129)

    sgc = [0]

    def attention(s, g):
        par = g % 2
        pend_tr = []
        pend_a = []
        conv_chunk(0)
        for h in range(8):
            qb_t, bq = qTb[h % 2], B_qT[h % 2]
            qsrc = qT_s[par, 2 * h:2 * h + 2, :, :].rearrange("c d t -> (c d) t")
            T.dma("sp", "qld%d" % (h % 2), lambda e, qb_t=qb_t, qsrc=qsrc: e.dma_start(out=qb_t[:], in_=qsrc), reads=B_qTs[par], writes=[bq])
            steps = [(sl, t) for sl in range(s + 1) for t in range(4)]
            rmap = {}

            def kv_load(sl):
                if sl in rmap:
                    return rmap[sl]
                r = kvctr[0] % NKV
                kvctr[0] += 1
                ksrc = kT_s[2 * h:2 * h + 2, :, sl * TK:(sl + 1) * TK].rearrange("c d t -> (c d) t")
                vsrc = v_s[h, :, 4 * sl:4 * sl + 4, :]
                T.dma("sp", "kv%d" % r, lambda e: e.dma_start(out=kbuf[r][:], in_=ksrc), reads=B_kTs[sl], writes=[B_kb[r]])
                T.dma("sp", "vv%d" % r, lambda e: e.dma_start(out=vbuf[r][:, :, 0:128], in_=vsrc), reads=[B_vs[sl]], writes=[B_vb[r]])
                rmap[sl] = r
                return r

            started = set()

            def qk_step(i):
                sl, t = steps[i]
                r = kv_load(sl)
                diag = sl == s
                kt = 4 * sl + t
                q0 = 128 * t if diag else 0
                gi = sgc[0] % 2
                sgc[0] += 1
                for c in range(2):
                    bk = 2 * gi + c
                    mm(P[bk][:, q0:TK], kbuf[r][64 * c:64 * c + 64, t * 128:(t + 1) * 128], qb_t[64 * c:64 * c + 64, q0:TK], True, not diag,
                       [B_kb[r], bq], [B_P[bk]])
                    if diag:
                        mm(P[bk][:, q0:q0 + 128], identb[:], maskneg[:], False, True, [B_const], [B_P[bk]])
                pi = pctr[0] % NP
                pctr[0] += 1
                if kt < 4:
                    T.op("act", lambda e: e.activation(out=pT[pi][:, :, q0:TK], in_=PS[:, 2 * gi:2 * gi + 2, q0:TK], func=AF.Exp,
                                                       bias=kbias[:, kt:kt + 1], scale=0.125),
                         reads=[B_P[2 * gi], B_P[2 * gi + 1], B_const], writes=[B_pT[pi]])
                else:
                    T.op("act", lambda e: e.activation(out=pT[pi][:, :, q0:TK], in_=PS[:, 2 * gi:2 * gi + 2, q0:TK], func=AF.Exp, scale=0.125),
                         reads=[B_P[2 * gi], B_P[2 * gi + 1]], writes=[B_pT[pi]])
                return pi

            def pv_step(i, pi):
                sl, t = steps[i]
                r = rmap[sl]
                diag = sl == s
                for c in range(2):
                    for qb in range(t if diag else 0, 4):
                        bk, off = oslot[(c, qb)]
                        first = bk not in started
                        started.add(bk)
                        mm(P[bk][:, off:off + 129], pT[pi][:, c, qb * 128:(qb + 1) * 128], vbuf[r][:, t, 0:129], first, bool(diag and t == qb),
                           [B_pT[pi], B_vb[r]], [B_P[bk]], skip_group_check=True)

            nst = len(steps)
            pis = {0: qk_step(0), 1: qk_step(1)}
            for i in range(nst):
                if i + 2 < nst:
                    pis[i + 2] = qk_step(i + 2)
                pv_step(i, pis[i])
                if i == min(4, nst - 1):
                    while pend_a:
                        pend_a.pop(0)()
                    if h >= 1:
                        conv_chunk(h)
                if i == min(12, nst - 1):
                    while pend_tr:
                        pend_tr.pop(0)()
            for bk in (4, 5, 6):
                n = 3 if bk < 6 else 2
                eng = "dve"
                if eng == "act":
                    T.op("act", lambda e, bk=bk, n=n: e.activation(out=osb[:, (bk - 4) * 387:(bk - 4) * 387 + n * 129], in_=P[bk][:, 0:n * 129], func=AF.Copy),
                         reads=[B_P[bk]], writes=[B_osb[bk - 4]])
                else:
                    T.op("dve", lambda e, bk=bk, n=n: e.tensor_copy(out=osb[:, (bk - 4) * 387:(bk - 4) * 387 + n * 129], in_=P[bk][:, 0:n * 129]),
                         reads=[B_P[bk]], writes=[B_osb[bk - 4]])
            lview = osb[:, 0:8 * 129].rearrange("p (a b) -> p a b", b=129)[:, :, 128]
            T.op("dve", lambda e: e.reciprocal(out=small[:, 0:8], in_=lview), reads=B_osb, writes=[B_small])
            T.op("dve", lambda e: e.tensor_scalar(out=small[:, 8:12], in0=small[:, 4:8], scalar1=neglam, scalar2=None, op0=ALU.mult), reads=[B_small], writes=[B_small])
            for qb in range(4):
                o0 = qb * 129
                o1 = (4 + qb) * 129
                T.op("dve", lambda e, qb=qb, o0=o0: e.tensor_scalar(out=on0[:, qb, :], in0=osb[:, o0:o0 + 128], scalar1=small[:, qb:qb + 1], scalar2=None, op0=ALU.mult),
                     reads=B_osb + [B_small], writes=[B_on0])
            for qb in range(4):
                o1 = (4 + qb) * 129
                T.op("dve", lambda e, qb=qb, o1=o1: e.scalar_tensor_tensor(out=ohs[:, qb, :], in0=osb[:, o1:o1 + 128], scalar=small[:, 8 + qb:9 + qb], in1=on0[:, qb, :],
                                                                           op0=ALU.mult, op1=ALU.add),
                     reads=B_osb + [B_small, B_on0], writes=[B_ohs])
            for qb in range(4):
                T.op("dve", lambda e, qb=qb: e.scalar_tensor_tensor(out=junk[:], in0=ohs[:, qb, :], scalar=1.0, in1=ohs[:, qb, :], op0=ALU.mult, op1=ALU.mult,
                                                                    accum_out=small[:, 16 + qb:17 + qb]),
                     reads=[B_ohs], writes=[B_junk, B_small])
            def do_p2(h=h):
                k1 = (1.0 - LAM_INIT) ** 2
                T.op("dve", lambda e: e.tensor_scalar(out=small[:, 20:24], in0=small[:, 16:20], scalar1=1.0 / (128.0 * k1), scalar2=EPS / k1, op0=ALU.mult, op1=ALU.add),
                     reads=[B_small], writes=[B_small])
                T.op("pool", lambda e: e.tensor_tensor(out=small[:, 24:28], in0=small[:, 20:24], in1=mhalf[:, 0:4], op=ALU.pow), reads=[B_small, B_const], writes=[B_small])
                for qb in range(4):
                    T.op("dve", lambda e, qb=qb: e.scalar_tensor_tensor(out=Otok[:, qb, h * 128:(h + 1) * 128], in0=ohs[:, qb, :], scalar=small[:, 24 + qb:25 + qb], in1=sublnB[:],
                                                                        op0=ALU.mult, op1=ALU.mult),
                         reads=[B_ohs, B_small, B_const], writes=[B_Otok[h]])
            pend_a.append(do_p2)

            def do_tr(h=h):
                p, bp = P[7], B_P[7]
                for qb in range(4):
                    mm(p[:, qb * 128:(qb + 1) * 128], Otok[:, qb, h * 128:(h + 1) * 128], identb[:], True, True, [B_Otok[h], B_const], [bp])
                evac_copy(OT[:, h, :], p[:], [bp], [B_OT[h]])
            pend_tr.append(do_tr)
        while pend_a:
            pend_a.pop(0)()
        while pend_tr:
            pend_tr.pop(0)()

    def mix_out():
        for j in range(2):
            wA, bwA = wload("wa", j)
            for i in range(4):
                m = 4 * j + i
                pa, bpa = bank()
                for kc in range(8):
                    mm(pa[:], wA[:, kc, i * 128:(i + 1) * 128], OT[:, kc, :], kc == 0, kc == 7, [bwA, B_OT[kc]], [bpa])
                T.op("dve", lambda e, pa=pa, m=m: e.tensor_tensor(out=hT[:, m, :], in0=pa[:], in1=gates[:, m, :], op=ALU.mult), reads=[bpa, B_gates[m]], writes=[B_hT[m]])
        for j in range(2):
            wB, bwB = wload("wb", j)
            for i in range(4):
                m = 4 * j + i
                pb, bpb = bank()
                for kc in range(8):
                    mm(pb[:], wB[:, kc, i * 128:(i + 1) * 128], cb[:, kc, :], kc == 0, kc == 7, [bwB, B_cb[kc]], [bpb])
                t2, b2 = tmp()
                T.op("dve", lambda e, t2=t2, pb=pb, m=m: e.tensor_tensor(out=t2[:], in0=pb[:], in1=gates[:, 8 + m, :], op=ALU.mult), reads=[bpb, B_gates[8 + m]], writes=[b2])
                T.op("dve", lambda e, t2=t2, m=m: e.tensor_tensor(out=qk[:, m, :], in0=t2[:], in1=hT[:, m, :], op=ALU.add), reads=[b2, B_hT[m]], writes=[B_qk[m]])
        for j in range(2):
            w, bw = wload("wo", j)
            for i in range(4):
                m = 4 * j + i
                p, bp = bank()
                for kc in range(8):
                    mm(p[:], w[:, kc, i * 128:(i + 1) * 128], qk[:, kc, :], kc == 0, kc == 7, [bw, B_qk[kc]], [bp])
                T.op("dve", lambda e, p=p, m=m: e.tensor_tensor(out=xT[:, m, :], in0=p[:], in1=xT[:, m, :], op=ALU.add), reads=[bp, B_xT[m]], writes=[B_xT[m]])
                stats_chunk(m)

    def store_out(g):
        for a in range(4):
            for half in range(2):
                p, bp = bank()
                for i in range(4):
                    fc = half * 4 + i
                    mm(p[:, i * 128:(i + 1) * 128], xT[:, fc, a * 128:(a + 1) * 128], identf[:], True, True, [B_xT[fc], B_const], [bp])
                evac_copy(stage_x[:, a, half * 512:(half + 1) * 512], p[:], [bp], [B_cacc[2 * a + half]])
            dst = out_d[g * TK + a * 128:g * TK + (a + 1) * 128, :]
            T.dma("pool", "outst%d" % a, lambda e, a=a, dst=dst: e.dma_start(out=dst, in_=stage_x[:, a, :]), reads=[B_cacc[2 * a], B_cacc[2 * a + 1]], writes=[B_out[a]])

    for g in range(NPAIR):
        s0, s1 = 2 * g, 2 * g + 1
        issue_x(s0)
        load_x()
        issue_x(s1)
        rmsnorm(0, hT, B_hT)
        ffn("1")
        rmsnorm(8, hT, B_hT)
        load_rope(s0)
        proj_qk("wk", lambda m, s0=s0: (qk_dst(kT_s, s0 * TK, TK)(m), "kw%d" % (s0 % 2)), B_kTs[s0])
        proj_v(s0)
        T.op("dve", lambda e: e.tensor_copy(out=hhalo[:], in_=hT[:, :, TK - 32:TK]), reads=B_hT, writes=[B_hhalo])
        load_x()
        rmsnorm(0, hT, B_hT)
        ffn("1")
        rmsnorm(8, hT, B_hT)
        load_rope(s1)
        proj_qk("wk", lambda m, s1=s1: (qk_dst(kT_s, s1 * TK, TK)(m), "kw%d" % (s1 % 2)), B_kTs[s1])
        proj_v(s1)
        proj_qk("wq", lambda m, g=g: (qk_dst(qT_s[g % 2], 0, TK)(m), "qw"), B_qTs[g % 2])
        proj_u()
        proj_g()
        attention(s1, g)
        conv_ln()
        mix_out()
        rmsnorm(16, hT, B_hT)
        ffn("2")
        rmsnorm(24, xT, B_xT)
        store_out(g)
    T.op("sp", lambda e: e.nop(), reads=B_out)
    T.emit()
    return nc


def _perm_qk_cols():
    perm = np.zeros(1024, dtype=np.int64)
    for m in range(8):
        for p in range(128):
            if m == 0:
                c, i = p // 8, p % 8
                o = c * 64 + i
            elif m == 1:
                c, i = p // 8, p % 8
                o = c * 64 + 8 + i
            else:
                mp = m - 2
                gq, db = mp % 2, mp // 2
                c, i = 8 * gq + p // 16, p % 16
                o = c * 64 + 16 + 16 * db + i
            perm[m * 128 + p] = o
    return perm


def _fm(v):
    v = np.asarray(v, dtype=np.float32).reshape(-1, 128)
    return np.ascontiguousarray(v.T)


_NC_CACHE = {}


def kernel(**inputs):
    x = np.asarray(inputs["x"], dtype=np.float32)
    Bn, S, _ = x.shape
    NSLOT = S // TK
    ncores = 2 * Bn
    f32 = lambda a: np.ascontiguousarray(np.asarray(a, dtype=np.float32))

    perm = _perm_qk_cols()
    w_in = f32(inputs["w_in"][0])
    win = np.concatenate([w_in[:, perm], w_in[:, 1024 + perm], w_in[:, 2048:]], axis=1)
    win = np.ascontiguousarray(win)

    cvec = np.zeros((128, 320), np.float32)
    cvec[:, 0:8] = _fm(inputs["ffn1_norm"][0])
    cvec[:, 8:16] = _fm(inputs["mix_norm"][0])
    cvec[:, 16:24] = _fm(inputs["ffn2_norm"][0])
    cvec[:, 24:32] = _fm(inputs["final_norm"])
    cvec[:, 32:48] = _fm(inputs["b_gate"][0])
    cvec[:, 48:56] = _fm(inputs["conv_b"][0])
    cvec[:, 56:64] = _fm(inputs["conv_ln_g"][0])
    cvec[:, 64:72] = _fm(inputs["conv_ln_b"][0])
    cw = f32(inputs["conv_w"][0])
    cvec[:, 72:320] = cw.T.reshape(8, 128, 31).transpose(1, 0, 2).reshape(128, 248)
    sublnB = np.ascontiguousarray(np.broadcast_to(f32(inputs["attn_subln"][0])[None, :], (128, 128)))
    lamv = np.concatenate([f32(inputs["lambda_q1"][0]), f32(inputs["lambda_k1"][0]), f32(inputs["lambda_q2"][0]), f32(inputs["lambda_k2"][0])])
    lamv = np.ascontiguousarray(np.broadcast_to(lamv[None, :], (128, 256)))

    inv_freq = (np.float32(500000.0) ** (-np.arange(0, 16, 2, dtype=np.float32) / np.float32(16))).astype(np.float32)

    shared = {
        "cvec": cvec, "sublnB": sublnB, "lamv": lamv, "win": win,
        "gu1": f32(inputs["ffn1_w_gate_up"][0]), "dn1": f32(inputs["ffn1_w_down"][0]),
        "wa": f32(inputs["w_attn_out"][0]), "wb": f32(inputs["w_conv_out"][0]), "wo": f32(inputs["w_out"][0]),
        "gu2": f32(inputs["ffn2_w_gate_up"][0]), "dn2": f32(inputs["ffn2_w_down"][0]),
    }
    in_maps = []
    for core in range(ncores):
        b, j = core // 2, core % 2
        shift = TK * (1 - j)
        xs = np.zeros((S, D), np.float32)
        xs[shift:] = x[b, :S - shift]
        pos = np.maximum(np.arange(S, dtype=np.float32) - np.float32(shift), np.float32(0.0)).astype(np.float32)
        ang = pos[None, :] * inv_freq[:, None]
        cosT = np.ascontiguousarray(np.tile(np.cos(ang).astype(np.float32), (16, 1)))
        sinT = np.ascontiguousarray(np.tile(np.sin(ang).astype(np.float32), (16, 1)))
        kbias = np.zeros((128, NSLOT * 4), np.float32)
        if j == 0:
            kbias[:, 0:4] = NEG
        m = dict(shared)
        m.update({"xs": xs, "cosT": cosT, "sinT": sinT, "kbias": kbias})
        in_maps.append(m)

    if NSLOT not in _NC_CACHE:
        _NC_CACHE[NSLOT] = build(NSLOT)
    nc = _NC_CACHE[NSLOT]
    res = run_bass_kernel_spmd(nc, in_maps, core_ids=list(range(ncores)))
    out = np.zeros((Bn, S, D), np.float32)
    for core in range(ncores):
        b, j = core // 2, core % 2
        o = res.results[core]["out"]
        for g in range(NSLOT // 2):
            sbk = 2 * g + j
            out[b, sbk * TK:(sbk + 1) * TK] = o[g * TK:(g + 1) * TK]
    return out
```
